# Optimizing a Trainium2 kernel written in Bass

```python
import math
import jax, jax.numpy as jnp
from jax import lax
import numpy as np


D_MODEL = 1024
BATCH = 2
SEQ = 8192
DEPTH = 1
DEC_BATCH = 32
DEC_SEQ = 4
PAST_LEN = 8192
PAGE_SIZE = 128

N_HEADS_A = 4
HEAD_DIM_A = (D_MODEL // 2) // (2 * N_HEADS_A)
N_HEADS_B = 4
KEY_DIM_B = (D_MODEL // 2) // N_HEADS_B
VAL_DIM_B = KEY_DIM_B
D_FF = 2816
CONV_W = 3

ROPE_THETA = 10000.0
Q_BLOCK = 128
GLA_CHUNK = 64
LN_EPS = 1e-5
RMS_EPS = 1e-5
ALPHA = (2 * DEPTH) ** 0.25
BETA = (8 * DEPTH) ** -0.25

QA_DIM = N_HEADS_A * 2 * HEAD_DIM_A
KA_DIM = N_HEADS_A * 2 * HEAD_DIM_A
VA_DIM = N_HEADS_A * 2 * HEAD_DIM_A
QB_DIM = N_HEADS_B * KEY_DIM_B
FB_DIM = N_HEADS_B * KEY_DIM_B
IB_DIM = N_HEADS_B * VAL_DIM_B
GB_DIM = N_HEADS_B * VAL_DIM_B
IN_SIZES = (QA_DIM, KA_DIM, VA_DIM, QB_DIM, FB_DIM, IB_DIM, GB_DIM, D_MODEL, D_MODEL)
IN_COLS = sum(IN_SIZES)

kernel_name = 'hybrid_diffattn_hgrn2_convffn_step'


def layer_norm(x, g, b):
    xf = x.astype(jnp.float32)
    mu = jnp.mean(xf, -1, keepdims=True)
    var = jnp.mean(jnp.square(xf - mu), -1, keepdims=True)
    return ((xf - mu) * lax.rsqrt(var + LN_EPS) * g.astype(jnp.float32) + b.astype(jnp.float32)).astype(x.dtype)


def rms_norm(x, g):
    xf = x.astype(jnp.float32)
    return (xf * lax.rsqrt(jnp.mean(xf * xf, -1, keepdims=True) + RMS_EPS) * g.astype(jnp.float32)).astype(x.dtype)


def rope(x, pos):
    half = HEAD_DIM_A // 2
    inv = ROPE_THETA ** (-jnp.arange(half, dtype=jnp.float32) * 2.0 / HEAD_DIM_A)
    ang = pos.astype(jnp.float32)[:, None] * inv[None, :]
    cos = jnp.cos(ang)[:, None, None, :].astype(x.dtype)
    sin = jnp.sin(ang)[:, None, None, :].astype(x.dtype)
    x1, x2 = x[..., :half], x[..., half:]
    return jnp.concatenate([x1 * cos - x2 * sin, x2 * cos + x1 * sin], -1)


def mixer_inputs(x, pos, w_in_l, lb_l):
    B, T, _ = x.shape
    proj = x @ w_in_l
    offs = np.cumsum(IN_SIZES)[:-1].tolist()
    qa, ka, va, qb, fb, ib, gb, ga, gg = jnp.split(proj, offs, axis=-1)
    qa = rope(qa.reshape(B, T, N_HEADS_A, 2, HEAD_DIM_A), pos)
    ka = rope(ka.reshape(B, T, N_HEADS_A, 2, HEAD_DIM_A), pos)
    va = va.reshape(B, T, N_HEADS_A, 2 * HEAD_DIM_A)
    lb = lb_l.astype(jnp.float32)
    fr = fb.astype(jnp.float32)
    logf = jnp.logaddexp(jnp.log(lb), jnp.log1p(-lb) + jax.nn.log_sigmoid(fr))
    kb = (1.0 - lb) * jax.nn.sigmoid(-fr)
    qb = qb.reshape(B, T, N_HEADS_B, KEY_DIM_B)
    kb = kb.reshape(B, T, N_HEADS_B, KEY_DIM_B)
    logf = logf.reshape(B, T, N_HEADS_B, KEY_DIM_B)
    ib = ib.reshape(B, T, N_HEADS_B, VAL_DIM_B)
    gb = gb.reshape(B, T, N_HEADS_B, VAL_DIM_B)
    return qa, ka, va, qb, kb, logf, ib, gb, ga, gg


def diff_attention(q, ks, vs, masks, lam):
    scale = HEAD_DIM_A ** -0.5
    s = jnp.concatenate([jnp.einsum('bqhmd,bkhmd->bhmqk', q, k, preferred_element_type=jnp.float32) * scale
                         for k in ks], -1)
    mask = jnp.concatenate(masks, -1)
    p = jax.nn.softmax(jnp.where(mask, s, -jnp.inf), axis=-1)
    a = (p[:, :, 0] - lam * p[:, :, 1]).astype(vs[0].dtype)
    offs = np.cumsum([0] + [k.shape[1] for k in ks]).tolist()
    return sum(jnp.einsum('bhqk,bkhe->bqhe', a[..., offs[i]:offs[i + 1]], vs[i]) for i in range(len(ks)))


def prompt_diff_attention(q, k, v, lam):
    B, T = q.shape[0], q.shape[1]
    nb = T // Q_BLOCK
    qb = q.reshape(B, nb, Q_BLOCK, N_HEADS_A, 2, HEAD_DIM_A).swapaxes(0, 1)
    starts = jnp.arange(nb) * Q_BLOCK
    kpos = jnp.arange(T)

    def one_block(args):
        qblk, st = args
        mask = kpos[None, :] <= (st + jnp.arange(Q_BLOCK))[:, None]
        return diff_attention(qblk, (k,), (v,), (mask,), lam)

    o = lax.map(one_block, (qb, starts))
    return o.swapaxes(0, 1).reshape(B, T, N_HEADS_A, 2 * HEAD_DIM_A)


def hgrn2_scan(q, k, v, logf, s0):
    B, T, H, _ = q.shape
    C = math.gcd(T, GLA_CHUNK)
    n = T // C

    def to_chunks(a):
        return a.astype(jnp.float32).reshape(B, n, C, H, a.shape[-1]).transpose(1, 0, 3, 2, 4)

    causal = jnp.tril(jnp.ones((C, C), bool))

    def step(S, xs):
        qc, kc, vc, lc = xs
        b = jnp.cumsum(lc, axis=2)
        o_inter = jnp.einsum('bhtk,bhkv->bhtv', qc * jnp.exp(b), S)
        diff = jnp.where(causal[None, None, :, :, None], b[:, :, :, None, :] - b[:, :, None, :, :], -jnp.inf)
        att = jnp.einsum('bhtk,bhsk,bhtsk->bhts', qc, kc, jnp.exp(diff))
        o = o_inter + jnp.einsum('bhts,bhsv->bhtv', att, vc)
        b_last = b[:, :, -1:, :]
        S_new = jnp.exp(b_last[:, :, 0, :])[..., None] * S + jnp.einsum('bhsk,bhsv->bhkv', kc * jnp.exp(b_last - b), vc)
        return S_new, o

    S, o = lax.scan(step, s0.astype(jnp.float32), (to_chunks(q), to_chunks(k), to_chunks(v), to_chunks(logf)))
    o = o.transpose(1, 0, 3, 2, 4).reshape(B, T, H, VAL_DIM_B)
    return o, S


def finish_layer(x, oa, ob, gb, ga, gg, conv0, l, hgrn_norm_g, w_branch_a, w_branch_b, w_out,
                 ln1_g, ln1_b, w_up, conv_w, conv_b, w_down, ln2_g, ln2_b):
    B, T, _ = x.shape
    ob = rms_norm(ob.astype(x.dtype), hgrn_norm_g[l]) * jax.nn.silu(gb)
    merged = (jax.nn.sigmoid(ga) * (oa.reshape(B, T, -1) @ w_branch_a[l])
              + jax.nn.sigmoid(gg) * (ob.reshape(B, T, -1) @ w_branch_b[l]))
    h = layer_norm(ALPHA * x + merged @ w_out[l], ln1_g[l], ln1_b[l])
    a, g = jnp.split(h @ w_up[l], 2, axis=-1)
    ap = jnp.concatenate([conv0.astype(a.dtype), a], axis=1)
    c = conv_b[l] + sum(ap[:, j:j + T] * conv_w[l, j] for j in range(CONV_W))
    y = (jax.nn.gelu(c, approximate=False) * g) @ w_down[l]
    out = layer_norm(ALPHA * h + y, ln2_g[l], ln2_b[l])
    return out, ap[:, -(CONV_W - 1):]


def setup_inputs(seed: int = 0) -> dict:
    key = jax.random.key(seed)
    ks = jax.random.split(key, 32)
    f32 = jnp.float32

    def nrm(k, shape, s):
        return jax.random.normal(k, shape, f32) * s

    n_pages = PAST_LEN // PAGE_SIZE
    n_used = DEC_BATCH * n_pages
    n_phys = n_used + n_used // 4
    page_table = jax.random.permutation(ks[4], n_phys)[:n_used].reshape(DEC_BATCH, n_pages).astype(jnp.int32)
    return {
        'x_prompt': nrm(ks[0], (BATCH, SEQ, D_MODEL), 1.0),
        'x_sample': nrm(ks[1], (DEC_BATCH, DEC_SEQ, D_MODEL), 1.0),
        'cache_k': nrm(ks[2], (n_phys, DEPTH, PAGE_SIZE, N_HEADS_A, 2, HEAD_DIM_A), 1.0),
        'cache_v': nrm(ks[3], (n_phys, DEPTH, PAGE_SIZE, N_HEADS_A, 2 * HEAD_DIM_A), 1.0),
        'state_hgrn': nrm(ks[5], (DEC_BATCH, DEPTH, N_HEADS_B, KEY_DIM_B, VAL_DIM_B), 0.5),
        'state_conv': nrm(ks[6], (DEC_BATCH, DEPTH, CONV_W - 1, D_FF), 1.0),
        'page_table': page_table,
        'w_in': nrm(ks[7], (DEPTH, D_MODEL, IN_COLS), D_MODEL ** -0.5),
        'lambda_q1': nrm(ks[8], (DEPTH, HEAD_DIM_A), 0.1),
        'lambda_k1': nrm(ks[9], (DEPTH, HEAD_DIM_A), 0.1),
        'lambda_q2': nrm(ks[10], (DEPTH, HEAD_DIM_A), 0.1),
        'lambda_k2': nrm(ks[11], (DEPTH, HEAD_DIM_A), 0.1),
        'subln_g': 1.0 + nrm(ks[12], (DEPTH, 2 * HEAD_DIM_A), 0.02),
        'lb_logits': nrm(ks[13], (DEPTH + 1, N_HEADS_B * KEY_DIM_B), 0.1),
        'hgrn_norm_g': 1.0 + nrm(ks[14], (DEPTH, VAL_DIM_B), 0.02),
        'w_branch_a': nrm(ks[15], (DEPTH, VA_DIM, D_MODEL), VA_DIM ** -0.5),
        'w_branch_b': nrm(ks[16], (DEPTH, IB_DIM, D_MODEL), IB_DIM ** -0.5),
        'w_out': nrm(ks[17], (DEPTH, D_MODEL, D_MODEL), BETA * D_MODEL ** -0.5),
        'ln1_g': 1.0 + nrm(ks[18], (DEPTH, D_MODEL), 0.02),
        'ln1_b': nrm(ks[19], (DEPTH, D_MODEL), 0.02),
        'w_up': nrm(ks[20], (DEPTH, D_MODEL, 2 * D_FF), D_MODEL ** -0.5),
        'conv_w': nrm(ks[21], (DEPTH, CONV_W, D_FF), CONV_W ** -0.5),
        'conv_b': nrm(ks[22], (DEPTH, D_FF), 0.02),
        'w_down': nrm(ks[23], (DEPTH, D_FF, D_MODEL), BETA * D_FF ** -0.5),
        'ln2_g': 1.0 + nrm(ks[24], (DEPTH, D_MODEL), 0.02),
        'ln2_b': nrm(ks[25], (DEPTH, D_MODEL), 0.02),
    }


def reference(x_prompt, x_sample, cache_k, cache_v, state_hgrn, state_conv, page_table,
              w_in, lambda_q1, lambda_k1, lambda_q2, lambda_k2, subln_g, lb_logits, hgrn_norm_g,
              w_branch_a, w_branch_b, w_out, ln1_g, ln1_b, w_up, conv_w, conv_b, w_down, ln2_g, ln2_b):
    f32 = jnp.float32
    Bp, Tp = x_prompt.shape[0], x_prompt.shape[1]
    Bs, Ts = x_sample.shape[0], x_sample.shape[1]
    past_len = page_table.shape[1] * cache_k.shape[2]
    pos_p = jnp.arange(Tp)
    pos_s = past_len + jnp.arange(Ts)
    lb_all = jnp.cumsum(jax.nn.softmax(lb_logits.astype(f32), axis=0), axis=0)
    sample_masks = (jnp.ones((Ts, past_len), bool), jnp.tril(jnp.ones((Ts, Ts), bool)))
    ffn_args = (hgrn_norm_g, w_branch_a, w_branch_b, w_out, ln1_g, ln1_b, w_up, conv_w, conv_b, w_down, ln2_g, ln2_b)

    xp, xs = x_prompt, x_sample
    kp_l, vp_l, sp_l, cp_l, ks_l, vs_l, ss_l, cs_l = [], [], [], [], [], [], [], []
    for l in range(DEPTH):
        lam_init = 0.8 - 0.6 * math.exp(-0.3 * l)
        lam = (jnp.exp(jnp.sum(lambda_q1[l].astype(f32) * lambda_k1[l].astype(f32)))
               - jnp.exp(jnp.sum(lambda_q2[l].astype(f32) * lambda_k2[l].astype(f32))) + lam_init)

        qa, ka, va, qb, kb, logf, ib, gb, ga, gg = mixer_inputs(xp, pos_p, w_in[l], lb_all[l])
        oa = prompt_diff_attention(qa, ka, va, lam)
        oa = rms_norm(oa, subln_g[l]) * (1.0 - lam_init)
        s0 = jnp.zeros((Bp, N_HEADS_B, KEY_DIM_B, VAL_DIM_B), f32)
        ob, S_p = hgrn2_scan(qb, kb, ib, logf, s0)
        conv0 = jnp.zeros((Bp, CONV_W - 1, D_FF), xp.dtype)
        xp_new, conv_p = finish_layer(xp, oa, ob, gb, ga, gg, conv0, l, *ffn_args)
        kp_l.append(ka); vp_l.append(va); sp_l.append(S_p.astype(xp.dtype)); cp_l.append(conv_p)

        qa, ka, va, qb, kb, logf, ib, gb, ga, gg = mixer_inputs(xs, pos_s, w_in[l], lb_all[l])
        k_past = cache_k[page_table, l].reshape(Bs, past_len, N_HEADS_A, 2, HEAD_DIM_A)
        v_past = cache_v[page_table, l].reshape(Bs, past_len, N_HEADS_A, 2 * HEAD_DIM_A)
        oa = diff_attention(qa, (k_past.astype(ka.dtype), ka), (v_past.astype(va.dtype), va), sample_masks, lam)
        oa = rms_norm(oa, subln_g[l]) * (1.0 - lam_init)
        ob, S_s = hgrn2_scan(qb, kb, ib, logf, state_hgrn[:, l])
        xs_new, conv_s = finish_layer(xs, oa, ob, gb, ga, gg, state_conv[:, l], l, *ffn_args)
        ks_l.append(ka); vs_l.append(va); ss_l.append(S_s.astype(xs.dtype)); cs_l.append(conv_s)

        xp, xs = xp_new, xs_new

    k_prompt = jnp.stack(kp_l, axis=1)
    v_prompt = jnp.stack(vp_l, axis=1)
    hgrn_prompt = jnp.stack(sp_l, axis=1)
    conv_prompt = jnp.stack(cp_l, axis=1)
    k_sample = jnp.stack(ks_l, axis=1)
    v_sample = jnp.stack(vs_l, axis=1)
    hgrn_sample = jnp.stack(ss_l, axis=1)
    conv_sample = jnp.stack(cs_l, axis=1)
    return (xp, xs, k_prompt, v_prompt, hgrn_prompt, conv_prompt, k_sample, v_sample, hgrn_sample, conv_sample)
```

```python
import math
import os
from contextlib import ExitStack
import numpy as np
import concourse.bass as bass
import concourse.mybir as mybir
from concourse.bass_utils import run_bass_kernel_spmd

F32 = mybir.dt.float32
BF16 = mybir.dt.bfloat16
I32 = mybir.dt.int32
ALU = mybir.AluOpType
AF = mybir.ActivationFunctionType
AX = mybir.AxisListType

D = 1024
DFF = 2816
NFC = DFF // 128
LN_EPS = 1e-5
RMS_EPS = 1e-5
ALPHA = 2.0 ** 0.25
LAM_INIT = 0.8 - 0.6 * math.exp(0.0)
SCALE = 64 ** -0.5
RING = 12


class Op:
    __slots__ = ("eng", "fn", "dma", "deps", "needs_inc", "sem", "val", "idx", "cc")

    def __init__(self, eng, fn, dma=False, cc=False):
        self.eng, self.fn, self.dma, self.cc = eng, fn, dma, cc
        self.deps = []
        self.needs_inc = dma or cc
        self.sem = None
        self.val = 0


class Prog:
    ENGS = ("pe", "act", "dve", "pool", "sp")

    def __init__(self, nc):
        self.nc = nc
        self.ops = {e: [] for e in self.ENGS}
        self.last_w = {}
        self.readers = {}
        self.all_dma = []
        self.cut = False

    def op(self, eng, fn, r=(), w=(), dma=False, cc=False, extra=()):
        o = Op(eng, fn, dma, cc)
        if self.cut:
            return o
        deps = set(extra)
        for k in r:
            d = self.last_w.get(k)
            if d is not None:
                deps.add(d)
        for k in w:
            d = self.last_w.get(k)
            if d is not None:
                deps.add(d)
            for rd in self.readers.get(k, ()):
                deps.add(rd)
        async_o = dma or cc
        for d in deps:
            if d is o:
                continue
            if d.eng == "pe" and eng == "pe" and not async_o:
                continue
            o.deps.append(d)
            d.needs_inc = True
        for k in w:
            self.last_w[k] = o
            self.readers[k] = []
        for k in r:
            lst = self.readers.setdefault(k, [])
            if not async_o:
                lst[:] = [x for x in lst if x.eng != eng or x.dma or x.cc]
            lst.append(o)
        self.ops[eng].append(o)
        if async_o:
            self.all_dma.append(o)
        return o

    def barrier(self):
        if self.cut:
            return
        lasts = [self.ops[e][-1] for e in self.ENGS if self.ops[e]]
        dm = list(self.all_dma)
        self.all_dma = []
        for e in self.ENGS:
            self.op(e, None, extra=lasts + dm)
        self.last_w = {}
        self.readers = {}

    def finalize(self, stack):
        nc = self.nc
        esem = {e: stack.enter_context(nc.semaphore("es_" + e)) for e in self.ENGS}
        rings = {e: [stack.enter_context(nc.semaphore("r%s%d" % (e, i))) for i in range(RING)]
                 for e in ("sp", "pool")}
        ccsem = stack.enter_context(nc.semaphore("ccsem"))
        cnt = {e: 0 for e in self.ENGS}
        dcnt = {"sp": 0, "pool": 0}
        cccnt = 0
        for e in self.ENGS:
            for o in self.ops[e]:
                if o.cc:
                    cccnt += 1
                    o.sem, o.val = ccsem, cccnt
                elif o.dma:
                    i = dcnt[e]
                    dcnt[e] += 1
                    o.idx = i
                    o.sem, o.val = rings[e][i % RING], 16 * (i // RING + 1)
                elif o.needs_inc and o.fn is not None:
                    cnt[e] += 1
                    o.sem, o.val = esem[e], cnt[e]
        block = stack.enter_context(nc.Block())

        def run(e, eng):
            waited = {}
            for o in self.ops[e]:
                ws = []
                for d in o.deps:
                    if d.sem is None:
                        continue
                    ws.append((d.sem, d.val))
                if o.dma and o.idx >= RING:
                    ws.append((o.sem, o.val - 16))
                for s, v in ws:
                    key = id(s)
                    if waited.get(key, 0) >= v:
                        continue
                    waited[key] = v
                    eng.wait_ge(s, v)
                if o.fn is None:
                    continue
                ins = o.fn(eng)
                if o.cc:
                    ins.then_inc(o.sem)
                elif o.dma:
                    ins.then_inc(o.sem, 16)
                elif o.sem is not None:
                    ins.then_inc(o.sem, 1)
            if e in ("sp", "pool"):
                for i in range(min(RING, dcnt[e])):
                    last = ((dcnt[e] - 1 - i) // RING) * RING + i
                    eng.wait_ge(rings[e][i], 16 * (last // RING + 1))
            if e == "pool" and cccnt:
                eng.wait_ge(ccsem, cccnt)

        @block.tensor
        def _(eng):
            run("pe", eng)

        @block.scalar
        def _(eng):
            run("act", eng)

        @block.vector
        def _(eng):
            run("dve", eng)

        @block.gpsimd
        def _(eng):
            run("pool", eng)

        @block.sync
        def _(eng):
            run("sp", eng)


def build(cfg):
    T, NSB, PAST, NPHYS = cfg["T"], cfg["NSB"], cfg["PAST"], cfg["NPHYS"]
    NBLK = T // 128
    NSUP = T // 512
    NG8 = PAST // 1024
    NKT = NG8 * 8
    TT = T // 4
    NTILE = TT // 512
    NSL = NSB // 4
    NS_T = NSL * 4
    NCH = NSUP + 4
    NCOL = NS_T + 2 + TT
    nc = bass.Bass("TRN2", target_bir_lowering=False)
    P = Prog(nc)
    STOP = int(os.environ.get("KSTOP", "99"))

    def din(name, shape, dt=F32):
        return nc.dram_tensor(name, list(shape), dt, kind="ExternalInput").ap()

    def dout(name, shape, dt=F32):
        return nc.dram_tensor(name, list(shape), dt, kind="ExternalOutput").ap()

    xT_seq = din("xT_seq", [D, T])
    xsT = din("xsT", [D, NSB * 4])
    wm = din("wm", [D, 896])
    rope_p = din("rope_p", [T, 2, 128])
    rope_s = din("rope_s", [4, 2, 128])
    lbl = din("lbl", [128, 2])
    lamv = din("lamv", [128, 256])
    ga_b = din("ga_b", [128, 128])
    gn_b = din("gn_b", [128, 128])
    tri_d = din("tri", [128, 128])
    ident_d = din("ident", [128, 128])
    mnew_d = din("mnew", [4, 8])
    cmbp_d = din("cmbp", [8, 8])
    a16_d = din("a16", [128, 1])
    ptr_d = din("ptr", [128, NSB * NG8], I32)
    ck = din("ck", [NPHYS * 16, 1024])
    cv = din("cv", [NPHYS * 16, 1024])
    st_h = din("st_h", [NSB, 128, 128])
    xT_t = din("xT_t", [D, NCOL])
    x_t = din("x_t", [NCOL, D])
    wg_d = din("wg_t", [16, 128, 8 * 128])
    wa_d = din("wa_t", [8, 128, 4 * 128])
    wb_d = din("wb_t", [8, 128, 4 * 128])
    wout_d = din("wout", [D, D])
    wup_d = din("wup_t", [NFC, 128, 2 * 8 * 128])
    wdn_d = din("wdn", [DFF, D])
    lnp_d = din("lnp", [128, 4, D])
    cvw_d = din("cvw", [128, NFC, 4])
    scv_d = din("scv", [128, NFC, NSL, 2])
    hmask_d = din("hmask", [128, 1])
    idx_d = din("idx", [128, (NTILE + 2) * 8], I32)
    k_p = dout("k_p", [T, 128])
    v_p = dout("v_p", [T, 128])
    k_s = dout("k_s", [NSB * 4, 128])
    v_s = dout("v_s", [NSB * 4, 128])
    S_p = dout("S_p", [128, 128])
    S_s = dout("S_s", [NSB, 128, 128])
    y_o = dout("y_o", [NS_T + TT, D])
    cv_s = dout("cv_s", [128, NFC, NSL, 2])
    cv_p = dout("cv_p", [128, NFC, 2])
    xTb = nc.dram_tensor("xTb", [D, T], BF16)
    snd = nc.dram_tensor("snd", [NCH * 256, 512], BF16)
    gat = nc.dram_tensor("gat", [4 * NCH * 256, 512], BF16)

    with ExitStack() as top:
        def sb(name, shape, dt=F32, st=top):
            return st.enter_context(nc.sbuf_tensor("s_" + name, list(shape), dt))

        banks = [top.enter_context(nc.psum_tensor("ps%d" % i, [128, 512], F32)) for i in range(8)]

        ident = sb("ident", [128, 128], BF16)
        tri = sb("tri", [128, 128], F32)
        ones = sb("ones", [128, 128], F32)
        lam_t = sb("lam_t", [128, 4], F32)

        def mm(out, lhsT, rhs, start, stop, r, w):
            return P.op("pe", lambda e: e.matmul(out, lhsT, rhs, start=start, stop=stop), r=r, w=w)

        def tp(out, in_, r, w):
            n = in_.shape[0]
            return P.op("pe", lambda e: e.transpose(out, in_, ident[0:n, 0:n]), r=list(r) + ["ident"], w=w)

        def act(out, in_, func, r, w, bias=None, scale=None, accum=None):
            kw = {}
            if bias is not None:
                kw["bias"] = bias
            if scale is not None:
                kw["scale"] = scale
            if accum is not None:
                kw["accum_out"] = accum
            return P.op("act", lambda e: e.activation(out, in_, func, **kw), r=r, w=w)

        def tt(eng, out, a, b, op, r, w):
            return P.op(eng, lambda e: e.tensor_tensor(out, a, b, op), r=r, w=w)

        def ts(eng, out, a, s1, s2, op0, op1, r, w):
            if op1 is None:
                return P.op(eng, lambda e: e.tensor_scalar(out, a, s1, None, op0), r=r, w=w)
            return P.op(eng, lambda e: e.tensor_scalar(out, a, s1, s2, op0, op1), r=r, w=w)

        def stt(eng, out, a, s, b, op0, op1, r, w):
            return P.op(eng, lambda e: e.scalar_tensor_tensor(out, a, s, b, op0, op1), r=r, w=w)

        def cp(eng, out, in_, r, w):
            if eng == "act":
                return P.op("act", lambda e: e.copy(out, in_), r=r, w=w)
            return P.op(eng, lambda e: e.tensor_copy(out, in_), r=r, w=w)

        def dma(q, out, in_, r, w):
            return P.op(q, lambda e: e.dma_start(out=out, in_=in_), r=r, w=w, dma=True)

        identf = sb("identf", [128, 128], F32)
        dma("sp", identf[:], ident_d, [], ["identf"])
        cp("dve", ident[:], identf[:], ["identf"], ["ident"])
        dma("sp", tri[:], tri_d, [], ["tri"])
        P.op("pool", lambda e: e.memset(ones[:], 1.0), w=["ones"])
        lamt = sb("lamt", [128, 256], F32)
        dma("sp", lamt[:], lamv, [], ["lamt"])
        lj = sb("lj", [128, 128], F32)
        ls = sb("ls", [128, 4], F32)
        tt("dve", lj[:], lamt[:, 0:128], lamt[:, 128:256], ALU.mult, ["lamt"], ["lj"])
        P.op("dve", lambda e: e.tensor_reduce(ls[:, 0:2], lj[:].rearrange("p (a b) -> p a b", a=2), AX.X, ALU.add),
             r=["lj"], w=["ls"])
        act(ls[:, 2:4], ls[:, 0:2], AF.Exp, ["ls"], ["ls2"])
        tt("dve", lam_t[:, 0:1], ls[:, 2:3], ls[:, 3:4], ALU.subtract, ["ls2"], ["lam0"])
        ts("dve", lam_t[:, 0:1], lam_t[:, 0:1], LAM_INIT, None, ALU.add, None, ["lam0"], ["lam0"])
        ts("dve", lam_t[:, 1:2], lam_t[:, 0:1], -1.0, None, ALU.mult, None, ["lam0"], ["lam"])

        for i in range(0, T, 2048):
            w_ = min(2048, T - i)
            P.op("pool", lambda e, i=i, w_=w_: e.dma_start(out=xTb.ap()[:, i:i + w_], in_=xT_seq[:, i:i + w_]),
                 w=[("xTb", i // 512 + j) for j in range(w_ // 512)], dma=True)

        with ExitStack() as ph:
            def sbp(name, shape, dt=F32):
                return sb(name, shape, dt, ph)
            OA = sbp("OA", [128, NCH, 512], BF16)
            OB = sbp("OB", [128, NCH, 512], BF16)
            WM = sbp("WM", [128, 8, 896], BF16)
            P.op("pool", lambda e: e.dma_start(out=WM[:], in_=wm.rearrange("(k p) c -> p k c", p=128)), w=["WM"], dma=True)
            gab = sbp("gab", [128, 128])
            gnb = sbp("gnb", [128, 128])
            dma("sp", gab[:], ga_b, [], ["gab"])
            dma("sp", gnb[:], gn_b, [], ["gnb"])
            P.op("act", lambda e: e.mul(gab[:], gab[:], 1.0 - LAM_INIT), r=["gab"], w=["gab"])
            lbt = sbp("lbt", [128, 2])
            dma("sp", lbt[:], lbl, [], ["lbt"])
            lbv = sbp("lbv", [128, 2])
            tt("dve", lbv[:, 0:1], lbt[:, 0:1], lbt[:, 1:2], ALU.subtract, ["lbt"], ["lbv0"])
            act(lbv[:, 0:1], lbv[:, 0:1], AF.Sigmoid, ["lbv0"], ["lbv0"])
            ts("dve", lbv[:, 1:2], lbv[:, 0:1], -1.0, 1.0, ALU.mult, ALU.add, ["lbv0"], ["lbv"])
            Sf = sbp("Sf", [128, 128])
            Sb = sbp("Sb", [128, 128], BF16)
            XS = sbp("XS", [128, 8, NSB * 4], BF16)
            NR = 3
            ring_names = {}

            def rt(name, shape, dt=F32):
                tl = [sbp("%s_%d" % (name, i), shape, dt) for i in range(NR)]
                ring_names[name] = tl
                return tl
            CS = rt("CS", [128, 2, 128])
            T1 = rt("T1", [128, 4, 32]); T2 = rt("T2", [128, 4, 32])
            ROT = rt("ROT", [128, 256]); ROTb = rt("ROTb", [128, 256], BF16)
            Vf = rt("Vf", [128, 128]); Ib = rt("Ib", [128, 128], BF16)
            Gs = rt("Gs", [128, 128]); Gb = rt("Gb", [128, 128], BF16)
            sg = rt("sg", [128, 128]); lf = rt("lf", [128, 128]); kk = rt("kk", [128, 128])
            bT = rt("bT", [128, 128]); eb = rt("eb", [128, 128]); enb = rt("enb", [128, 128]); ehb = rt("ehb", [128, 128])
            qt = rt("qt", [128, 128], BF16); kt_ = rt("kt", [128, 128], BF16); kh = rt("kh", [128, 128], BF16)
            khT = rt("khT", [128, 128], BF16); attb = rt("attb", [128, 128], BF16)
            dec = rt("dec", [128, 2]); sq = rt("sq", [128, 128]); st4 = rt("st4", [128, 4])
            obb = rt("obb", [128, 128], BF16)
            qbS = rt("qbS", [128, 128])

            ph1 = ExitStack()
            ph1.__enter__()
            def sb1(name, shape, dt=F32):
                return sb(name, shape, dt, ph1)
            KT = sb1("KT", [128, T], BF16)
            QT = sb1("QT", [128, T], BF16)
            VA = sb1("VA", [128, NBLK, 132], BF16)
            P.op("pool", lambda e: e.memset(VA[:], 1.0), w=["VAinit"])
            XT = [sb1("XT%d" % i, [128, 8, 512], BF16) for i in range(2)]
            B = banks
            B5b = B[5][:].bitcast(BF16)

            def mixer_block(n, bi, xcols, fmq, fmf, fm_keys, rope_src, k_dst, v_dst, KTd, QTd, Vd, OBd, tag):
                s = bi % NR
                K = lambda nm: (nm, s)
                xk = fm_keys
                tmA = B[0][0:n, :]
                tmB = B[1][0:n, 0:128]
                for kc in range(8):
                    mm(tmA, xcols(kc), WM[:, kc, 0:512], kc == 0, kc == 7, xk + ["WM"], [("B",0)])
                for kc in range(8):
                    mm(tmB, xcols(kc), WM[:, kc, 512:640], kc == 0, kc == 7, xk + ["WM"], [("B",1)])
                FMFK = ("B", 4)
                if fmq is None:
                    FMFK = ("B", 3)
                    fmq = B[3][:, 0:n]
                    fmf = B[3][:, 8:8 + n]
                    for kc in range(8):
                        mm(fmq, WM[:, kc, 640:768], xcols(kc), kc == 0, kc == 7, xk + ["WM"], [("B",3)])
                    for kc in range(8):
                        mm(fmf, WM[:, kc, 768:896], xcols(kc), kc == 0, kc == 7, xk + ["WM"], [FMFK])
                cs = CS[s]
                dma("sp", cs[0:n], rope_src, [], [K("CS")])
                xa = tmA[:, 0:256].rearrange("p (g h d) -> p g h d", g=4, h=2)
                x1, x2 = xa[:, :, 0, :], xa[:, :, 1, :]
                cosv = cs[0:n, 0, :].rearrange("p (g d) -> p g d", g=4)
                sinv = cs[0:n, 1, :].rearrange("p (g d) -> p g d", g=4)
                rot = ROT[s][0:n].rearrange("p (g h d) -> p g h d", g=4, h=2)
                t1, t2 = T1[s][0:n], T2[s][0:n]
                tt("dve", t1, x1, cosv, ALU.mult, [("B",0), K("CS")], [K("T1")])
                tt("dve", t2, x2, sinv, ALU.mult, [("B",0), K("CS")], [K("T2")])
                tt("dve", rot[:, :, 0, :], t1, t2, ALU.subtract, [K("T1"), K("T2")], [K("ROT0")])
                tt("dve", t1, x2, cosv, ALU.mult, [("B",0), K("CS"), K("ROT0")], [K("T1")])
                tt("dve", t2, x1, sinv, ALU.mult, [("B",0), K("CS"), K("ROT0")], [K("T2")])
                tt("dve", rot[:, :, 1, :], t1, t2, ALU.add, [K("T1"), K("T2")], [K("ROT1")])
                RK = [K("ROT0"), K("ROT1")]
                dma("sp", k_dst, ROT[s][0:n, 0:128], RK, [])
                cp("act", ROTb[s][0:n], ROT[s][0:n], RK, [K("ROTb")])
                tp(B5b[:, 0:n], ROTb[s][0:n, 0:128], [K("ROTb")], [("B",5)])
                tp(B5b[:, 128:128 + n], ROTb[s][0:n, 128:256], [K("ROTb")], [("B",5)])
                cp("act", KTd, B5b[:, 0:n], [("B",5)], [("KT", tag)])
                cp("act", QTd, B5b[:, 128:128 + n], [("B",5)], [("QT", tag)])
                cp("act", Vf[s][0:n], tmA[:, 256:384], [("B",0)], [K("Vf")])
                dma("sp", v_dst, Vf[s][0:n], [K("Vf")], [])
                cp("pool", Vd, Vf[s][0:n], [K("Vf"), "VAinit"], [("V", tag)])
                cp("act", Ib[s][0:n], tmA[:, 384:512], [("B",0)], [K("Ib")])
                act(Gs[s][0:n], tmB, AF.Silu, [("B",1)], [K("Gs")])
                tt("pool", Gb[s][0:n], Gs[s][0:n], gnb[0:n], ALU.mult, [K("Gs"), "gnb"], [K("Gb")])
                cp("act", qbS[s][:, 0:n], fmq, [("B",3)], [K("qbS")])
                act(sg[s][:, 0:n], fmf, AF.Sigmoid, [FMFK], [K("sg")])
                ts("dve", sg[s][:, 0:n], sg[s][:, 0:n], lbv[:, 1:2], lbv[:, 0:1], ALU.mult, ALU.add, [K("sg"), "lbv", "lbv0"], [K("sg")])
                act(lf[s][:, 0:n], sg[s][:, 0:n], AF.Ln, [K("sg")], [K("lf")])
                ts("pool", kk[s][:, 0:n], sg[s][:, 0:n], -1.0, 1.0, ALU.mult, ALU.add, [K("sg")], [K("kk")])
                P.op("dve", lambda e: e.tensor_tensor_scan(bT[s][:, 0:n], ones[:, 0:n], lf[s][:, 0:n], 0.0, ALU.mult, ALU.add),
                     r=[K("lf"), "ones"], w=[K("bT")])
                act(eb[s][:, 0:n], bT[s][:, 0:n], AF.Exp, [K("bT")], [K("eb")])
                tt("dve", qt[s][:, 0:n], qbS[s][:, 0:n], eb[s][:, 0:n], ALU.mult, [K("qbS"), K("eb")], [K("qt")])
                act(enb[s][:, 0:n], bT[s][:, 0:n], AF.Exp, [K("bT")], [K("enb")], scale=-1.0)
                tt("pool", kt_[s][:, 0:n], kk[s][:, 0:n], enb[s][:, 0:n], ALU.mult, [K("kk"), K("enb")], [K("kt")])
                act(ehb[s][:, 0:n], bT[s][:, 0:n], AF.Exp, [K("bT")], [K("ehb")], scale=-1.0, bias=bT[s][:, n - 1:n])
                tt("pool", kh[s][:, 0:n], kk[s][:, 0:n], ehb[s][:, 0:n], ALU.mult, [K("kk"), K("ehb")], [K("kh")])
                act(dec[s][:, 0:1], bT[s][:, n - 1:n], AF.Exp, [K("bT")], [K("dec")])
                tp(B5b[0:n, 256:384], kh[s][:, 0:n], [K("kh")], [("B",5)])
                cp("act", khT[s][0:n], B5b[0:n, 256:384], [("B",5)], [K("khT")])
                attT = B[1][0:n, 128:128 + n]
                mm(attT, kt_[s][:, 0:n], qt[s][:, 0:n], True, True, [K("kt"), K("qt")], [("B",1)])
                tt("dve", attb[s][0:n, 0:n], attT, tri[0:n, 0:n], ALU.mult, [("B",1), "tri"], [K("attb")])
                o_ps = B[2][0:n, 0:128]
                mm(o_ps, qt[s][:, 0:n], Sb[:], True, False, [K("qt"), "Sb"], [("B",2)])
                mm(o_ps, attb[s][0:n, 0:n], Ib[s][0:n], False, True, [K("attb"), K("Ib")], [("B",2)])
                U = B[1][:, 256:384]
                mm(U, khT[s][0:n], Ib[s][0:n], True, True, [K("khT"), K("Ib")], [("B",1)])
                stt("dve", Sf[:], Sf[:], dec[s][:, 0:1], U, ALU.mult, ALU.add, ["Sf", K("dec"), ("B",1)], ["Sf"])
                cp("pool", Sb[:], Sf[:], ["Sf"], ["Sb"])
                act(sq[s][0:n], o_ps, AF.Square, [("B",2)], [K("sq"), K("ss")], accum=st4[s][0:n, 0:1])
                ts("dve", st4[s][0:n, 1:2], st4[s][0:n, 0:1], 1.0 / 128, RMS_EPS, ALU.mult, ALU.add, [K("ss")], [K("ms")])
                act(st4[s][0:n, 2:3], st4[s][0:n, 1:2], AF.Sqrt, [K("ms")], [K("sd")])
                P.op("dve", lambda e: e.reciprocal(st4[s][0:n, 3:4], st4[s][0:n, 2:3]), r=[K("sd")], w=[K("rstd")])
                stt("dve", obb[s][0:n], o_ps, st4[s][0:n, 3:4], Gb[s][0:n], ALU.mult, ALU.mult, [("B",2), K("rstd"), K("Gb")], [K("obb")])
                tp(B5b[:, 384:384 + n], obb[s][0:n], [K("obb")], [("B",5)])
                cp("act", OBd, B5b[:, 384:384 + n], [("B",5)], [("OBs", tag)])

            if STOP <= 1:
                P.cut = True
            P.op("pool", lambda e: e.memset(Sf[:], 0.0), w=["Sf"])
            P.op("pool", lambda e: e.memset(Sb[:], 0.0), w=["Sb"])
            for su in range(NSUP):
                xt = XT[su % 2]
                xkey = ("XT", su % 2)
                dma("sp", xt[:], xTb.ap()[:, su * 512:(su + 1) * 512].rearrange("(k p) n -> p k n", p=128),
                    [("xTb", su)], [xkey])
                fmq, fmf = B[3][:, :], B[4][:, :]
                for kc in range(8):
                    mm(fmq, WM[:, kc, 640:768], xt[:, kc, :], kc == 0, kc == 7, [xkey, "WM"], [("B",3)])
                for kc in range(8):
                    mm(fmf, WM[:, kc, 768:896], xt[:, kc, :], kc == 0, kc == 7, [xkey, "WM"], [("B", 4)])
                for j in range(4):
                    blk = su * 4 + j
                    c0 = blk * 128
                    mixer_block(128, blk, lambda kc, xt=xt, j=j: xt[:, kc, j * 128:(j + 1) * 128],
                                fmq[:, j * 128:(j + 1) * 128], fmf[:, j * 128:(j + 1) * 128], [xkey],
                                rope_p[c0:c0 + 128], k_p[c0:c0 + 128, :], v_p[c0:c0 + 128, :],
                                KT[:, c0:c0 + 128], QT[:, c0:c0 + 128], VA[:, blk, 0:128],
                                OB[:, blk // 4, (blk % 4) * 128:(blk % 4 + 1) * 128], blk)
            dma("sp", S_p, Sf[:], ["Sf"], [])
            P.barrier()

            if STOP <= 2:
                P.cut = True
            PT = [[sb1("PT%d_%d" % (i, m), [128, 512], BF16) for m in range(2)] for i in range(2)]
            ep = [sb1("ep%d" % i, [128, 8]) for i in range(NR)]
            o0 = [sb1("o0%d" % i, [128, 128]) for i in range(NR)]; o1 = [sb1("o1%d" % i, [128, 128]) for i in range(NR)]; oab = [sb1("oab%d" % i, [128, 128], BF16) for i in range(NR)]
            gi = 0
            for qb in range(NBLK):
                q0 = qb * 128
                nkb = qb + 1
                acc = [B[4][:, 0:129], B[5][:, 0:129]]
                for j0 in range(0, nkb, 4):
                    nj = min(4, nkb - j0)
                    st_ = gi % 2
                    gi += 1
                    for m in range(2):
                        stb = B[st_ * 2 + m]
                        for jj in range(nj):
                            j = j0 + jj
                            mm(stb[:, jj * 128:(jj + 1) * 128], KT[64 * m:64 * m + 64, j * 128:(j + 1) * 128],
                               QT[64 * m:64 * m + 64, q0:q0 + 128], True, True, [("KT", j), ("QT", qb)], [("B", st_ * 2 + m)])
                        act(PT[st_][m][:, 0:nj * 128], stb[:, 0:nj * 128], AF.Exp, [("B", st_ * 2 + m)], [("PT", st_, m)], scale=SCALE)
                        if j0 + nj == nkb:
                            dcol = (nj - 1) * 128
                            tt("dve", PT[st_][m][:, dcol:dcol + 128], PT[st_][m][:, dcol:dcol + 128], tri[:], ALU.mult,
                               [("PT", st_, m), "tri"], [("PT", st_, m)])
                    for jj in range(nj):
                        j = j0 + jj
                        for m in range(2):
                            mm(acc[m], PT[st_][m][:, jj * 128:(jj + 1) * 128], VA[:, j, 0:129], j == 0, j == nkb - 1,
                               [("PT", st_, m), ("V", j), "VAinit"], [("B", 4 + m)])
                s = qb % NR
                K = lambda nm: (nm + "_e", s)
                e_ = ep[s]
                P.op("dve", lambda e, e_=e_: e.reciprocal(e_[:, 0:1], acc[0][:, 128:129]), r=[("B", 4)], w=[K("rl0")])
                P.op("dve", lambda e, e_=e_: e.reciprocal(e_[:, 1:2], acc[1][:, 128:129]), r=[("B", 5)], w=[K("rl1")])
                tt("dve", e_[:, 2:3], e_[:, 1:2], lam_t[:, 1:2], ALU.mult, [K("rl1"), "lam"], [K("nl1")])
                ts("dve", o0[s][:], acc[0][:, 0:128], e_[:, 0:1], None, ALU.mult, None, [("B", 4), K("rl0")], [K("o0")])
                stt("dve", o1[s][:], acc[1][:, 0:128], e_[:, 2:3], o0[s][:], ALU.mult, ALU.add, [("B", 5), K("nl1"), K("o0")], [K("o1")])
                act(o0[s][:], o1[s][:], AF.Square, [K("o1")], [K("o0"), K("ss")], accum=e_[:, 3:4])
                ts("dve", e_[:, 4:5], e_[:, 3:4], 1.0 / 128, RMS_EPS, ALU.mult, ALU.add, [K("ss")], [K("ms")])
                act(e_[:, 5:6], e_[:, 4:5], AF.Sqrt, [K("ms")], [K("sd")])
                P.op("dve", lambda e, e_=e_: e.reciprocal(e_[:, 6:7], e_[:, 5:6]), r=[K("sd")], w=[K("rstd")])
                stt("dve", oab[s][:], o1[s][:], e_[:, 6:7], gab[:], ALU.mult, ALU.mult, [K("o1"), K("rstd"), "gab"], [K("oab")])
                tp(B[6][:].bitcast(BF16)[:, 0:128], oab[s][:], [K("oab")], [("B",6)])
                cp("act", OA[:, qb // 4, (qb % 4) * 128:(qb % 4 + 1) * 128], B[6][:].bitcast(BF16)[:, 0:128], [("B",6)], [("OAs", qb)])

            if STOP <= 3:
                P.cut = True
            P.barrier()
            ph1.close()
            P.op("pool", lambda e: e.dma_start(out=XS[:], in_=xsT.rearrange("(k p) n -> p k n", p=128)), w=["XS"], dma=True)
            a16 = sbp("a16", [128, 1])
            dma("sp", a16[:], a16_d, [], ["a16"])
            pti = sbp("pti", [128, NSB * NG8], I32)
            ptf = sbp("ptf", [128, NSB * NG8])
            gix = sbp("gix", [128, NSB * NG8], I32)
            dma("sp", pti[:], ptr_d, [], ["pti"])
            cp("dve", ptf[:], pti[:], ["pti"], ["ptf"])
            ts("dve", ptf[:], ptf[:], 16.0, a16[:, 0:1], ALU.mult, ALU.add, ["ptf", "a16"], ["ptf"])
            cp("dve", gix[:], ptf[:], ["ptf"], ["gix"])
            KG = [sbp("KG%d" % i, [128, NKT, 128], BF16) for i in range(2)]
            VG = [sbp("VG%d" % i, [128, NKT, 128], BF16) for i in range(2)]
            KTs = [sbp("KTs%d" % i, [128, 1024], BF16) for i in range(2)]
            Qbd = sbp("Qbd", [128, 8], BF16)
            P.op("pool", lambda e: e.memset(Qbd[:], 0.0), w=["Qbd"])
            KTn = sbp("KTn", [128, 4], BF16); QTn = sbp("QTn", [128, 4], BF16)
            Vn = sbp("Vn", [4, 128], BF16)
            PTs = [sbp("PTs%d" % i, [128, NKT * 8], BF16) for i in range(2)]
            PTn = sbp("PTn", [4, 8]); PTnb = sbp("PTnb", [4, 8], BF16)
            mnew = sbp("mnew", [4, 8])
            dma("sp", mnew[:], mnew_d, [], ["mnew"])
            rs = sbp("rs", [128, 8])
            OS = sbp("OS", [8, NSB, 128]); LS = sbp("LS", [8, NSB])
            OBs = sbp("OBs", [128, NSB * 4], BF16)
            ck_v = ck
            cv_v = cv
            for b in range(NSB):
                s2 = b % 2
                for G in range(NG8):
                    col = b * NG8 + G
                    P.op("pool", lambda e, G=G, col=col, s2=s2: e.indirect_dma_start(
                        out=KG[s2][:, G * 8:(G + 1) * 8, :].rearrange("p r d -> p (r d)"), out_offset=None, in_=ck_v,
                        in_offset=bass.IndirectOffsetOnAxis(ap=gix[:, col:col + 1], axis=0)),
                        r=["gix"], w=[("KG", s2, G)], dma=True)
                    P.op("pool", lambda e, G=G, col=col, s2=s2: e.indirect_dma_start(
                        out=VG[s2][:, G * 8:(G + 1) * 8, :].rearrange("p r d -> p (r d)"), out_offset=None, in_=cv_v,
                        in_offset=bass.IndirectOffsetOnAxis(ap=gix[:, col:col + 1], axis=0)),
                        r=["gix"], w=[("VG", s2, G)], dma=True)
                if STOP == 4 and os.environ.get("KSUB") == "1":
                    P.cut = True
                dma("sp", Sf[:], st_h[b], [], ["Sf"])
                cp("pool", Sb[:], Sf[:], ["Sf"], ["Sb"])
                mixer_block(4, NBLK + b, lambda kc, b=b: XS[:, kc, b * 4:(b + 1) * 4], None, None, ["XS"],
                            rope_s, k_s[b * 4:(b + 1) * 4, :], v_s[b * 4:(b + 1) * 4, :],
                            KTn[:], QTn[:], Vn[:], OBs[:, b * 4:(b + 1) * 4], ("s", b))
                dma("sp", S_s[b], Sf[:], ["Sf"], [])
                tg = ("s", b)
                cp("dve", Qbd[0:64, 0:4], QTn[0:64, :], [("QT", tg)], ["Qbd"])
                cp("dve", Qbd[64:128, 4:8], QTn[64:128, :], [("QT", tg)], ["Qbd"])
                if STOP == 4 and os.environ.get("KSUB") == "2":
                    P.cut = True
                STb = B[7][:, 0:NKT * 8]
                for G in range(NG8):
                    kb = (b * NG8 + G) % 2
                    psb = B[6][:].bitcast(BF16)
                    for r_ in range(8):
                        tp(psb[:, r_ * 128:(r_ + 1) * 128], KG[s2][:, G * 8 + r_, :], [("KG", s2, G)], [("B", 6)])
                    cp("act" if G % 2 == 0 else "dve", KTs[kb][:], psb[:, :], [("B", 6)], [("KTs", kb)])
                    for r_ in range(8):
                        kt = G * 8 + r_
                        mm(STb[:, kt * 8:(kt + 1) * 8], KTs[kb][:, r_ * 128:(r_ + 1) * 128], Qbd[:], True, True,
                           [("KTs", kb), "Qbd"], [("B",7)])
                STn = B[1][0:4, 384:392]
                mm(STn, KTn[:], Qbd[:], True, True, [("KT", tg), "Qbd"], [("B",1)])
                act(PTs[s2][:], STb, AF.Exp, [("B",7)], [("PTs", s2)], scale=SCALE)
                act(PTn[:], STn, AF.Exp, [("B",1)], ["PTn"], scale=SCALE)
                tt("dve", PTn[:], PTn[:], mnew[:], ALU.mult, ["PTn", "mnew"], ["PTn"])
                cp("dve", PTnb[:], PTn[:], ["PTn"], ["PTnb"])
                P.op("dve", lambda e, s2=s2: e.tensor_reduce(rs[:], PTs[s2][:].rearrange("p (k q) -> p q k", q=8), AX.X, ALU.add),
                     r=[("PTs", s2)], w=["rs"])
                Lp = B[1][0:8, 400:401]
                mm(Lp, rs[:], ones[:, 0:1], True, False, ["rs", "ones"], [("B",1)])
                mm(Lp, PTn[:], ones[0:4, 0:1], False, True, ["PTn", "ones"], [("B",1)])
                if STOP == 4 and os.environ.get("KSUB") == "3":
                    P.cut = True
                Op_ = B[4][0:8, 0:128]
                for kt in range(NKT):
                    mm(Op_, PTs[s2][:, kt * 8:(kt + 1) * 8], VG[s2][:, kt, :], kt == 0, False,
                       [("PTs", s2), ("VG", s2, kt // 8)], [("B",4)])
                mm(Op_, PTnb[:], Vn[:], False, True, ["PTnb", ("V", tg)], [("B",4)])
                cp("act", OS[:, b, :], Op_, [("B",4)], ["OS"])
                cp("act", LS[:, b:b + 1], Lp, [("B",1)], ["LS"])
            if STOP == 4 and os.environ.get("KSUB") == "4":
                P.cut = True
            RL = sbp("RL", [8, NSB])
            P.op("dve", lambda e: e.reciprocal(RL[:], LS[:]), r=["LS"], w=["RL"])
            ON = sbp("ON", [8, NSB, 128])
            for b in range(NSB):
                ts("dve", ON[:, b, :], OS[:, b, :], RL[:, b:b + 1], None, ALU.mult, None, ["OS", "RL"], [("ON", b)])
            cmbp = sbp("cmbp", [8, 8])
            cmb = sbp("cmb", [8, 4])
            dma("sp", cmbp[:], cmbp_d, [], ["cmbp"])
            stt("dve", cmb[:], cmbp[:, 4:8], lam_t[0:8, 1:2], cmbp[:, 0:4], ALU.mult, ALU.add, ["cmbp", "lam"], ["cmb"])
            osb = sbp("osb", [4, NSB, 128]); osq = sbp("osq", [4, 128])
            sst = sbp("sst", [4, 4, NSB])
            oasb = sbp("oasb", [4, NSB, 128], BF16)
            B6b = B[6][:].bitcast(BF16)
            for b in range(NSB):
                bk = B[b % 3]
                mm(bk[0:4, 0:128], cmb[:], ON[:, b, :], True, True, ["cmb", ("ON", b)], [("B", b % 3)])
                cp("act", osb[:, b, :], bk[0:4, 0:128], [("B", b % 3)], [("osb", b)])
                act(osq[:], osb[:, b, :], AF.Square, [("osb", b)], ["osq", ("sst0", b)], accum=sst[:, 0, b:b + 1])
            allb = [("sst0", b) for b in range(NSB)]
            ts("dve", sst[:, 1, :], sst[:, 0, :], 1.0 / 128, RMS_EPS, ALU.mult, ALU.add, allb, ["sst1"])
            act(sst[:, 2, :], sst[:, 1, :], AF.Sqrt, ["sst1"], ["sst2"])
            P.op("dve", lambda e: e.reciprocal(sst[:, 3, :], sst[:, 2, :]), r=["sst2"], w=["sst3"])
            for b in range(NSB):
                stt("dve", oasb[:, b, :], osb[:, b, :], sst[:, 3, b:b + 1], gab[0:4, :], ALU.mult, ALU.mult, [("osb", b), "sst3", "gab"], [("oasb", b)])
                tp(B6b[:, 256 + b * 4:256 + (b + 1) * 4], oasb[:, b, :], [("oasb", b)], [("B", 6)])
            for j in range(4):
                cp("act", OA[:, NSUP + j, 0:NSL * 4], B6b[:, 256 + j * NSL * 4:256 + (j + 1) * NSL * 4], [("B", 6)], [("OAsmp", j)])
                cp("dve", OB[:, NSUP + j, 0:NSL * 4], OBs[:, j * NSL * 4:(j + 1) * NSL * 4],
                   [("OBs", ("s", b)) for b in range(NSB)], [("OBsmp", j)])
            if STOP <= 4:
                P.cut = True
            P.barrier()
            sview = snd.ap().rearrange("(c two f) n -> f two c n", two=2, f=128)
            d1 = P.op("sp", lambda e: e.dma_start(out=sview[:, 0], in_=OA[:]), dma=True)
            d2 = P.op("sp", lambda e: e.dma_start(out=sview[:, 1], in_=OB[:]), dma=True)

        NGR = NCH // 4
        for k in range(NGR):
            P.op("pool", lambda e, k=k: e.collective_compute(
                "AllGather", ALU.bypass, replica_groups=[[0, 1, 2, 3], [4, 5, 6, 7]],
                ins=[snd.ap()[k * 1024:(k + 1) * 1024, :].opt()],
                outs=[gat.ap()[k * 4096:(k + 1) * 4096, :].opt()]), cc=True, extra=[d1, d2])
        P.barrier()

        if STOP <= 5:
            P.cut = True
        with ExitStack() as ph:
            def sbp(name, shape, dt=F32):
                return sb(name, shape, dt, ph)
            B = banks
            WOUT = sbp("WOUT", [128, 8, D], BF16)
            WDN = sbp("WDN", [128, NFC, D], BF16)
            for kc in range(0, 8, 2):
                P.op("pool", lambda e, kc=kc: e.dma_start(out=WOUT[:, kc:kc + 2, :], in_=wout_d.rearrange("(k p) c -> p k c", p=128)[:, kc:kc + 2, :]), w=["WOUT"], dma=True)
            for kc in range(0, NFC, 2):
                P.op("pool", lambda e, kc=kc: e.dma_start(out=WDN[:, kc:kc + 2, :], in_=wdn_d.rearrange("(k p) c -> p k c", p=128)[:, kc:kc + 2, :]), w=["WDN"], dma=True)
            LNP = sbp("LNP", [128, 4, D])
            dma("sp", LNP[:], lnp_d, [], ["LNP"])
            CVW = sbp("CVW", [128, NFC, 4])
            dma("sp", CVW[:], cvw_d, [], ["CVW"])
            SCV = sbp("SCV", [128, NFC, NSL, 2])
            dma("sp", SCV[:], scv_d, [], ["SCV"])
            hmask = sbp("hmask", [128, 1])
            dma("sp", hmask[:], hmask_d, [], ["hmask"])
            IDX = sbp("IDX", [128, (NTILE + 2) * 8], I32)
            dma("sp", IDX[:], idx_d, [], ["IDX"])
            carry = sbp("carry", [128, NFC, 2])
            CVS = sbp("CVS", [128, NFC, NSL, 2])
            XTt = sbp("XTt", [128, 8, 512], BF16)
            OAT = sbp("OAT", [128, 4, 512], BF16)
            OBT = sbp("OBT", [128, 4, 512], BF16)
            SG = sbp("SG", [128, 4, 512], BF16)
            MT = sbp("MT", [128, 8, 512], BF16)
            tA = sbp("tA", [128, 512]); tB = sbp("tB", [128, 512])
            HTM = sbp("HTM", [128, 4, D])
            HT = sbp("HT", [128, 8, 512], BF16)
            UT = sbp("UT", [128, NFC, 512], BF16)
            GT = UT[:, 0:16, :].rearrange("p (w h t) n -> p w h t n", w=2, h=4)
            WS = [sbp("WS%d" % i, [128, 2048], BF16) for i in range(4)]
            Xr = [sbp("Xr%d" % i, [128, D]) for i in range(2)]
            Z = [sbp("Z%d" % i, [128, D]) for i in range(2)]
            Hb = [sbp("Hb%d" % i, [128, D], BF16) for i in range(2)]
            bst = [sbp("bst%d" % i, [128, 16]) for i in range(2)]
            AE = [sbp("AE%d" % i, [128, 516]) for i in range(2)]
            Cc = [sbp("Cc%d" % i, [128, 512]) for i in range(2)]
            Gl = [sbp("Gl%d" % i, [128, 512]) for i in range(2)]
            wsi = [0]

            def wload(src, nel, name):
                i = wsi[0] % 4
                wsi[0] += 1
                P.op("pool", lambda e: e.dma_start(out=WS[i][:, 0:nel], in_=src), w=[("WS", i)], dma=True)
                return WS[i], ("WS", i)

            def layer_norm(zt, nt, gi_, out, rkeys, wkey, s):
                FM = 512
                nchk = D // FM
                st = bst[s]
                for c in range(nchk):
                    P.op("dve", lambda e, c=c: e.bn_stats(st[0:nt, c * 6:(c + 1) * 6], zt[0:nt, c * FM:(c + 1) * FM]),
                         r=rkeys, w=[("bst", s, c)])
                P.op("dve", lambda e: e.bn_aggr(st[0:nt, 12:14], st[0:nt, 0:12].rearrange("p (c k) -> p c k", k=6)),
                     r=[("bst", s, c) for c in range(nchk)], w=[("mv", s)])
                ts("dve", st[0:nt, 14:15], st[0:nt, 13:14], LN_EPS, None, ALU.add, None, [("mv", s)], [("ve", s)])
                act(st[0:nt, 14:15], st[0:nt, 14:15], AF.Sqrt, [("ve", s)], [("ve", s)])
                P.op("dve", lambda e: e.reciprocal(st[0:nt, 15:16], st[0:nt, 14:15]), r=[("ve", s)], w=[("rs", s)])
                ts("dve", zt[0:nt], zt[0:nt], st[0:nt, 12:13], st[0:nt, 15:16], ALU.subtract, ALU.mult, rkeys + [("mv", s), ("rs", s)], rkeys)
                tt("pool", zt[0:nt], zt[0:nt], LNP[0:nt, gi_, :], ALU.mult, rkeys + ["LNP"], rkeys)
                tt("pool", out, zt[0:nt], LNP[0:nt, gi_ + 1, :], ALU.add, rkeys + ["LNP"], [wkey])

            P.op("pool", lambda e: e.memset(carry[:], 0.0), w=["carry"])
            gview = gat.ap()

            def gather(dst, col, key):
                P.op("pool", lambda e: e.indirect_dma_start(out=dst, out_offset=None, in_=gview,
                     in_offset=bass.IndirectOffsetOnAxis(ap=IDX[:, col:col + 1], axis=0)), r=["IDX"], w=[key], dma=True)

            ybase = 0
            for ti in range(NTILE + 1):
                stile = ti == 0
                if stile:
                    n, c0, segs, seglen = NS_T + 2, 0, NSL, 4
                    ny = NS_T
                else:
                    n, c0, segs, seglen = 512, NS_T + 2 + (ti - 1) * 512, 1, 512
                    ny = 512
                tk = ("tile", ti)
                P.op("pool", lambda e, c0=c0, n=n: e.dma_start(out=XTt[:, :, 0:n], in_=xT_t[:, c0:c0 + n].rearrange("(k p) n -> p k n", p=128)),
                     w=["XTt"], dma=True)
                if stile:
                    for wh in range(2):
                        for h in range(4):
                            for pt in range(2):
                                gather(GT[:, wh, h, pt, :], wh * 8 + h * 2 + pt, ("GT", wh, h, pt))
                    gk = [("GT", wh, h, pt) for wh in range(2) for h in range(4) for pt in range(2)]
                    cp("dve", OAT[:, :, 0:NS_T], GT[:, 0, :, 0, 0:NS_T], gk, ["OAT"])
                    cp("dve", OAT[:, :, NS_T:NS_T + 2], GT[:, 1, :, 0, 510:512], gk, ["OAT"])
                    cp("dve", OBT[:, :, 0:NS_T], GT[:, 0, :, 1, 0:NS_T], gk, ["OBT"])
                    cp("dve", OBT[:, :, NS_T:NS_T + 2], GT[:, 1, :, 1, 510:512], gk, ["OBT"])
                else:
                    for h in range(4):
                        gather(OAT[:, h, :], (ti + 1) * 8 + h * 2, "OAT")
                        gather(OBT[:, h, :], (ti + 1) * 8 + h * 2 + 1, "OBT")
                for cb in range(8):
                    sl = (cb % 2) * 2
                    for gi2 in range(2):
                        wsb, wk = wload(wg_d[gi2 * 8 + cb], 1024, "wg")
                        ps = B[gi2][:, 0:n]
                        for kc in range(8):
                            mm(ps, wsb[:, kc * 128:(kc + 1) * 128], XTt[:, kc, 0:n], kc == 0, kc == 7, [wk, "XTt"], [("B", gi2)])
                        act(SG[:, sl + gi2, 0:n], ps, AF.Sigmoid, [("B", gi2)], [("SG", sl + gi2)])
                    wsa, wka = wload(wa_d[cb], 512, "wa")
                    wsb2, wkb = wload(wb_d[cb], 512, "wb")
                    pa = B[2 + cb % 2][:, 0:n]
                    pb = B[4 + cb % 2][:, 0:n]
                    for kc in range(4):
                        mm(pa, wsa[:, kc * 128:(kc + 1) * 128], OAT[:, kc, 0:n], kc == 0, kc == 3, [wka, "OAT"], [("B", 2 + cb % 2)])
                    for kc in range(4):
                        mm(pb, wsb2[:, kc * 128:(kc + 1) * 128], OBT[:, kc, 0:n], kc == 0, kc == 3, [wkb, "OBT"], [("B", 4 + cb % 2)])
                    tt("dve", tA[:, 0:n], pa, SG[:, sl, 0:n], ALU.mult, [("B", 2 + cb % 2), ("SG", sl)], ["tA"])
                    tt("dve", tB[:, 0:n], pb, SG[:, sl + 1, 0:n], ALU.mult, [("B", 4 + cb % 2), ("SG", sl + 1)], ["tB"])
                    tt("pool", MT[:, cb, 0:n], tA[:, 0:n], tB[:, 0:n], ALU.add, ["tA", "tB"], [("MT", cb)])
                nblk = (n + 127) // 128
                MTk = [("MT", cb) for cb in range(8)]
                for tb in range(nblk):
                    t0 = tb * 128
                    nt = min(128, n - t0)
                    s = tb % 2
                    dma("sp", Xr[s][0:nt], x_t[c0 + t0:c0 + t0 + nt, :], [], [("Xr", s)])
                    for hf in range(2):
                        ps = B[6 + hf][0:nt, :]
                        for kc in range(8):
                            mm(ps, MT[:, kc, t0:t0 + nt], WOUT[:, kc, hf * 512:(hf + 1) * 512], kc == 0, kc == 7, MTk + ["WOUT"], [("B", 6 + hf)])
                        stt("dve", Z[s][0:nt, hf * 512:(hf + 1) * 512], Xr[s][0:nt, hf * 512:(hf + 1) * 512], ALPHA, ps, ALU.mult, ALU.add,
                            [("Xr", s), ("B", 6 + hf)], [("Z", s)])
                    layer_norm(Z[s], nt, 0, HTM[0:nt, tb, :], [("Z", s)], ("HTM", tb), s)
                    cp("act", Hb[s][0:nt], HTM[0:nt, tb, :], [("HTM", tb)], [("Hb", s)])
                    psb = B[tb % 2][:].bitcast(BF16)
                    for kc in range(8):
                        tp(psb[:, kc * 128:kc * 128 + nt], Hb[s][0:nt, kc * 128:(kc + 1) * 128], [("Hb", s)], [("B", tb % 2)])
                    cp("act", HT[:, :, t0:t0 + nt], psb[:, :].rearrange("p (k t) -> p k t", k=8)[:, :, 0:nt], [("B", tb % 2)], [("HT", tb)])
                HTk = [("HT", tb) for tb in range(nblk)]
                for fc in range(NFC):
                    wsu, wku = wload(wup_d[fc], 2048, "wup")
                    s = fc % 2
                    pa = B[2 + s][:, 0:n]
                    pg = B[4 + s][:, 0:n]
                    for kc in range(8):
                        mm(pa, wsu[:, kc * 128:(kc + 1) * 128], HT[:, kc, 0:n], kc == 0, kc == 7, [wku] + HTk, [("B", 2 + s)])
                    for kc in range(8):
                        mm(pg, wsu[:, 1024 + kc * 128:1024 + (kc + 1) * 128], HT[:, kc, 0:n], kc == 0, kc == 7, [wku] + HTk, [("B", 4 + s)])
                    nsg = segs * seglen
                    ae = AE[s][:, 0:segs * (seglen + 2)].rearrange("p (g t) -> p g t", g=segs)
                    cp("act", ae[:, :, 2:], pa[:, 0:nsg].rearrange("p (g t) -> p g t", g=segs), [("B", 2 + s)], [("AE", s)])
                    if stile:
                        cp("pool", ae[:, :, 0:2], SCV[:, fc, :, :], ["SCV", ("AE", s)], [("AE", s)])
                        ts("dve", carry[:, fc, :], pa[:, NS_T:NS_T + 2], hmask[:, 0:1], None, ALU.mult, None, [("B", 2 + s), "hmask"], ["carry"])
                        cp("pool", CVS[:, fc, :, :], ae[:, :, seglen:seglen + 2], [("AE", s)], ["CVS"])
                    else:
                        cp("pool", ae[:, :, 0:2], carry[:, fc, :].unsqueeze(1), ["carry", ("AE", s)], [("AE", s)])
                        cp("pool", carry[:, fc, :].unsqueeze(1), ae[:, :, seglen:seglen + 2], [("AE", s)], ["carry"])
                    cc_ = Cc[s][:, 0:nsg].rearrange("p (g t) -> p g t", g=segs)
                    ts("dve", cc_, ae[:, :, 2:], CVW[:, fc, 2:3], CVW[:, fc, 3:4], ALU.mult, ALU.add, [("AE", s), "CVW"], [("Cc", s)])
                    stt("dve", cc_, ae[:, :, 1:seglen + 1], CVW[:, fc, 1:2], cc_, ALU.mult, ALU.add, [("AE", s), "CVW", ("Cc", s)], [("Cc", s)])
                    stt("dve", cc_, ae[:, :, 0:seglen], CVW[:, fc, 0:1], cc_, ALU.mult, ALU.add, [("AE", s), "CVW", ("Cc", s)], [("Cc", s)])
                    act(Gl[s][:, 0:nsg], Cc[s][:, 0:nsg], AF.Gelu, [("Cc", s)], [("Gl", s)])
                    tt("dve", UT[:, fc, 0:nsg], pg[:, 0:nsg], Gl[s][:, 0:nsg], ALU.mult, [("B", 4 + s), ("Gl", s)], [("UT", fc)])
                UTk = [("UT", fc) for fc in range(NFC)]
                nblk4 = (ny + 127) // 128
                for tb in range(nblk4):
                    t0 = tb * 128
                    nt = min(128, ny - t0)
                    s = tb % 2
                    for hf in range(2):
                        ps = B[6 + hf][0:nt, :]
                        for fc in range(NFC):
                            mm(ps, UT[:, fc, t0:t0 + nt], WDN[:, fc, hf * 512:(hf + 1) * 512], fc == 0, fc == NFC - 1, UTk + ["WDN"], [("B", 6 + hf)])
                        stt("dve", Z[s][0:nt, hf * 512:(hf + 1) * 512], HTM[0:nt, tb, hf * 512:(hf + 1) * 512], ALPHA, ps, ALU.mult, ALU.add,
                            [("HTM", tb), ("B", 6 + hf)], [("Z", s)])
                    layer_norm(Z[s], nt, 2, Xr[s][0:nt], [("Z", s)], ("Xr", s), s)
                    dma("sp", y_o[ybase + t0:ybase + t0 + nt, :], Xr[s][0:nt], [("Xr", s)], [])
                ybase += ny
            dma("sp", cv_s, CVS[:], ["CVS"], [])
            dma("sp", cv_p, carry[:], ["carry"], [])

        with ExitStack() as fin:
            P.finalize(fin)
    return nc


def _prep(inputs):
    g = lambda k: np.asarray(inputs[k])
    xp, xs = g("x_prompt"), g("x_sample")
    Bp, T, _ = xp.shape
    Bs, TS, _ = xs.shape
    ck, cvv = g("cache_k"), g("cache_v")
    NPHYS, _, PG, H, _, DH = ck.shape
    pt = g("page_table")
    PAST = pt.shape[1] * PG
    NSB = Bs // 2
    cfg = dict(T=T, NSB=NSB, PAST=PAST, NPHYS=NPHYS)
    NG8 = PAST // 1024
    TT = T // 4
    NTILE = TT // 512
    NSUP = T // 512
    NCH = NSUP + 4
    NSL = NSB // 4
    NS_T = NSL * 4
    w_in = g("w_in")[0]
    f32 = np.float32
    half = DH // 2
    inv = (10000.0 ** (-np.arange(half, dtype=f32) * 2.0 / DH)).astype(f32)

    def rope_tab(pos):
        ang = pos.astype(f32)[:, None] * inv[None, :]
        c, s = np.cos(ang).astype(f32), np.sin(ang).astype(f32)
        return np.ascontiguousarray(np.stack([np.tile(c, (1, 4)), np.tile(s, (1, 4))], axis=1))
    rope_p = rope_tab(np.arange(T))
    rope_s = rope_tab(PAST + np.arange(TS))
    tri = np.triu(np.ones((128, 128), f32))
    ident = np.eye(128, dtype=f32)
    mnew = np.tile(np.triu(np.ones((4, 4), f32)), (1, 2))
    cmbp = np.zeros((8, 8), f32)
    cmbp[0:4, 0:4] = np.eye(4)
    cmbp[4:8, 4:8] = np.eye(4)
    a16 = (np.arange(128) % 16).astype(f32).reshape(128, 1)
    lamv = np.tile(np.concatenate([g("lambda_q1")[0], g("lambda_q2")[0], g("lambda_k1")[0], g("lambda_k2")[0]])[None, :], (128, 1)).astype(f32)
    ga_b = np.tile(g("subln_g")[0][None, :], (128, 1)).astype(f32)
    gn_b = np.tile(g("hgrn_norm_g")[0][None, :], (128, 1)).astype(f32)
    wgt = w_in[:, 3584:5632].reshape(8, 128, 16, 128).transpose(2, 1, 0, 3).reshape(16, 128, 1024)
    wa = g("w_branch_a")[0].reshape(4, 128, 8, 128).transpose(2, 1, 0, 3).reshape(8, 128, 512)
    wb = g("w_branch_b")[0].reshape(4, 128, 8, 128).transpose(2, 1, 0, 3).reshape(8, 128, 512)
    wup = g("w_up")[0].reshape(8, 128, 2, NFC, 128).transpose(3, 1, 2, 0, 4).reshape(NFC, 128, 2048)
    lnp = np.stack([g("ln1_g")[0], g("ln1_b")[0], g("ln2_g")[0], g("ln2_b")[0]], 0)
    lnp = np.ascontiguousarray(np.tile(lnp[None], (128, 1, 1))).astype(f32)
    cvw = np.concatenate([g("conv_w")[0], g("conv_b")], 0).reshape(4, NFC, 128).transpose(2, 1, 0)
    shared = dict(rope_p=rope_p, rope_s=rope_s, lamv=lamv, ga_b=ga_b, gn_b=gn_b, tri=tri, ident=ident, mnew=mnew,
                  cmbp=cmbp, a16=a16, wg_t=np.ascontiguousarray(wgt), wa_t=np.ascontiguousarray(wa),
                  wb_t=np.ascontiguousarray(wb), wout=np.ascontiguousarray(g("w_out")[0]),
                  wup_t=np.ascontiguousarray(wup), wdn=np.ascontiguousarray(g("w_down")[0]), lnp=lnp,
                  cvw=np.ascontiguousarray(cvw))
    cols = {"ka": 512, "qa": 0, "va": 1024, "ib": 2560, "gb": 3072, "qb": 1536, "fb": 2048}
    order = ["ka", "qa", "va", "ib", "gb", "qb", "fb"]
    sconv = g("state_conv")[:, 0]
    st_all = g("state_hgrn")[:, 0]
    lbl_all = g("lb_logits")
    in_maps = []
    for c in range(8):
        gI, h = c // 4, c % 4
        j = h
        m = dict(shared)
        m["xT_seq"] = np.ascontiguousarray(xp[gI].T)
        sb_ids = np.arange(gI * NSB, (gI + 1) * NSB)
        m["xsT"] = np.ascontiguousarray(xs[sb_ids].reshape(NSB * TS, D).T)
        m["wm"] = np.ascontiguousarray(np.concatenate([w_in[:, cols[k] + h * 128: cols[k] + (h + 1) * 128] for k in order], 1))
        m["lbl"] = np.ascontiguousarray(lbl_all[:, h * 128:(h + 1) * 128].T)
        ptb = pt[sb_ids].reshape(NSB, NG8, 8)
        m["ptr"] = np.ascontiguousarray(np.repeat(ptb.transpose(2, 0, 1), 16, axis=0).reshape(128, NSB * NG8)).astype(np.int32)
        m["ck"] = np.ascontiguousarray(ck[:, 0, :, h]).reshape(NPHYS * 16, 1024)
        m["cv"] = np.ascontiguousarray(cvv[:, 0, :, h]).reshape(NPHYS * 16, 1024)
        m["st_h"] = np.ascontiguousarray(st_all[sb_ids, h])
        tb_ids = sb_ids[j * NSL:(j + 1) * NSL]
        xs_t = xs[tb_ids].reshape(NS_T, D)
        p0 = j * TT
        halo = xp[gI, max(p0 - 2, 0):max(p0 - 2, 0) + 2]
        xcols = np.concatenate([xs_t, halo, xp[gI, p0:p0 + TT]], 0)
        m["x_t"] = np.ascontiguousarray(xcols)
        m["xT_t"] = np.ascontiguousarray(xcols.T)
        m["scv"] = np.ascontiguousarray(sconv[tb_ids].reshape(NSL, 2, NFC, 128).transpose(3, 2, 0, 1))
        m["hmask"] = np.full((128, 1), 0.0 if j == 0 else 1.0, f32)
        idx = np.zeros((128, NTILE + 2, 4, 2), np.int64)
        pr = np.arange(128)
        chunks = [NSUP + j, max(j * NTILE - 1, 0)] + [j * NTILE + k for k in range(NTILE)]
        for ci, chn in enumerate(chunks):
            for hh in range(4):
                for part in range(2):
                    idx[:, ci, hh, part] = (chn // 4) * 4096 + hh * 1024 + (chn % 4) * 256 + part * 128 + pr
        m["idx"] = np.ascontiguousarray(idx.reshape(128, -1)).astype(np.int32)
        in_maps.append({k: np.ascontiguousarray(v) for k, v in m.items()})
    return cfg, in_maps


_CACHE = {}


def kernel(**inputs):
    cfg, in_maps = _prep(inputs)
    key = tuple(sorted(cfg.items()))
    if key not in _CACHE:
        _CACHE[key] = build(cfg)
    nc = _CACHE[key]
    res = run_bass_kernel_spmd(nc, in_maps, core_ids=list(range(8))).results
    T, NSB = cfg["T"], cfg["NSB"]
    TT = T // 4
    NSL = NSB // 4
    NS_T = NSL * 4
    f32 = np.float32
    Bs = NSB * 2
    y_p = np.zeros((2, T, D), f32); y_s = np.zeros((Bs, 4, D), f32)
    k_p = np.zeros((2, 1, T, 4, 2, 64), f32); v_p = np.zeros((2, 1, T, 4, 128), f32)
    h_p = np.zeros((2, 1, 4, 128, 128), f32); c_p = np.zeros((2, 1, 2, DFF), f32)
    k_s = np.zeros((Bs, 1, 4, 4, 2, 64), f32); v_s = np.zeros((Bs, 1, 4, 4, 128), f32)
    h_s = np.zeros((Bs, 1, 4, 128, 128), f32); c_s = np.zeros((Bs, 1, 2, DFF), f32)
    for c in range(8):
        r = res[c]
        gI, h = c // 4, c % 4
        j = h
        k_p[gI, 0, :, h] = r["k_p"].reshape(T, 2, 64)
        v_p[gI, 0, :, h] = r["v_p"]
        h_p[gI, 0, h] = r["S_p"]
        sl = slice(gI * NSB, (gI + 1) * NSB)
        k_s[sl, 0, :, h] = r["k_s"].reshape(NSB, 4, 2, 64)
        v_s[sl, 0, :, h] = r["v_s"].reshape(NSB, 4, 128)
        h_s[sl, 0, h] = r["S_s"]
        tb = slice(gI * NSB + j * NSL, gI * NSB + (j + 1) * NSL)
        y_s[tb] = r["y_o"][0:NS_T].reshape(NSL, 4, D)
        y_p[gI, j * TT:(j + 1) * TT] = r["y_o"][NS_T:]
        c_s[tb, 0] = r["cv_s"].transpose(2, 3, 1, 0).reshape(NSL, 2, DFF)
        if j == 3:
            c_p[gI, 0] = r["cv_p"].transpose(2, 1, 0).reshape(2, DFF)
    return (y_p, y_s, k_p, v_p, h_p, c_p, k_s, v_s, h_s, c_s)
```

```python
import math
import os
from contextlib import ExitStack
import numpy as np
import concourse.bass as bass
import concourse.mybir as mybir
from concourse.bass_utils import run_bass_kernel_spmd

F32 = mybir.dt.float32
BF16 = mybir.dt.bfloat16
I32 = mybir.dt.int32
ALU = mybir.AluOpType
AF = mybir.ActivationFunctionType
AX = mybir.AxisListType

D = 1024
DFF = 2816
NFC = DFF // 128
LN_EPS = 1e-5
RMS_EPS = 1e-5
ALPHA = 2.0 ** 0.25
LAM_INIT = 0.8 - 0.6 * math.exp(0.0)
SCALE = 64 ** -0.5
RING = 12


class Op:
    __slots__ = ("eng", "fn", "dma", "deps", "needs_inc", "sem", "val", "idx", "cc")

    def __init__(self, eng, fn, dma=False, cc=False):
        self.eng, self.fn, self.dma, self.cc = eng, fn, dma, cc
        self.deps = []
        self.needs_inc = dma or cc
        self.sem = None
        self.val = 0


class Prog:
    ENGS = ("pe", "act", "dve", "pool", "sp")

    def __init__(self, nc):
        self.nc = nc
        self.ops = {e: [] for e in self.ENGS}
        self.last_w = {}
        self.readers = {}
        self.all_dma = []
        self.cut = False

    def op(self, eng, fn, r=(), w=(), dma=False, cc=False, extra=()):
        o = Op(eng, fn, dma, cc)
        if self.cut:
            return o
        deps = set(extra)
        for k in r:
            d = self.last_w.get(k)
            if d is not None:
                deps.add(d)
        for k in w:
            d = self.last_w.get(k)
            if d is not None:
                deps.add(d)
            for rd in self.readers.get(k, ()):
                deps.add(rd)
        async_o = dma or cc
        for d in deps:
            if d is o:
                continue
            if d.eng == "pe" and eng == "pe" and not async_o:
                continue
            o.deps.append(d)
            d.needs_inc = True
        for k in w:
            self.last_w[k] = o
            self.readers[k] = []
        for k in r:
            lst = self.readers.setdefault(k, [])
            if not async_o:
                lst[:] = [x for x in lst if x.eng != eng or x.dma or x.cc]
            lst.append(o)
        self.ops[eng].append(o)
        if async_o:
            self.all_dma.append(o)
        return o

    def barrier(self):
        if self.cut:
            return
        lasts = [self.ops[e][-1] for e in self.ENGS if self.ops[e]]
        dm = list(self.all_dma)
        self.all_dma = []
        for e in self.ENGS:
            self.op(e, None, extra=lasts + dm)
        self.last_w = {}
        self.readers = {}

    def finalize(self, stack):
        nc = self.nc
        esem = {e: stack.enter_context(nc.semaphore("es_" + e)) for e in self.ENGS}
        rings = {e: [stack.enter_context(nc.semaphore("r%s%d" % (e, i))) for i in range(RING)]
                 for e in ("sp", "pool")}
        ccsem = stack.enter_context(nc.semaphore("ccsem"))
        cnt = {e: 0 for e in self.ENGS}
        dcnt = {"sp": 0, "pool": 0}
        cccnt = 0
        for e in self.ENGS:
            for o in self.ops[e]:
                if o.cc:
                    cccnt += 1
                    o.sem, o.val = ccsem, cccnt
                elif o.dma:
                    i = dcnt[e]
                    dcnt[e] += 1
                    o.idx = i
                    o.sem, o.val = rings[e][i % RING], 16 * (i // RING + 1)
                elif o.needs_inc and o.fn is not None:
                    cnt[e] += 1
                    o.sem, o.val = esem[e], cnt[e]
        block = stack.enter_context(nc.Block())

        def run(e, eng):
            waited = {}
            for o in self.ops[e]:
                ws = []
                for d in o.deps:
                    if d.sem is None:
                        continue
                    ws.append((d.sem, d.val))
                if o.dma and o.idx >= RING:
                    ws.append((o.sem, o.val - 16))
                for s, v in ws:
                    key = id(s)
                    if waited.get(key, 0) >= v:
                        continue
                    waited[key] = v
                    eng.wait_ge(s, v)
                if o.fn is None:
                    continue
                ins = o.fn(eng)
                if o.cc:
                    ins.then_inc(o.sem)
                elif o.dma:
                    ins.then_inc(o.sem, 16)
                elif o.sem is not None:
                    ins.then_inc(o.sem, 1)
            if e in ("sp", "pool"):
                for i in range(min(RING, dcnt[e])):
                    last = ((dcnt[e] - 1 - i) // RING) * RING + i
                    eng.wait_ge(rings[e][i], 16 * (last // RING + 1))
            if e == "pool" and cccnt:
                eng.wait_ge(ccsem, cccnt)

        @block.tensor
        def _(eng):
            run("pe", eng)

        @block.scalar
        def _(eng):
            run("act", eng)

        @block.vector
        def _(eng):
            run("dve", eng)

        @block.gpsimd
        def _(eng):
            run("pool", eng)

        @block.sync
        def _(eng):
            run("sp", eng)


def build(cfg):
    T, NSB, PAST, NPHYS = cfg["T"], cfg["NSB"], cfg["PAST"], cfg["NPHYS"]
    NBLK = T // 128
    NSUP = T // 512
    NG8 = PAST // 1024
    NKT = NG8 * 8
    TT = T // 4
    NTILE = TT // 512
    NSL = NSB // 4
    NS_T = NSL * 4
    NCH = NSUP + 4
    NCOL = NS_T + 2 + TT
    nc = bass.Bass("TRN2", target_bir_lowering=False)
    P = Prog(nc)
    STOP = int(os.environ.get("KSTOP", "99"))

    def din(name, shape, dt=F32):
        return nc.dram_tensor(name, list(shape), dt, kind="ExternalInput").ap()

    def dout(name, shape, dt=F32):
        return nc.dram_tensor(name, list(shape), dt, kind="ExternalOutput").ap()

    xT_seq = din("xT_seq", [D, T])
    xsT = din("xsT", [D, NSB * 4])
    wm = din("wm", [D, 896])
    rope_p = din("rope_p", [T, 2, 128])
    rope_s = din("rope_s", [4, 2, 128])
    lbl = din("lbl", [128, 2])
    lamv = din("lamv", [128, 256])
    ga_b = din("ga_b", [128, 128])
    gn_b = din("gn_b", [128, 128])
    tri_d = din("tri", [128, 128])
    ident_d = din("ident", [128, 128])
    mnew_d = din("mnew", [4, 8])
    cmbp_d = din("cmbp", [8, 8])
    a16_d = din("a16", [128, 1])
    ptr_d = din("ptr", [128, NSB * NG8], I32)
    ck = din("ck", [NPHYS * 16, 1024])
    cv = din("cv", [NPHYS * 16, 1024])
    st_h = din("st_h", [NSB, 128, 128])
    xT_t = din("xT_t", [D, NCOL])
    x_t = din("x_t", [NCOL, D])
    wg_d = din("wg_t", [16, 128, 8 * 128])
    wa_d = din("wa_t", [8, 128, 4 * 128])
    wb_d = din("wb_t", [8, 128, 4 * 128])
    wout_d = din("wout", [D, D])
    wup_d = din("wup_t", [NFC, 128, 2 * 8 * 128])
    wdn_d = din("wdn", [DFF, D])
    lnp_d = din("lnp", [128, 4, D])
    cvw_d = din("cvw", [128, NFC, 4])
    scv_d = din("scv", [128, NFC, NSL, 2])
    hmask_d = din("hmask", [128, 1])
    idx_d = din("idx", [128, (NTILE + 2) * 8], I32)
    k_p = dout("k_p", [T, 128])
    v_p = dout("v_p", [T, 128])
    k_s = dout("k_s", [NSB * 4, 128])
    v_s = dout("v_s", [NSB * 4, 128])
    S_p = dout("S_p", [128, 128])
    S_s = dout("S_s", [NSB, 128, 128])
    y_o = dout("y_o", [NS_T + TT, D])
    cv_s = dout("cv_s", [128, NFC, NSL, 2])
    cv_p = dout("cv_p", [128, NFC, 2])
    xTb = nc.dram_tensor("xTb", [D, T], BF16)
    snd = nc.dram_tensor("snd", [NCH * 256, 512], BF16)
    gat = nc.dram_tensor("gat", [4 * NCH * 256, 512], BF16)
    wgb = nc.dram_tensor("wgb", [16 * 128, 1024], BF16)
    wab = nc.dram_tensor("wab", [8 * 128, 512], BF16)
    wbb = nc.dram_tensor("wbb", [8 * 128, 512], BF16)
    wupb = nc.dram_tensor("wupb", [NFC * 128, 2048], BF16)
    woutb = nc.dram_tensor("woutb", [D, D], BF16)
    wdnb = nc.dram_tensor("wdnb", [DFF, D], BF16)

    with ExitStack() as top:
        def sb(name, shape, dt=F32, st=top):
            return st.enter_context(nc.sbuf_tensor("s_" + name, list(shape), dt))

        banks = [top.enter_context(nc.psum_tensor("ps%d" % i, [128, 512], F32)) for i in range(8)]

        ident = sb("ident", [128, 128], BF16)
        tri = sb("tri", [128, 128], F32)
        ones = sb("ones", [128, 128], F32)
        trib = sb("trib", [128, 128], BF16)
        lam_t = sb("lam_t", [128, 4], F32)

        def mm(out, lhsT, rhs, start, stop, r, w):
            return P.op("pe", lambda e: e.matmul(out, lhsT, rhs, start=start, stop=stop), r=r, w=w)

        def tp(out, in_, r, w):
            n = in_.shape[0]
            return P.op("pe", lambda e: e.transpose(out, in_, ident[0:n, 0:n]), r=list(r) + ["ident"], w=w)

        def act(out, in_, func, r, w, bias=None, scale=None, accum=None):
            kw = {}
            if bias is not None:
                kw["bias"] = bias
            if scale is not None:
                kw["scale"] = scale
            if accum is not None:
                kw["accum_out"] = accum
            return P.op("act", lambda e: e.activation(out, in_, func, **kw), r=r, w=w)

        def tt(eng, out, a, b, op, r, w):
            return P.op(eng, lambda e: e.tensor_tensor(out, a, b, op), r=r, w=w)

        def ts(eng, out, a, s1, s2, op0, op1, r, w):
            if op1 is None:
                return P.op(eng, lambda e: e.tensor_scalar(out, a, s1, None, op0), r=r, w=w)
            return P.op(eng, lambda e: e.tensor_scalar(out, a, s1, s2, op0, op1), r=r, w=w)

        def stt(eng, out, a, s, b, op0, op1, r, w):
            return P.op(eng, lambda e: e.scalar_tensor_tensor(out, a, s, b, op0, op1), r=r, w=w)

        def cp(eng, out, in_, r, w):
            if eng == "act":
                return P.op("act", lambda e: e.copy(out, in_), r=r, w=w)
            return P.op(eng, lambda e: e.tensor_copy(out, in_), r=r, w=w)

        def dma(q, out, in_, r, w):
            return P.op(q, lambda e: e.dma_start(out=out, in_=in_), r=r, w=w, dma=True)

        identf = sb("identf", [128, 128], F32)
        dma("sp", identf[:], ident_d, [], ["identf"])
        cp("dve", ident[:], identf[:], ["identf"], ["ident"])
        dma("sp", tri[:], tri_d, [], ["tri"])
        cp("dve", trib[:], tri[:], ["tri"], ["trib"])
        P.op("pool", lambda e: e.memset(ones[:], 1.0), w=["ones"])
        lamt = sb("lamt", [128, 256], F32)
        dma("sp", lamt[:], lamv, [], ["lamt"])
        lj = sb("lj", [128, 128], F32)
        ls = sb("ls", [128, 4], F32)
        tt("dve", lj[:], lamt[:, 0:128], lamt[:, 128:256], ALU.mult, ["lamt"], ["lj"])
        P.op("dve", lambda e: e.tensor_reduce(ls[:, 0:2], lj[:].rearrange("p (a b) -> p a b", a=2), AX.X, ALU.add),
             r=["lj"], w=["ls"])
        act(ls[:, 2:4], ls[:, 0:2], AF.Exp, ["ls"], ["ls2"])
        tt("dve", lam_t[:, 0:1], ls[:, 2:3], ls[:, 3:4], ALU.subtract, ["ls2"], ["lam0"])
        ts("dve", lam_t[:, 0:1], lam_t[:, 0:1], LAM_INIT, None, ALU.add, None, ["lam0"], ["lam0"])
        ts("dve", lam_t[:, 1:2], lam_t[:, 0:1], -1.0, None, ALU.mult, None, ["lam0"], ["lam"])

        for i in range(0, T, 2048):
            w_ = min(2048, T - i)
            P.op("pool", lambda e, i=i, w_=w_: e.dma_start(out=xTb.ap()[:, i:i + w_], in_=xT_seq[:, i:i + w_]),
                 w=[("xTb", i // 512 + j) for j in range(w_ // 512)], dma=True)

        def precast(dst, src2d, rows):
            for r0 in range(0, rows, 512):
                r1 = min(rows, r0 + 512)
                P.op("pool", lambda e, r0=r0, r1=r1: e.dma_start(out=dst.ap()[r0:r1, :], in_=src2d[r0:r1, :]), dma=True)
        with ExitStack() as ph:
            def sbp(name, shape, dt=F32):
                return sb(name, shape, dt, ph)
            OA = sbp("OA", [128, NCH, 512], BF16)
            OB = sbp("OB", [128, NCH, 512], BF16)
            WM = sbp("WM", [128, 8, 896], BF16)
            P.op("pool", lambda e: e.dma_start(out=WM[:], in_=wm.rearrange("(k p) c -> p k c", p=128)), w=["WM"], dma=True)
            precast(wgb, wg_d.rearrange("a p c -> (a p) c"), 16 * 128)
            precast(wab, wa_d.rearrange("a p c -> (a p) c"), 8 * 128)
            precast(wbb, wb_d.rearrange("a p c -> (a p) c"), 8 * 128)
            precast(woutb, wout_d, D)
            precast(wupb, wup_d.rearrange("a p c -> (a p) c"), NFC * 128)
            precast(wdnb, wdn_d, DFF)
            gab = sbp("gab", [128, 128])
            gnb = sbp("gnb", [128, 128])
            dma("sp", gab[:], ga_b, [], ["gab"])
            dma("sp", gnb[:], gn_b, [], ["gnb"])
            P.op("act", lambda e: e.mul(gab[:], gab[:], 1.0 - LAM_INIT), r=["gab"], w=["gab"])
            lbt = sbp("lbt", [128, 2])
            dma("sp", lbt[:], lbl, [], ["lbt"])
            lbv = sbp("lbv", [128, 2])
            tt("dve", lbv[:, 0:1], lbt[:, 0:1], lbt[:, 1:2], ALU.subtract, ["lbt"], ["lbv0"])
            act(lbv[:, 0:1], lbv[:, 0:1], AF.Sigmoid, ["lbv0"], ["lbv0"])
            ts("dve", lbv[:, 1:2], lbv[:, 0:1], -1.0, 1.0, ALU.mult, ALU.add, ["lbv0"], ["lbv"])
            Sf = sbp("Sf", [128, 128])
            Sb = sbp("Sb", [128, 128], BF16)
            XS = sbp("XS", [128, 8, NSB * 4], BF16)
            NR = 3
            ring_names = {}

            def rt(name, shape, dt=F32):
                tl = [sbp("%s_%d" % (name, i), shape, dt) for i in range(NR)]
                ring_names[name] = tl
                return tl
            CS = rt("CS", [128, 2, 128])
            T1 = rt("T1", [128, 4, 32]); T2 = rt("T2", [128, 4, 32])
            ROT = rt("ROT", [128, 256]); ROTb = rt("ROTb", [128, 256], BF16)
            Vf = rt("Vf", [128, 128]); Ib = rt("Ib", [128, 128], BF16)
            Gs = rt("Gs", [128, 128]); Gb = rt("Gb", [128, 128], BF16)
            sg = rt("sg", [128, 128]); lf = rt("lf", [128, 128]); kk = rt("kk", [128, 128])
            bT = rt("bT", [128, 128]); eb = rt("eb", [128, 128]); enb = rt("enb", [128, 128]); ehb = rt("ehb", [128, 128])
            qt = rt("qt", [128, 128], BF16); kt_ = rt("kt", [128, 128], BF16); kh = rt("kh", [128, 128], BF16)
            khT = rt("khT", [128, 128], BF16); attb = rt("attb", [128, 128], BF16)
            dec = rt("dec", [128, 2]); sq = rt("sq", [128, 128]); st4 = rt("st4", [128, 4])
            obb = rt("obb", [128, 128], BF16)
            qbS = rt("qbS", [128, 128])

            ph1 = ExitStack()
            ph1.__enter__()
            def sb1(name, shape, dt=F32):
                return sb(name, shape, dt, ph1)
            KT = sb1("KT", [128, T], BF16)
            QT = sb1("QT", [128, T], BF16)
            VA = sb1("VA", [128, NBLK, 132], BF16)
            P.op("pool", lambda e: e.memset(VA[:], 1.0), w=["VAinit"])
            XT = [sb1("XT%d" % i, [128, 8, 512], BF16) for i in range(2)]
            B = banks
            B5b = B[5][:].bitcast(BF16)

            def mixer_block(n, bi, xcols, fmq, fmf, fm_keys, rope_src, k_dst, v_dst, KTd, QTd, Vd, OBd, tag):
                s = bi % NR
                K = lambda nm: (nm, s)
                xk = fm_keys
                tmA = B[0][0:n, :]
                tmB = B[1][0:n, 0:128]
                for kc in range(8):
                    mm(tmA, xcols(kc), WM[:, kc, 0:512], kc == 0, kc == 7, xk + ["WM"], [("B",0)])
                for kc in range(8):
                    mm(tmB, xcols(kc), WM[:, kc, 512:640], kc == 0, kc == 7, xk + ["WM"], [("B",1)])
                FMFK = ("B", 4)
                if fmq is None:
                    FMFK = ("B", 3)
                    fmq = B[3][:, 0:n]
                    fmf = B[3][:, 8:8 + n]
                    for kc in range(8):
                        mm(fmq, WM[:, kc, 640:768], xcols(kc), kc == 0, kc == 7, xk + ["WM"], [("B",3)])
                    for kc in range(8):
                        mm(fmf, WM[:, kc, 768:896], xcols(kc), kc == 0, kc == 7, xk + ["WM"], [FMFK])
                cs = CS[s]
                dma("sp", cs[0:n], rope_src, [], [K("CS")])
                xa = tmA[:, 0:256].rearrange("p (g h d) -> p g h d", g=4, h=2)
                x1, x2 = xa[:, :, 0, :], xa[:, :, 1, :]
                cosv = cs[0:n, 0, :].rearrange("p (g d) -> p g d", g=4)
                sinv = cs[0:n, 1, :].rearrange("p (g d) -> p g d", g=4)
                rot = ROT[s][0:n].rearrange("p (g h d) -> p g h d", g=4, h=2)
                t1, t2 = T1[s][0:n], T2[s][0:n]
                tt("dve", t1, x1, cosv, ALU.mult, [("B",0), K("CS")], [K("T1")])
                tt("dve", t2, x2, sinv, ALU.mult, [("B",0), K("CS")], [K("T2")])
                tt("dve", rot[:, :, 0, :], t1, t2, ALU.subtract, [K("T1"), K("T2")], [K("ROT0")])
                tt("dve", t1, x2, cosv, ALU.mult, [("B",0), K("CS"), K("ROT0")], [K("T1")])
                tt("dve", t2, x1, sinv, ALU.mult, [("B",0), K("CS"), K("ROT0")], [K("T2")])
                tt("dve", rot[:, :, 1, :], t1, t2, ALU.add, [K("T1"), K("T2")], [K("ROT1")])
                RK = [K("ROT0"), K("ROT1")]
                dma("sp", k_dst, ROT[s][0:n, 0:128], RK, [])
                cp("act", ROTb[s][0:n], ROT[s][0:n], RK, [K("ROTb")])
                tp(B5b[:, 0:n], ROTb[s][0:n, 0:128], [K("ROTb")], [("B",5)])
                tp(B5b[:, 128:128 + n], ROTb[s][0:n, 128:256], [K("ROTb")], [("B",5)])
                cp("act", KTd, B5b[:, 0:n], [("B",5)], [("KT", tag)])
                cp("act", QTd, B5b[:, 128:128 + n], [("B",5)], [("QT", tag)])
                cp("act", Vf[s][0:n], tmA[:, 256:384], [("B",0)], [K("Vf")])
                dma("sp", v_dst, Vf[s][0:n], [K("Vf")], [])
                cp("pool", Vd, Vf[s][0:n], [K("Vf"), "VAinit"], [("V", tag)])
                cp("act", Ib[s][0:n], tmA[:, 384:512], [("B",0)], [K("Ib")])
                act(Gs[s][0:n], tmB, AF.Silu, [("B",1)], [K("Gs")])
                tt("pool", Gb[s][0:n], Gs[s][0:n], gnb[0:n], ALU.mult, [K("Gs"), "gnb"], [K("Gb")])
                cp("act", qbS[s][:, 0:n], fmq, [("B",3)], [K("qbS")])
                act(sg[s][:, 0:n], fmf, AF.Sigmoid, [FMFK], [K("sg")])
                ts("dve", sg[s][:, 0:n], sg[s][:, 0:n], lbv[:, 1:2], lbv[:, 0:1], ALU.mult, ALU.add, [K("sg"), "lbv", "lbv0"], [K("sg")])
                act(lf[s][:, 0:n], sg[s][:, 0:n], AF.Ln, [K("sg")], [K("lf")])
                ts("pool", kk[s][:, 0:n], sg[s][:, 0:n], -1.0, 1.0, ALU.mult, ALU.add, [K("sg")], [K("kk")])
                P.op("dve", lambda e: e.tensor_tensor_scan(bT[s][:, 0:n], ones[:, 0:n], lf[s][:, 0:n], 0.0, ALU.mult, ALU.add),
                     r=[K("lf"), "ones"], w=[K("bT")])
                act(eb[s][:, 0:n], bT[s][:, 0:n], AF.Exp, [K("bT")], [K("eb")])
                tt("dve", qt[s][:, 0:n], qbS[s][:, 0:n], eb[s][:, 0:n], ALU.mult, [K("qbS"), K("eb")], [K("qt")])
                act(enb[s][:, 0:n], bT[s][:, 0:n], AF.Exp, [K("bT")], [K("enb")], scale=-1.0)
                tt("pool", kt_[s][:, 0:n], kk[s][:, 0:n], enb[s][:, 0:n], ALU.mult, [K("kk"), K("enb")], [K("kt")])
                act(ehb[s][:, 0:n], bT[s][:, 0:n], AF.Exp, [K("bT")], [K("ehb")], scale=-1.0, bias=bT[s][:, n - 1:n])
                tt("pool", kh[s][:, 0:n], kk[s][:, 0:n], ehb[s][:, 0:n], ALU.mult, [K("kk"), K("ehb")], [K("kh")])
                act(dec[s][:, 0:1], bT[s][:, n - 1:n], AF.Exp, [K("bT")], [K("dec")])
                tp(B5b[0:n, 256:384], kh[s][:, 0:n], [K("kh")], [("B",5)])
                cp("act", khT[s][0:n], B5b[0:n, 256:384], [("B",5)], [K("khT")])
                attT = B[1][0:n, 128:128 + n]
                mm(attT, kt_[s][:, 0:n], qt[s][:, 0:n], True, True, [K("kt"), K("qt")], [("B",1)])
                tt("dve", attb[s][0:n, 0:n], attT, tri[0:n, 0:n], ALU.mult, [("B",1), "tri"], [K("attb")])
                o_ps = B[2][0:n, 0:128]
                mm(o_ps, qt[s][:, 0:n], Sb[:], True, False, [K("qt"), "Sb"], [("B",2)])
                mm(o_ps, attb[s][0:n, 0:n], Ib[s][0:n], False, True, [K("attb"), K("Ib")], [("B",2)])
                U = B[1][:, 256:384]
                mm(U, khT[s][0:n], Ib[s][0:n], True, True, [K("khT"), K("Ib")], [("B",1)])
                stt("dve", Sf[:], Sf[:], dec[s][:, 0:1], U, ALU.mult, ALU.add, ["Sf", K("dec"), ("B",1)], ["Sf"])
                cp("pool", Sb[:], Sf[:], ["Sf"], ["Sb"])
                act(sq[s][0:n], o_ps, AF.Square, [("B",2)], [K("sq"), K("ss")], accum=st4[s][0:n, 0:1])
                ts("dve", st4[s][0:n, 1:2], st4[s][0:n, 0:1], 1.0 / 128, RMS_EPS, ALU.mult, ALU.add, [K("ss")], [K("ms")])
                act(st4[s][0:n, 2:3], st4[s][0:n, 1:2], AF.Sqrt, [K("ms")], [K("sd")])
                P.op("dve", lambda e: e.reciprocal(st4[s][0:n, 3:4], st4[s][0:n, 2:3]), r=[K("sd")], w=[K("rstd")])
                stt("dve", obb[s][0:n], o_ps, st4[s][0:n, 3:4], Gb[s][0:n], ALU.mult, ALU.mult, [("B",2), K("rstd"), K("Gb")], [K("obb")])
                tp(B5b[:, 384:384 + n], obb[s][0:n], [K("obb")], [("B",5)])
                cp("act", OBd, B5b[:, 384:384 + n], [("B",5)], [("OBs", tag)])

            if STOP <= 1:
                P.cut = True
            P.op("pool", lambda e: e.memset(Sf[:], 0.0), w=["Sf"])
            P.op("pool", lambda e: e.memset(Sb[:], 0.0), w=["Sb"])
            for su in range(NSUP):
                xt = XT[su % 2]
                xkey = ("XT", su % 2)
                dma("sp", xt[:], xTb.ap()[:, su * 512:(su + 1) * 512].rearrange("(k p) n -> p k n", p=128),
                    [("xTb", su)], [xkey])
                fmq, fmf = B[3][:, :], B[4][:, :]
                for kc in range(8):
                    mm(fmq, WM[:, kc, 640:768], xt[:, kc, :], kc == 0, kc == 7, [xkey, "WM"], [("B",3)])
                for kc in range(8):
                    mm(fmf, WM[:, kc, 768:896], xt[:, kc, :], kc == 0, kc == 7, [xkey, "WM"], [("B", 4)])
                for j in range(4):
                    blk = su * 4 + j
                    c0 = blk * 128
                    mixer_block(128, blk, lambda kc, xt=xt, j=j: xt[:, kc, j * 128:(j + 1) * 128],
                                fmq[:, j * 128:(j + 1) * 128], fmf[:, j * 128:(j + 1) * 128], [xkey],
                                rope_p[c0:c0 + 128], k_p[c0:c0 + 128, :], v_p[c0:c0 + 128, :],
                                KT[:, c0:c0 + 128], QT[:, c0:c0 + 128], VA[:, blk, 0:128],
                                OB[:, blk // 4, (blk % 4) * 128:(blk % 4 + 1) * 128], blk)
            dma("sp", S_p, Sf[:], ["Sf"], [])
            P.barrier()

            if STOP <= 2:
                P.cut = True
            PT = [[sb1("PT%d_%d" % (i, m), [128, 512], BF16) for m in range(2)] for i in range(2)]
            ep = [sb1("ep%d" % i, [128, 8]) for i in range(NR)]
            o0 = [sb1("o0%d" % i, [128, 128]) for i in range(NR)]; o1 = [sb1("o1%d" % i, [128, 128]) for i in range(NR)]; oab = [sb1("oab%d" % i, [128, 128], BF16) for i in range(NR)]
            groups = []
            gi = 0
            for qb in range(NBLK):
                nkb = qb + 1
                for j0 in range(0, nkb, 4):
                    groups.append((qb, j0, min(4, nkb - j0), gi % 2))
                    gi += 1

            def accs(qb):
                base = 4 + 2 * (qb % 2)
                return [B[base][:, 0:129], B[base + 1][:, 0:129]], base

            def emit_qk(g):
                qb, j0, nj, st_ = g
                q0 = qb * 128
                nkb = qb + 1
                for m in range(2):
                    stb = B[st_ * 2 + m]
                    for jj in range(nj):
                        j = j0 + jj
                        mm(stb[:, jj * 128:(jj + 1) * 128], KT[64 * m:64 * m + 64, j * 128:(j + 1) * 128],
                           QT[64 * m:64 * m + 64, q0:q0 + 128], True, True, [("KT", j), ("QT", qb)], [("B", st_ * 2 + m)])
                    act(PT[st_][m][:, 0:nj * 128], stb[:, 0:nj * 128], AF.Exp, [("B", st_ * 2 + m)], [("PT", st_, m)], scale=SCALE)
                    if j0 + nj == nkb:
                        dcol = (nj - 1) * 128
                        tt("pool", PT[st_][m][:, dcol:dcol + 128], PT[st_][m][:, dcol:dcol + 128], trib[:], ALU.mult,
                           [("PT", st_, m), "trib"], [("PT", st_, m)])

            def emit_pv(g):
                qb, j0, nj, st_ = g
                nkb = qb + 1
                acc, base = accs(qb)
                for jj in range(nj):
                    j = j0 + jj
                    for m in range(2):
                        mm(acc[m], PT[st_][m][:, jj * 128:(jj + 1) * 128], VA[:, j, 0:129], j == 0, j == nkb - 1,
                           [("PT", st_, m), ("V", j), "VAinit"], [("B", base + m)])
                if j0 + nj != nkb:
                    return
                s = qb % NR
                K = lambda nm: (nm + "_e", s)
                e_ = ep[s]
                P.op("dve", lambda e, e_=e_: e.reciprocal(e_[:, 0:1], acc[0][:, 128:129]), r=[("B", base)], w=[K("rl0")])
                P.op("dve", lambda e, e_=e_: e.reciprocal(e_[:, 1:2], acc[1][:, 128:129]), r=[("B", base + 1)], w=[K("rl1")])
                tt("dve", e_[:, 2:3], e_[:, 1:2], lam_t[:, 1:2], ALU.mult, [K("rl1"), "lam"], [K("nl1")])
                ts("dve", o0[s][:], acc[0][:, 0:128], e_[:, 0:1], None, ALU.mult, None, [("B", base), K("rl0")], [K("o0")])
                stt("dve", o1[s][:], acc[1][:, 0:128], e_[:, 2:3], o0[s][:], ALU.mult, ALU.add, [("B", base + 1), K("nl1"), K("o0")], [K("o1")])
                P.op("dve", lambda e, e_=e_, s=s: e.scalar_tensor_tensor(o0[s][:], o1[s][:], 1.0, o1[s][:], ALU.mult, ALU.mult, accum_out=e_[:, 3:4]),
                     r=[K("o1")], w=[K("o0"), K("ss")])
                ts("dve", e_[:, 4:5], e_[:, 3:4], 1.0 / 128, RMS_EPS, ALU.mult, ALU.add, [K("ss")], [K("ms")])
                act(e_[:, 5:6], e_[:, 4:5], AF.Ln, [K("ms")], [K("sd")])
                act(e_[:, 6:7], e_[:, 5:6], AF.Exp, [K("sd")], [K("rstd")], scale=-0.5)
                stt("dve", oab[s][:], o1[s][:], e_[:, 6:7], gab[:], ALU.mult, ALU.mult, [K("o1"), K("rstd"), "gab"], [K("oab")])
                tpo = B[base][:].bitcast(BF16)[:, 512:640]
                tp(tpo, oab[s][:], [K("oab")], [("B", base)])
                cp("act", OA[:, qb // 4, (qb % 4) * 128:(qb % 4 + 1) * 128], tpo, [("B", base)], [("OAs", qb)])

            for i, g in enumerate(groups):
                emit_qk(g)
                if i >= 1:
                    emit_pv(groups[i - 1])
            emit_pv(groups[-1])

            if STOP <= 3:
                P.cut = True
            P.barrier()
            ph1.close()
            P.op("pool", lambda e: e.dma_start(out=XS[:], in_=xsT.rearrange("(k p) n -> p k n", p=128)), w=["XS"], dma=True)
            a16 = sbp("a16", [128, 1])
            dma("sp", a16[:], a16_d, [], ["a16"])
            pti = sbp("pti", [128, NSB * NG8], I32)
            ptf = sbp("ptf", [128, NSB * NG8])
            gix = sbp("gix", [128, NSB * NG8], I32)
            dma("sp", pti[:], ptr_d, [], ["pti"])
            cp("dve", ptf[:], pti[:], ["pti"], ["ptf"])
            ts("dve", ptf[:], ptf[:], 16.0, a16[:, 0:1], ALU.mult, ALU.add, ["ptf", "a16"], ["ptf"])
            cp("dve", gix[:], ptf[:], ["ptf"], ["gix"])
            KG = [sbp("KG%d" % i, [128, NKT, 128], BF16) for i in range(2)]
            VG = [sbp("VG%d" % i, [128, NKT, 128], BF16) for i in range(2)]
            KTs = [sbp("KTs%d" % i, [128, 1024], BF16) for i in range(2)]
            Qbd = sbp("Qbd", [128, 8], BF16)
            P.op("pool", lambda e: e.memset(Qbd[:], 0.0), w=["Qbd"])
            KTn = sbp("KTn", [128, 4], BF16); QTn = sbp("QTn", [128, 4], BF16)
            Vn = sbp("Vn", [4, 128], BF16)
            PTs = [sbp("PTs%d" % i, [128, NKT * 8], BF16) for i in range(2)]
            PTn = sbp("PTn", [4, 8]); PTnb = sbp("PTnb", [4, 8], BF16)
            mnew = sbp("mnew", [4, 8])
            dma("sp", mnew[:], mnew_d, [], ["mnew"])
            rs = sbp("rs", [128, 8])
            OS = sbp("OS", [8, NSB, 128]); LS = sbp("LS", [8, NSB])
            OBs = sbp("OBs", [128, NSB * 4], BF16)
            ck_v = ck
            cv_v = cv
            for b in range(NSB):
                s2 = b % 2
                for G in range(NG8):
                    col = b * NG8 + G
                    P.op("pool", lambda e, G=G, col=col, s2=s2: e.indirect_dma_start(
                        out=KG[s2][:, G * 8:(G + 1) * 8, :].rearrange("p r d -> p (r d)"), out_offset=None, in_=ck_v,
                        in_offset=bass.IndirectOffsetOnAxis(ap=gix[:, col:col + 1], axis=0)),
                        r=["gix"], w=[("KG", s2, G)], dma=True)
                    P.op("pool", lambda e, G=G, col=col, s2=s2: e.indirect_dma_start(
                        out=VG[s2][:, G * 8:(G + 1) * 8, :].rearrange("p r d -> p (r d)"), out_offset=None, in_=cv_v,
                        in_offset=bass.IndirectOffsetOnAxis(ap=gix[:, col:col + 1], axis=0)),
                        r=["gix"], w=[("VG", s2, G)], dma=True)
                if STOP == 4 and os.environ.get("KSUB") == "1":
                    P.cut = True
                dma("sp", Sf[:], st_h[b], [], ["Sf"])
                cp("pool", Sb[:], Sf[:], ["Sf"], ["Sb"])
                mixer_block(4, NBLK + b, lambda kc, b=b: XS[:, kc, b * 4:(b + 1) * 4], None, None, ["XS"],
                            rope_s, k_s[b * 4:(b + 1) * 4, :], v_s[b * 4:(b + 1) * 4, :],
                            KTn[:], QTn[:], Vn[:], OBs[:, b * 4:(b + 1) * 4], ("s", b))
                dma("sp", S_s[b], Sf[:], ["Sf"], [])
                tg = ("s", b)
                cp("dve", Qbd[0:64, 0:4], QTn[0:64, :], [("QT", tg)], ["Qbd"])
                cp("dve", Qbd[64:128, 4:8], QTn[64:128, :], [("QT", tg)], ["Qbd"])
                if STOP == 4 and os.environ.get("KSUB") == "2":
                    P.cut = True
                STb = B[7][:, 0:NKT * 8]
                for G in range(NG8):
                    kb = (b * NG8 + G) % 2
                    psb = B[6][:].bitcast(BF16)
                    for r_ in range(8):
                        tp(psb[:, r_ * 128:(r_ + 1) * 128], KG[s2][:, G * 8 + r_, :], [("KG", s2, G)], [("B", 6)])
                    cp("act" if G % 2 == 0 else "dve", KTs[kb][:], psb[:, :], [("B", 6)], [("KTs", kb)])
                    for r_ in range(8):
                        kt = G * 8 + r_
                        mm(STb[:, kt * 8:(kt + 1) * 8], KTs[kb][:, r_ * 128:(r_ + 1) * 128], Qbd[:], True, True,
                           [("KTs", kb), "Qbd"], [("B",7)])
                STn = B[1][0:4, 384:392]
                mm(STn, KTn[:], Qbd[:], True, True, [("KT", tg), "Qbd"], [("B",1)])
                act(PTs[s2][:], STb, AF.Exp, [("B",7)], [("PTs", s2)], scale=SCALE)
                act(PTn[:], STn, AF.Exp, [("B",1)], ["PTn"], scale=SCALE)
                tt("dve", PTn[:], PTn[:], mnew[:], ALU.mult, ["PTn", "mnew"], ["PTn"])
                cp("dve", PTnb[:], PTn[:], ["PTn"], ["PTnb"])
                P.op("dve", lambda e, s2=s2: e.tensor_reduce(rs[:], PTs[s2][:].rearrange("p (k q) -> p q k", q=8), AX.X, ALU.add),
                     r=[("PTs", s2)], w=["rs"])
                Lp = B[1][0:8, 400:401]
                mm(Lp, rs[:], ones[:, 0:1], True, False, ["rs", "ones"], [("B",1)])
                mm(Lp, PTn[:], ones[0:4, 0:1], False, True, ["PTn", "ones"], [("B",1)])
                if STOP == 4 and os.environ.get("KSUB") == "3":
                    P.cut = True
                Op_ = B[4][0:8, 0:128]
                for kt in range(NKT):
                    mm(Op_, PTs[s2][:, kt * 8:(kt + 1) * 8], VG[s2][:, kt, :], kt == 0, False,
                       [("PTs", s2), ("VG", s2, kt // 8)], [("B",4)])
                mm(Op_, PTnb[:], Vn[:], False, True, ["PTnb", ("V", tg)], [("B",4)])
                cp("act", OS[:, b, :], Op_, [("B",4)], ["OS"])
                cp("act", LS[:, b:b + 1], Lp, [("B",1)], ["LS"])
            if STOP == 4 and os.environ.get("KSUB") == "4":
                P.cut = True
            RL = sbp("RL", [8, NSB])
            P.op("dve", lambda e: e.reciprocal(RL[:], LS[:]), r=["LS"], w=["RL"])
            ON = sbp("ON", [8, NSB, 128])
            for b in range(NSB):
                ts("dve", ON[:, b, :], OS[:, b, :], RL[:, b:b + 1], None, ALU.mult, None, ["OS", "RL"], [("ON", b)])
            cmbp = sbp("cmbp", [8, 8])
            cmb = sbp("cmb", [8, 4])
            dma("sp", cmbp[:], cmbp_d, [], ["cmbp"])
            stt("dve", cmb[:], cmbp[:, 4:8], lam_t[0:8, 1:2], cmbp[:, 0:4], ALU.mult, ALU.add, ["cmbp", "lam"], ["cmb"])
            osb = sbp("osb", [4, NSB, 128]); osq = sbp("osq", [4, 128])
            sst = sbp("sst", [4, 4, NSB])
            oasb = sbp("oasb", [4, NSB, 128], BF16)
            B6b = B[6][:].bitcast(BF16)
            for b in range(NSB):
                bk = B[b % 3]
                mm(bk[0:4, 0:128], cmb[:], ON[:, b, :], True, True, ["cmb", ("ON", b)], [("B", b % 3)])
                cp("act", osb[:, b, :], bk[0:4, 0:128], [("B", b % 3)], [("osb", b)])
                act(osq[:], osb[:, b, :], AF.Square, [("osb", b)], ["osq", ("sst0", b)], accum=sst[:, 0, b:b + 1])
            allb = [("sst0", b) for b in range(NSB)]
            ts("dve", sst[:, 1, :], sst[:, 0, :], 1.0 / 128, RMS_EPS, ALU.mult, ALU.add, allb, ["sst1"])
            act(sst[:, 2, :], sst[:, 1, :], AF.Sqrt, ["sst1"], ["sst2"])
            P.op("dve", lambda e: e.reciprocal(sst[:, 3, :], sst[:, 2, :]), r=["sst2"], w=["sst3"])
            for b in range(NSB):
                stt("dve", oasb[:, b, :], osb[:, b, :], sst[:, 3, b:b + 1], gab[0:4, :], ALU.mult, ALU.mult, [("osb", b), "sst3", "gab"], [("oasb", b)])
                tp(B6b[:, 256 + b * 4:256 + (b + 1) * 4], oasb[:, b, :], [("oasb", b)], [("B", 6)])
            for j in range(4):
                cp("act", OA[:, NSUP + j, 0:NSL * 4], B6b[:, 256 + j * NSL * 4:256 + (j + 1) * NSL * 4], [("B", 6)], [("OAsmp", j)])
                cp("dve", OB[:, NSUP + j, 0:NSL * 4], OBs[:, j * NSL * 4:(j + 1) * NSL * 4],
                   [("OBs", ("s", b)) for b in range(NSB)], [("OBsmp", j)])
            if STOP <= 4:
                P.cut = True
            P.barrier()
            sview = snd.ap().rearrange("(c two f) n -> f two c n", two=2, f=128)
            d1 = P.op("sp", lambda e: e.dma_start(out=sview[:, 0], in_=OA[:]), dma=True)
            d2 = P.op("sp", lambda e: e.dma_start(out=sview[:, 1], in_=OB[:]), dma=True)

        NGR = NCH // 4
        for k in range(NGR):
            P.op("pool", lambda e, k=k: e.collective_compute(
                "AllGather", ALU.bypass, replica_groups=[[0, 1, 2, 3], [4, 5, 6, 7]],
                ins=[snd.ap()[k * 1024:(k + 1) * 1024, :].opt()],
                outs=[gat.ap()[k * 4096:(k + 1) * 4096, :].opt()]), cc=True, extra=[d1, d2])
        P.barrier()

        if STOP <= 5:
            P.cut = True
        with ExitStack() as ph:
            def sbp(name, shape, dt=F32):
                return sb(name, shape, dt, ph)
            B = banks
            WOUT = sbp("WOUT", [128, 8, D], BF16)
            WDN = sbp("WDN", [128, NFC, D], BF16)
            for kc in range(0, 8, 2):
                P.op("sp", lambda e, kc=kc: e.dma_start(out=WOUT[:, kc:kc + 2, :], in_=woutb.ap().rearrange("(k p) c -> p k c", p=128)[:, kc:kc + 2, :]), w=["WOUT"], dma=True)
            for kc in range(0, NFC, 2):
                P.op("sp", lambda e, kc=kc: e.dma_start(out=WDN[:, kc:kc + 2, :], in_=wdnb.ap().rearrange("(k p) c -> p k c", p=128)[:, kc:kc + 2, :]), w=["WDN"], dma=True)
            LNP = sbp("LNP", [128, 4, D])
            dma("sp", LNP[:], lnp_d, [], ["LNP"])
            CVW = sbp("CVW", [128, NFC, 4])
            dma("sp", CVW[:], cvw_d, [], ["CVW"])
            SCV = sbp("SCV", [128, NFC, NSL, 2])
            dma("sp", SCV[:], scv_d, [], ["SCV"])
            hmask = sbp("hmask", [128, 1])
            dma("sp", hmask[:], hmask_d, [], ["hmask"])
            IDX = sbp("IDX", [128, (NTILE + 2) * 8], I32)
            dma("sp", IDX[:], idx_d, [], ["IDX"])
            carry = sbp("carry", [128, NFC, 2])
            CVS = sbp("CVS", [128, NFC, NSL, 2])
            XTt = sbp("XTt", [128, 8, 512], BF16)
            OAT = sbp("OAT", [128, 4, 512], BF16)
            OBT = sbp("OBT", [128, 4, 512], BF16)
            SG = sbp("SG", [128, 4, 512], BF16)
            MT = sbp("MT", [128, 8, 512], BF16)
            tA = sbp("tA", [128, 512]); tB = sbp("tB", [128, 512])
            HTM = sbp("HTM", [128, 4, D])
            HT = sbp("HT", [128, 8, 512], BF16)
            UT = sbp("UT", [128, NFC, 512], BF16)
            GT = UT[:, 0:16, :].rearrange("p (w h t) n -> p w h t n", w=2, h=4)
            WS = [sbp("WS%d" % i, [128, 2048], BF16) for i in range(4)]
            Xr = [sbp("Xr%d" % i, [128, D]) for i in range(2)]
            Z = [sbp("Z%d" % i, [128, D]) for i in range(2)]
            Hb = [sbp("Hb0", [128, D], BF16)] * 2
            bst = [sbp("bst%d" % i, [128, 16]) for i in range(2)]
            AE = [sbp("AE%d" % i, [128, 516]) for i in range(2)]
            Cc = [sbp("Cc%d" % i, [128, 512]) for i in range(2)]
            Gl = [sbp("Gl%d" % i, [128, 512]) for i in range(2)]
            wsi = [0]

            def wload(src, nel, name):
                i = wsi[0] % 4
                wsi[0] += 1
                P.op("sp", lambda e: e.dma_start(out=WS[i][:, 0:nel], in_=src), w=[("WS", i)], dma=True)
                return WS[i], ("WS", i)

            def layer_norm(zt, nt, gi_, out, rkeys, wkey, s):
                FM = 512
                nchk = D // FM
                st = bst[s]
                for c in range(nchk):
                    P.op("dve", lambda e, c=c: e.bn_stats(st[0:nt, c * 6:(c + 1) * 6], zt[0:nt, c * FM:(c + 1) * FM]),
                         r=rkeys, w=[("bst", s, c)])
                P.op("dve", lambda e: e.bn_aggr(st[0:nt, 12:14], st[0:nt, 0:12].rearrange("p (c k) -> p c k", k=6)),
                     r=[("bst", s, c) for c in range(nchk)], w=[("mv", s)])
                ts("dve", st[0:nt, 14:15], st[0:nt, 13:14], LN_EPS, None, ALU.add, None, [("mv", s)], [("ve", s)])
                act(st[0:nt, 14:15], st[0:nt, 14:15], AF.Sqrt, [("ve", s)], [("ve", s)])
                P.op("dve", lambda e: e.reciprocal(st[0:nt, 15:16], st[0:nt, 14:15]), r=[("ve", s)], w=[("rs", s)])
                ts("dve", zt[0:nt], zt[0:nt], st[0:nt, 12:13], st[0:nt, 15:16], ALU.subtract, ALU.mult, rkeys + [("mv", s), ("rs", s)], rkeys)
                tt("pool", zt[0:nt], zt[0:nt], LNP[0:nt, gi_, :], ALU.mult, rkeys + ["LNP"], rkeys)
                tt("pool", out, zt[0:nt], LNP[0:nt, gi_ + 1, :], ALU.add, rkeys + ["LNP"], [wkey])

            P.op("pool", lambda e: e.memset(carry[:], 0.0), w=["carry"])
            gview = gat.ap()

            def gather(dst, col, key):
                P.op("pool", lambda e: e.indirect_dma_start(out=dst, out_offset=None, in_=gview,
                     in_offset=bass.IndirectOffsetOnAxis(ap=IDX[:, col:col + 1], axis=0)), r=["IDX"], w=[key], dma=True)

            ybase = 0
            for ti in range(NTILE + 1):
                stile = ti == 0
                if stile:
                    n, c0, segs, seglen = NS_T + 2, 0, NSL, 4
                    ny = NS_T
                else:
                    n, c0, segs, seglen = 512, NS_T + 2 + (ti - 1) * 512, 1, 512
                    ny = 512
                tk = ("tile", ti)
                P.op("pool", lambda e, c0=c0, n=n: e.dma_start(out=XTt[:, :, 0:n], in_=xT_t[:, c0:c0 + n].rearrange("(k p) n -> p k n", p=128)),
                     w=["XTt"], dma=True)
                if stile:
                    for wh in range(2):
                        for h in range(4):
                            for pt in range(2):
                                gather(GT[:, wh, h, pt, :], wh * 8 + h * 2 + pt, ("GT", wh, h, pt))
                    gk = [("GT", wh, h, pt) for wh in range(2) for h in range(4) for pt in range(2)]
                    cp("dve", OAT[:, :, 0:NS_T], GT[:, 0, :, 0, 0:NS_T], gk, ["OAT"])
                    cp("dve", OAT[:, :, NS_T:NS_T + 2], GT[:, 1, :, 0, 510:512], gk, ["OAT"])
                    cp("dve", OBT[:, :, 0:NS_T], GT[:, 0, :, 1, 0:NS_T], gk, ["OBT"])
                    cp("dve", OBT[:, :, NS_T:NS_T + 2], GT[:, 1, :, 1, 510:512], gk, ["OBT"])
                else:
                    for h in range(4):
                        gather(OAT[:, h, :], (ti + 1) * 8 + h * 2, "OAT")
                        gather(OBT[:, h, :], (ti + 1) * 8 + h * 2 + 1, "OBT")
                for cb in range(8):
                    sl = (cb % 2) * 2
                    for gi2 in range(2):
                        wsb, wk = wload(wgb.ap()[(gi2 * 8 + cb) * 128:(gi2 * 8 + cb + 1) * 128, :], 1024, "wg")
                        ps = B[gi2][:, 0:n]
                        for kc in range(8):
                            mm(ps, wsb[:, kc * 128:(kc + 1) * 128], XTt[:, kc, 0:n], kc == 0, kc == 7, [wk, "XTt"], [("B", gi2)])
                        act(SG[:, sl + gi2, 0:n], ps, AF.Sigmoid, [("B", gi2)], [("SG", sl + gi2)])
                    wsa, wka = wload(wab.ap()[cb * 128:(cb + 1) * 128, :], 512, "wa")
                    wsb2, wkb = wload(wbb.ap()[cb * 128:(cb + 1) * 128, :], 512, "wb")
                    pa = B[2 + cb % 2][:, 0:n]
                    pb = B[4 + cb % 2][:, 0:n]
                    for kc in range(4):
                        mm(pa, wsa[:, kc * 128:(kc + 1) * 128], OAT[:, kc, 0:n], kc == 0, kc == 3, [wka, "OAT"], [("B", 2 + cb % 2)])
                    for kc in range(4):
                        mm(pb, wsb2[:, kc * 128:(kc + 1) * 128], OBT[:, kc, 0:n], kc == 0, kc == 3, [wkb, "OBT"], [("B", 4 + cb % 2)])
                    tt("dve", tA[:, 0:n], pa, SG[:, sl, 0:n], ALU.mult, [("B", 2 + cb % 2), ("SG", sl)], ["tA"])
                    tt("dve", tB[:, 0:n], pb, SG[:, sl + 1, 0:n], ALU.mult, [("B", 4 + cb % 2), ("SG", sl + 1)], ["tB"])
                    tt("pool", MT[:, cb, 0:n], tA[:, 0:n], tB[:, 0:n], ALU.add, ["tA", "tB"], [("MT", cb)])
                nblk = (n + 127) // 128
                MTk = [("MT", cb) for cb in range(8)]
                for tb in range(nblk):
                    t0 = tb * 128
                    nt = min(128, n - t0)
                    s = tb % 2
                    dma("sp", Xr[s][0:nt], x_t[c0 + t0:c0 + t0 + nt, :], [], [("Xr", s)])
                    for hf in range(2):
                        ps = B[6 + hf][0:nt, :]
                        for kc in range(8):
                            mm(ps, MT[:, kc, t0:t0 + nt], WOUT[:, kc, hf * 512:(hf + 1) * 512], kc == 0, kc == 7, MTk + ["WOUT"], [("B", 6 + hf)])
                        stt("dve", Z[s][0:nt, hf * 512:(hf + 1) * 512], Xr[s][0:nt, hf * 512:(hf + 1) * 512], ALPHA, ps, ALU.mult, ALU.add,
                            [("Xr", s), ("B", 6 + hf)], [("Z", s)])
                    layer_norm(Z[s], nt, 0, HTM[0:nt, tb, :], [("Z", s)], ("HTM", tb), s)
                    cp("act", Hb[s][0:nt], HTM[0:nt, tb, :], [("HTM", tb)], ["Hb"])
                    psb = B[tb % 2][:].bitcast(BF16)
                    for kc in range(8):
                        tp(psb[:, kc * 128:kc * 128 + nt], Hb[s][0:nt, kc * 128:(kc + 1) * 128], ["Hb"], [("B", tb % 2)])
                    cp("act", HT[:, :, t0:t0 + nt], psb[:, :].rearrange("p (k t) -> p k t", k=8)[:, :, 0:nt], [("B", tb % 2)], [("HT", tb)])
                HTk = [("HT", tb) for tb in range(nblk)]
                for fc in range(NFC):
                    wsu, wku = wload(wupb.ap()[fc * 128:(fc + 1) * 128, :], 2048, "wup")
                    s = fc % 2
                    pa = B[2 + s][:, 0:n]
                    pg = B[4 + s][:, 0:n]
                    for kc in range(8):
                        mm(pa, wsu[:, kc * 128:(kc + 1) * 128], HT[:, kc, 0:n], kc == 0, kc == 7, [wku] + HTk, [("B", 2 + s)])
                    for kc in range(8):
                        mm(pg, wsu[:, 1024 + kc * 128:1024 + (kc + 1) * 128], HT[:, kc, 0:n], kc == 0, kc == 7, [wku] + HTk, [("B", 4 + s)])
                    nsg = segs * seglen
                    ae = AE[s][:, 0:segs * (seglen + 2)].rearrange("p (g t) -> p g t", g=segs)
                    cp("act", ae[:, :, 2:], pa[:, 0:nsg].rearrange("p (g t) -> p g t", g=segs), [("B", 2 + s)], [("AE", s)])
                    if stile:
                        cp("pool", ae[:, :, 0:2], SCV[:, fc, :, :], ["SCV", ("AE", s)], [("AE", s)])
                        ts("dve", carry[:, fc, :], pa[:, NS_T:NS_T + 2], hmask[:, 0:1], None, ALU.mult, None, [("B", 2 + s), "hmask"], ["carry"])
                        cp("pool", CVS[:, fc, :, :], ae[:, :, seglen:seglen + 2], [("AE", s)], ["CVS"])
                    else:
                        cp("pool", ae[:, :, 0:2], carry[:, fc, :].unsqueeze(1), ["carry", ("AE", s)], [("AE", s)])
                        cp("pool", carry[:, fc, :].unsqueeze(1), ae[:, :, seglen:seglen + 2], [("AE", s)], ["carry"])
                    cc_ = Cc[s][:, 0:nsg].rearrange("p (g t) -> p g t", g=segs)
                    ts("dve", cc_, ae[:, :, 2:], CVW[:, fc, 2:3], CVW[:, fc, 3:4], ALU.mult, ALU.add, [("AE", s), "CVW"], [("Cc", s)])
                    stt("dve", cc_, ae[:, :, 1:seglen + 1], CVW[:, fc, 1:2], cc_, ALU.mult, ALU.add, [("AE", s), "CVW", ("Cc", s)], [("Cc", s)])
                    stt("dve", cc_, ae[:, :, 0:seglen], CVW[:, fc, 0:1], cc_, ALU.mult, ALU.add, [("AE", s), "CVW", ("Cc", s)], [("Cc", s)])
                    act(Gl[s][:, 0:nsg], Cc[s][:, 0:nsg], AF.Gelu, [("Cc", s)], [("Gl", s)])
                    tt("dve", UT[:, fc, 0:nsg], pg[:, 0:nsg], Gl[s][:, 0:nsg], ALU.mult, [("B", 4 + s), ("Gl", s)], [("UT", fc)])
                UTk = [("UT", fc) for fc in range(NFC)]
                nblk4 = (ny + 127) // 128
                for tb in range(nblk4):
                    t0 = tb * 128
                    nt = min(128, ny - t0)
                    s = tb % 2
                    for hf in range(2):
                        ps = B[6 + hf][0:nt, :]
                        for fc in range(NFC):
                            mm(ps, UT[:, fc, t0:t0 + nt], WDN[:, fc, hf * 512:(hf + 1) * 512], fc == 0, fc == NFC - 1, UTk + ["WDN"], [("B", 6 + hf)])
                        stt("dve", Z[s][0:nt, hf * 512:(hf + 1) * 512], HTM[0:nt, tb, hf * 512:(hf + 1) * 512], ALPHA, ps, ALU.mult, ALU.add,
                            [("HTM", tb), ("B", 6 + hf)], [("Z", s)])
                    layer_norm(Z[s], nt, 2, Xr[s][0:nt], [("Z", s)], ("Xr", s), s)
                    dma("sp", y_o[ybase + t0:ybase + t0 + nt, :], Xr[s][0:nt], [("Xr", s)], [])
                ybase += ny
            dma("sp", cv_s, CVS[:], ["CVS"], [])
            dma("sp", cv_p, carry[:], ["carry"], [])

        with ExitStack() as fin:
            P.finalize(fin)
    return nc


def _prep(inputs):
    g = lambda k: np.asarray(inputs[k])
    xp, xs = g("x_prompt"), g("x_sample")
    Bp, T, _ = xp.shape
    Bs, TS, _ = xs.shape
    ck, cvv = g("cache_k"), g("cache_v")
    NPHYS, _, PG, H, _, DH = ck.shape
    pt = g("page_table")
    PAST = pt.shape[1] * PG
    NSB = Bs // 2
    cfg = dict(T=T, NSB=NSB, PAST=PAST, NPHYS=NPHYS)
    NG8 = PAST // 1024
    TT = T // 4
    NTILE = TT // 512
    NSUP = T // 512
    NCH = NSUP + 4
    NSL = NSB // 4
    NS_T = NSL * 4
    w_in = g("w_in")[0]
    f32 = np.float32
    half = DH // 2
    inv = (10000.0 ** (-np.arange(half, dtype=f32) * 2.0 / DH)).astype(f32)

    def rope_tab(pos):
        ang = pos.astype(f32)[:, None] * inv[None, :]
        c, s = np.cos(ang).astype(f32), np.sin(ang).astype(f32)
        return np.ascontiguousarray(np.stack([np.tile(c, (1, 4)), np.tile(s, (1, 4))], axis=1))
    rope_p = rope_tab(np.arange(T))
    rope_s = rope_tab(PAST + np.arange(TS))
    tri = np.triu(np.ones((128, 128), f32))
    ident = np.eye(128, dtype=f32)
    mnew = np.tile(np.triu(np.ones((4, 4), f32)), (1, 2))
    cmbp = np.zeros((8, 8), f32)
    cmbp[0:4, 0:4] = np.eye(4)
    cmbp[4:8, 4:8] = np.eye(4)
    a16 = (np.arange(128) % 16).astype(f32).reshape(128, 1)
    lamv = np.tile(np.concatenate([g("lambda_q1")[0], g("lambda_q2")[0], g("lambda_k1")[0], g("lambda_k2")[0]])[None, :], (128, 1)).astype(f32)
    ga_b = np.tile(g("subln_g")[0][None, :], (128, 1)).astype(f32)
    gn_b = np.tile(g("hgrn_norm_g")[0][None, :], (128, 1)).astype(f32)
    wgt = w_in[:, 3584:5632].reshape(8, 128, 16, 128).transpose(2, 1, 0, 3).reshape(16, 128, 1024)
    wa = g("w_branch_a")[0].reshape(4, 128, 8, 128).transpose(2, 1, 0, 3).reshape(8, 128, 512)
    wb = g("w_branch_b")[0].reshape(4, 128, 8, 128).transpose(2, 1, 0, 3).reshape(8, 128, 512)
    wup = g("w_up")[0].reshape(8, 128, 2, NFC, 128).transpose(3, 1, 2, 0, 4).reshape(NFC, 128, 2048)
    lnp = np.stack([g("ln1_g")[0], g("ln1_b")[0], g("ln2_g")[0], g("ln2_b")[0]], 0)
    lnp = np.ascontiguousarray(np.tile(lnp[None], (128, 1, 1))).astype(f32)
    cvw = np.concatenate([g("conv_w")[0], g("conv_b")], 0).reshape(4, NFC, 128).transpose(2, 1, 0)
    shared = dict(rope_p=rope_p, rope_s=rope_s, lamv=lamv, ga_b=ga_b, gn_b=gn_b, tri=tri, ident=ident, mnew=mnew,
                  cmbp=cmbp, a16=a16, wg_t=np.ascontiguousarray(wgt), wa_t=np.ascontiguousarray(wa),
                  wb_t=np.ascontiguousarray(wb), wout=np.ascontiguousarray(g("w_out")[0]),
                  wup_t=np.ascontiguousarray(wup), wdn=np.ascontiguousarray(g("w_down")[0]), lnp=lnp,
                  cvw=np.ascontiguousarray(cvw))
    cols = {"ka": 512, "qa": 0, "va": 1024, "ib": 2560, "gb": 3072, "qb": 1536, "fb": 2048}
    order = ["ka", "qa", "va", "ib", "gb", "qb", "fb"]
    sconv = g("state_conv")[:, 0]
    st_all = g("state_hgrn")[:, 0]
    lbl_all = g("lb_logits")
    in_maps = []
    for c in range(8):
        gI, h = c // 4, c % 4
        j = h
        m = dict(shared)
        m["xT_seq"] = np.ascontiguousarray(xp[gI].T)
        sb_ids = np.arange(gI * NSB, (gI + 1) * NSB)
        m["xsT"] = np.ascontiguousarray(xs[sb_ids].reshape(NSB * TS, D).T)
        m["wm"] = np.ascontiguousarray(np.concatenate([w_in[:, cols[k] + h * 128: cols[k] + (h + 1) * 128] for k in order], 1))
        m["lbl"] = np.ascontiguousarray(lbl_all[:, h * 128:(h + 1) * 128].T)
        ptb = pt[sb_ids].reshape(NSB, NG8, 8)
        m["ptr"] = np.ascontiguousarray(np.repeat(ptb.transpose(2, 0, 1), 16, axis=0).reshape(128, NSB * NG8)).astype(np.int32)
        m["ck"] = np.ascontiguousarray(ck[:, 0, :, h]).reshape(NPHYS * 16, 1024)
        m["cv"] = np.ascontiguousarray(cvv[:, 0, :, h]).reshape(NPHYS * 16, 1024)
        m["st_h"] = np.ascontiguousarray(st_all[sb_ids, h])
        tb_ids = sb_ids[j * NSL:(j + 1) * NSL]
        xs_t = xs[tb_ids].reshape(NS_T, D)
        p0 = j * TT
        halo = xp[gI, max(p0 - 2, 0):max(p0 - 2, 0) + 2]
        xcols = np.concatenate([xs_t, halo, xp[gI, p0:p0 + TT]], 0)
        m["x_t"] = np.ascontiguousarray(xcols)
        m["xT_t"] = np.ascontiguousarray(xcols.T)
        m["scv"] = np.ascontiguousarray(sconv[tb_ids].reshape(NSL, 2, NFC, 128).transpose(3, 2, 0, 1))
        m["hmask"] = np.full((128, 1), 0.0 if j == 0 else 1.0, f32)
        idx = np.zeros((128, NTILE + 2, 4, 2), np.int64)
        pr = np.arange(128)
        chunks = [NSUP + j, max(j * NTILE - 1, 0)] + [j * NTILE + k for k in range(NTILE)]
        for ci, chn in enumerate(chunks):
            for hh in range(4):
                for part in range(2):
                    idx[:, ci, hh, part] = (chn // 4) * 4096 + hh * 1024 + (chn % 4) * 256 + part * 128 + pr
        m["idx"] = np.ascontiguousarray(idx.reshape(128, -1)).astype(np.int32)
        in_maps.append({k: np.ascontiguousarray(v) for k, v in m.items()})
    return cfg, in_maps


_CACHE = {}


def kernel(**inputs):
    cfg, in_maps = _prep(inputs)
    key = tuple(sorted(cfg.items()))
    if key not in _CACHE:
        _CACHE[key] = build(cfg)
    nc = _CACHE[key]
    res = run_bass_kernel_spmd(nc, in_maps, core_ids=list(range(8))).results
    T, NSB = cfg["T"], cfg["NSB"]
    TT = T // 4
    NSL = NSB // 4
    NS_T = NSL * 4
    f32 = np.float32
    Bs = NSB * 2
    y_p = np.zeros((2, T, D), f32); y_s = np.zeros((Bs, 4, D), f32)
    k_p = np.zeros((2, 1, T, 4, 2, 64), f32); v_p = np.zeros((2, 1, T, 4, 128), f32)
    h_p = np.zeros((2, 1, 4, 128, 128), f32); c_p = np.zeros((2, 1, 2, DFF), f32)
    k_s = np.zeros((Bs, 1, 4, 4, 2, 64), f32); v_s = np.zeros((Bs, 1, 4, 4, 128), f32)
    h_s = np.zeros((Bs, 1, 4, 128, 128), f32); c_s = np.zeros((Bs, 1, 2, DFF), f32)
    for c in range(8):
        r = res[c]
        gI, h = c // 4, c % 4
        j = h
        k_p[gI, 0, :, h] = r["k_p"].reshape(T, 2, 64)
        v_p[gI, 0, :, h] = r["v_p"]
        h_p[gI, 0, h] = r["S_p"]
        sl = slice(gI * NSB, (gI + 1) * NSB)
        k_s[sl, 0, :, h] = r["k_s"].reshape(NSB, 4, 2, 64)
        v_s[sl, 0, :, h] = r["v_s"].reshape(NSB, 4, 128)
        h_s[sl, 0, h] = r["S_s"]
        tb = slice(gI * NSB + j * NSL, gI * NSB + (j + 1) * NSL)
        y_s[tb] = r["y_o"][0:NS_T].reshape(NSL, 4, D)
        y_p[gI, j * TT:(j + 1) * TT] = r["y_o"][NS_T:]
        c_s[tb, 0] = r["cv_s"].transpose(2, 3, 1, 0).reshape(NSL, 2, DFF)
        if j == 3:
            c_p[gI, 0] = r["cv_p"].transpose(2, 1, 0).reshape(2, DFF)
    return (y_p, y_s, k_p, v_p, h_p, c_p, k_s, v_s, h_s, c_s)
```

```python
import math
import os
from contextlib import ExitStack
import numpy as np
import concourse.bass as bass
import concourse.mybir as mybir
from concourse.bass_utils import run_bass_kernel_spmd

F32 = mybir.dt.float32
BF16 = mybir.dt.bfloat16
I32 = mybir.dt.int32
ALU = mybir.AluOpType
AF = mybir.ActivationFunctionType
AX = mybir.AxisListType

D = 1024
DFF = 2816
NFC = DFF // 128
LN_EPS = 1e-5
RMS_EPS = 1e-5
ALPHA = 2.0 ** 0.25
LAM_INIT = 0.8 - 0.6 * math.exp(0.0)
SCALE = 64 ** -0.5
RING = 12


class Op:
    __slots__ = ("eng", "fn", "dma", "deps", "needs_inc", "sem", "val", "idx", "cc", "bg")

    def __init__(self, eng, fn, dma=False, cc=False):
        self.eng, self.fn, self.dma, self.cc = eng, fn, dma, cc
        self.deps = []
        self.needs_inc = dma or cc
        self.sem = None
        self.val = 0
        self.bg = False


class Prog:
    ENGS = ("pe", "act", "dve", "pool", "sp")

    def __init__(self, nc):
        self.nc = nc
        self.ops = {e: [] for e in self.ENGS}
        self.last_w = {}
        self.readers = {}
        self.all_dma = []
        self.cut = False

    def op(self, eng, fn, r=(), w=(), dma=False, cc=False, extra=()):
        o = Op(eng, fn, dma, cc)
        if self.cut:
            return o
        deps = set(extra)
        for k in r:
            d = self.last_w.get(k)
            if d is not None:
                deps.add(d)
        for k in w:
            d = self.last_w.get(k)
            if d is not None:
                deps.add(d)
            for rd in self.readers.get(k, ()):
                deps.add(rd)
        async_o = dma or cc
        for d in deps:
            if d is o:
                continue
            if d.eng == "pe" and eng == "pe" and not async_o:
                continue
            o.deps.append(d)
            d.needs_inc = True
        for k in w:
            self.last_w[k] = o
            self.readers[k] = []
        for k in r:
            lst = self.readers.setdefault(k, [])
            if not async_o:
                lst[:] = [x for x in lst if x.eng != eng or x.dma or x.cc]
            lst.append(o)
        self.ops[eng].append(o)
        if async_o:
            self.all_dma.append(o)
        return o

    def barrier(self, include_bg=False):
        if self.cut:
            return
        lasts = []
        for e in self.ENGS:
            for o_ in reversed(self.ops[e]):
                if include_bg or not o_.bg:
                    lasts.append(o_)
                    break
        dm = [d for d in self.all_dma if include_bg or not getattr(d, "bg", False)]
        self.all_dma = [d for d in self.all_dma if not (include_bg or not getattr(d, "bg", False))]
        for e in self.ENGS:
            self.op(e, None, extra=lasts + dm)
        self.last_w = {}
        self.readers = {}

    def finalize(self, stack):
        nc = self.nc
        esem = {e: stack.enter_context(nc.semaphore("es_" + e)) for e in self.ENGS}
        rings = {e: [stack.enter_context(nc.semaphore("r%s%d" % (e, i))) for i in range(RING)]
                 for e in ("sp", "pool")}
        ccsem = stack.enter_context(nc.semaphore("ccsem"))
        cnt = {e: 0 for e in self.ENGS}
        dcnt = {"sp": 0, "pool": 0}
        cccnt = 0
        for e in self.ENGS:
            for o in self.ops[e]:
                if o.cc:
                    cccnt += 1
                    o.sem, o.val = ccsem, cccnt
                elif o.dma:
                    i = dcnt[e]
                    dcnt[e] += 1
                    o.idx = i
                    o.sem, o.val = rings[e][i % RING], 16 * (i // RING + 1)
                elif o.needs_inc and o.fn is not None:
                    cnt[e] += 1
                    o.sem, o.val = esem[e], cnt[e]
        block = stack.enter_context(nc.Block())

        def run(e, eng):
            waited = {}
            for o in self.ops[e]:
                ws = []
                for d in o.deps:
                    if d.sem is None:
                        continue
                    ws.append((d.sem, d.val))
                if o.dma and o.idx >= RING:
                    ws.append((o.sem, o.val - 16))
                for s, v in ws:
                    key = id(s)
                    if waited.get(key, 0) >= v:
                        continue
                    waited[key] = v
                    eng.wait_ge(s, v)
                if o.fn is None:
                    continue
                ins = o.fn(eng)
                if o.cc:
                    ins.then_inc(o.sem)
                elif o.dma:
                    ins.then_inc(o.sem, 16)
                elif o.sem is not None:
                    ins.then_inc(o.sem, 1)
            if e in ("sp", "pool"):
                for i in range(min(RING, dcnt[e])):
                    last = ((dcnt[e] - 1 - i) // RING) * RING + i
                    eng.wait_ge(rings[e][i], 16 * (last // RING + 1))
            if e == "pool" and cccnt:
                eng.wait_ge(ccsem, cccnt)

        @block.tensor
        def _(eng):
            run("pe", eng)

        @block.scalar
        def _(eng):
            run("act", eng)

        @block.vector
        def _(eng):
            run("dve", eng)

        @block.gpsimd
        def _(eng):
            run("pool", eng)

        @block.sync
        def _(eng):
            run("sp", eng)


def build(cfg):
    T, NSB, PAST, NPHYS = cfg["T"], cfg["NSB"], cfg["PAST"], cfg["NPHYS"]
    NBLK = T // 128
    NSUP = T // 512
    NG8 = PAST // 1024
    NKT = NG8 * 8
    TT = T // 4
    NTILE = TT // 512
    NSL = NSB // 4
    NS_T = NSL * 4
    NCH = NSUP + 4
    NCOL = NS_T + 2 + TT
    nc = bass.Bass("TRN2", target_bir_lowering=False)
    P = Prog(nc)
    STOP = int(os.environ.get("KSTOP", "99"))

    def din(name, shape, dt=F32):
        return nc.dram_tensor(name, list(shape), dt, kind="ExternalInput").ap()

    def dout(name, shape, dt=F32):
        return nc.dram_tensor(name, list(shape), dt, kind="ExternalOutput").ap()

    xT_seq = din("xT_seq", [D, T])
    xsT = din("xsT", [D, NSB * 4])
    wm = din("wm", [D, 896])
    rope_p = din("rope_p", [T, 2, 128])
    rope_s = din("rope_s", [4, 2, 128])
    lbl = din("lbl", [128, 2])
    lamv = din("lamv", [128, 256])
    ga_b = din("ga_b", [128, 128])
    gn_b = din("gn_b", [128, 128])
    tri_d = din("tri", [128, 128])
    ident_d = din("ident", [128, 128])
    mnew_d = din("mnew", [4, 8])
    cmbp_d = din("cmbp", [8, 8])
    a16_d = din("a16", [128, 1])
    ptr_d = din("ptr", [128, NSB * NG8], I32)
    ck = din("ck", [NPHYS * 16, 1024])
    cv = din("cv", [NPHYS * 16, 1024])
    st_h = din("st_h", [NSB, 128, 128])
    xT_t = din("xT_t", [D, NCOL])
    x_t = din("x_t", [NCOL, D])
    wg_d = din("wg_t", [16, 128, 8 * 128])
    wa_d = din("wa_t", [8, 128, 4 * 128])
    wb_d = din("wb_t", [8, 128, 4 * 128])
    wout_d = din("wout", [D, D])
    wup_d = din("wup_t", [NFC, 128, 2 * 8 * 128])
    wdn_d = din("wdn", [DFF, D])
    lnp_d = din("lnp", [128, 4, D])
    cvw_d = din("cvw", [128, NFC, 4])
    scv_d = din("scv", [128, NFC, NSL, 2])
    hmask_d = din("hmask", [128, 1])
    idx_d = din("idx", [128, (NTILE + 2) * 8], I32)
    k_p = dout("k_p", [T, 128])
    v_p = dout("v_p", [T, 128])
    k_s = dout("k_s", [NSB * 4, 128])
    v_s = dout("v_s", [NSB * 4, 128])
    S_p = dout("S_p", [128, 128])
    S_s = dout("S_s", [NSB, 128, 128])
    y_o = dout("y_o", [NS_T + TT, D])
    cv_s = dout("cv_s", [128, NFC, NSL, 2])
    cv_p = dout("cv_p", [128, NFC, 2])
    xTb = nc.dram_tensor("xTb", [D, T], BF16)
    snd = nc.dram_tensor("snd", [NCH * 256, 512], BF16)
    gat = nc.dram_tensor("gat", [4 * NCH * 256, 512], BF16)
    wgb = nc.dram_tensor("wgb", [16 * 128, 1024], BF16)
    wab = nc.dram_tensor("wab", [8 * 128, 512], BF16)
    wbb = nc.dram_tensor("wbb", [8 * 128, 512], BF16)
    wupb = nc.dram_tensor("wupb", [NFC * 128, 2048], BF16)
    woutb = nc.dram_tensor("woutb", [D, D], BF16)
    wdnb = nc.dram_tensor("wdnb", [DFF, D], BF16)

    with ExitStack() as top:
        def sb(name, shape, dt=F32, st=top):
            return st.enter_context(nc.sbuf_tensor("s_" + name, list(shape), dt))

        banks = [top.enter_context(nc.psum_tensor("ps%d" % i, [128, 512], F32)) for i in range(8)]

        ident = sb("ident", [128, 128], BF16)
        tri = sb("tri", [128, 128], F32)
        ones = sb("ones", [128, 128], F32)
        trib = sb("trib", [128, 128], BF16)
        lam_t = sb("lam_t", [128, 4], F32)

        def mm(out, lhsT, rhs, start, stop, r, w):
            return P.op("pe", lambda e: e.matmul(out, lhsT, rhs, start=start, stop=stop), r=r, w=w)

        def tp(out, in_, r, w):
            n = in_.shape[0]
            return P.op("pe", lambda e: e.transpose(out, in_, ident[0:n, 0:n]), r=list(r) + ["ident"], w=w)

        def act(out, in_, func, r, w, bias=None, scale=None, accum=None):
            kw = {}
            if bias is not None:
                kw["bias"] = bias
            if scale is not None:
                kw["scale"] = scale
            if accum is not None:
                kw["accum_out"] = accum
            return P.op("act", lambda e: e.activation(out, in_, func, **kw), r=r, w=w)

        def tt(eng, out, a, b, op, r, w):
            return P.op(eng, lambda e: e.tensor_tensor(out, a, b, op), r=r, w=w)

        def ts(eng, out, a, s1, s2, op0, op1, r, w):
            if op1 is None:
                return P.op(eng, lambda e: e.tensor_scalar(out, a, s1, None, op0), r=r, w=w)
            return P.op(eng, lambda e: e.tensor_scalar(out, a, s1, s2, op0, op1), r=r, w=w)

        def stt(eng, out, a, s, b, op0, op1, r, w):
            return P.op(eng, lambda e: e.scalar_tensor_tensor(out, a, s, b, op0, op1), r=r, w=w)

        def cp(eng, out, in_, r, w):
            if eng == "act":
                return P.op("act", lambda e: e.copy(out, in_), r=r, w=w)
            return P.op(eng, lambda e: e.tensor_copy(out, in_), r=r, w=w)

        def dma(q, out, in_, r, w):
            return P.op(q, lambda e: e.dma_start(out=out, in_=in_), r=r, w=w, dma=True)

        identf = sb("identf", [128, 128], F32)
        dma("sp", identf[:], ident_d, [], ["identf"])
        cp("dve", ident[:], identf[:], ["identf"], ["ident"])
        dma("sp", tri[:], tri_d, [], ["tri"])
        cp("dve", trib[:], tri[:], ["tri"], ["trib"])
        P.op("pool", lambda e: e.memset(ones[:], 1.0), w=["ones"])
        lamt = sb("lamt", [128, 256], F32)
        dma("sp", lamt[:], lamv, [], ["lamt"])
        lj = sb("lj", [128, 128], F32)
        ls = sb("ls", [128, 4], F32)
        tt("dve", lj[:], lamt[:, 0:128], lamt[:, 128:256], ALU.mult, ["lamt"], ["lj"])
        P.op("dve", lambda e: e.tensor_reduce(ls[:, 0:2], lj[:].rearrange("p (a b) -> p a b", a=2), AX.X, ALU.add),
             r=["lj"], w=["ls"])
        act(ls[:, 2:4], ls[:, 0:2], AF.Exp, ["ls"], ["ls2"])
        tt("dve", lam_t[:, 0:1], ls[:, 2:3], ls[:, 3:4], ALU.subtract, ["ls2"], ["lam0"])
        ts("dve", lam_t[:, 0:1], lam_t[:, 0:1], LAM_INIT, None, ALU.add, None, ["lam0"], ["lam0"])
        ts("dve", lam_t[:, 1:2], lam_t[:, 0:1], -1.0, None, ALU.mult, None, ["lam0"], ["lam"])

        for i in range(0, T, 2048):
            w_ = min(2048, T - i)
            P.op("pool", lambda e, i=i, w_=w_: e.dma_start(out=xTb.ap()[:, i:i + w_], in_=xT_seq[:, i:i + w_]),
                 w=[("xTb", i // 512 + j) for j in range(w_ // 512)], dma=True)

        def precast(dst, src2d, rows):
            for r0 in range(0, rows, 512):
                r1 = min(rows, r0 + 512)
                o_ = P.op("pool", lambda e, r0=r0, r1=r1: e.dma_start(out=dst.ap()[r0:r1, :], in_=src2d[r0:r1, :]), dma=True)
                o_.bg = True
        with ExitStack() as ph:
            def sbp(name, shape, dt=F32):
                return sb(name, shape, dt, ph)
            OA = sbp("OA", [128, NCH, 512], BF16)
            OB = sbp("OB", [128, NCH, 512], BF16)
            WM = sbp("WM", [128, 8, 896], BF16)
            P.op("pool", lambda e: e.dma_start(out=WM[:], in_=wm.rearrange("(k p) c -> p k c", p=128)), w=["WM"], dma=True)
            gab = sbp("gab", [128, 128])
            gnb = sbp("gnb", [128, 128])
            dma("sp", gab[:], ga_b, [], ["gab"])
            dma("sp", gnb[:], gn_b, [], ["gnb"])
            P.op("act", lambda e: e.mul(gab[:], gab[:], 1.0 - LAM_INIT), r=["gab"], w=["gab"])
            lbt = sbp("lbt", [128, 2])
            dma("sp", lbt[:], lbl, [], ["lbt"])
            lbv = sbp("lbv", [128, 2])
            tt("dve", lbv[:, 0:1], lbt[:, 0:1], lbt[:, 1:2], ALU.subtract, ["lbt"], ["lbv0"])
            act(lbv[:, 0:1], lbv[:, 0:1], AF.Sigmoid, ["lbv0"], ["lbv0"])
            ts("dve", lbv[:, 1:2], lbv[:, 0:1], -1.0, 1.0, ALU.mult, ALU.add, ["lbv0"], ["lbv"])
            Sf = sbp("Sf", [128, 128])
            Sb = sbp("Sb", [128, 128], BF16)
            XS = sbp("XS", [128, 8, NSB * 4], BF16)
            NR = 3
            ring_names = {}

            def rt(name, shape, dt=F32):
                tl = [sbp("%s_%d" % (name, i), shape, dt) for i in range(NR)]
                ring_names[name] = tl
                return tl
            CS = rt("CS", [128, 2, 128])
            T1 = rt("T1", [128, 4, 32]); T2 = rt("T2", [128, 4, 32])
            ROT = rt("ROT", [128, 256]); ROTb = rt("ROTb", [128, 256], BF16)
            Vf = rt("Vf", [128, 128]); Ib = rt("Ib", [128, 128], BF16)
            Gs = rt("Gs", [128, 128]); Gb = rt("Gb", [128, 128], BF16)
            sg = rt("sg", [128, 128]); lf = rt("lf", [128, 128]); kk = rt("kk", [128, 128])
            bT = rt("bT", [128, 128]); eb = rt("eb", [128, 128]); enb = rt("enb", [128, 128]); ehb = rt("ehb", [128, 128])
            qt = rt("qt", [128, 128], BF16); kt_ = rt("kt", [128, 128], BF16); kh = rt("kh", [128, 128], BF16)
            khT = rt("khT", [128, 128], BF16); attb = rt("attb", [128, 128], BF16)
            dec = rt("dec", [128, 2]); sq = rt("sq", [128, 128]); st4 = rt("st4", [128, 4])
            obb = rt("obb", [128, 128], BF16)
            qbS = rt("qbS", [128, 128])

            ph1 = ExitStack()
            ph1.__enter__()
            def sb1(name, shape, dt=F32):
                return sb(name, shape, dt, ph1)
            KT = sb1("KT", [128, T], BF16)
            QT = sb1("QT", [128, T], BF16)
            VA = sb1("VA", [128, NBLK, 132], BF16)
            P.op("pool", lambda e: e.memset(VA[:], 1.0), w=["VAinit"])
            XT = [sb1("XT%d" % i, [128, 8, 512], BF16) for i in range(2)]
            B = banks
            B5b = B[5][:].bitcast(BF16)

            def mixer_block(n, bi, xcols, fmq, fmf, fm_keys, rope_src, k_dst, v_dst, KTd, QTd, Vd, OBd, tag):
                s = bi % NR
                K = lambda nm: (nm, s)
                xk = fm_keys
                tmA = B[0][0:n, :]
                tmB = B[1][0:n, 0:128]
                for kc in range(8):
                    mm(tmA, xcols(kc), WM[:, kc, 0:512], kc == 0, kc == 7, xk + ["WM"], [("B",0)])
                for kc in range(8):
                    mm(tmB, xcols(kc), WM[:, kc, 512:640], kc == 0, kc == 7, xk + ["WM"], [("B",1)])
                FMFK = ("B", 4)
                if fmq is None:
                    FMFK = ("B", 3)
                    fmq = B[3][:, 0:n]
                    fmf = B[3][:, 8:8 + n]
                    for kc in range(8):
                        mm(fmq, WM[:, kc, 640:768], xcols(kc), kc == 0, kc == 7, xk + ["WM"], [("B",3)])
                    for kc in range(8):
                        mm(fmf, WM[:, kc, 768:896], xcols(kc), kc == 0, kc == 7, xk + ["WM"], [FMFK])
                cs = CS[s]
                dma("sp", cs[0:n], rope_src, [], [K("CS")])
                xa = tmA[:, 0:256].rearrange("p (g h d) -> p g h d", g=4, h=2)
                x1, x2 = xa[:, :, 0, :], xa[:, :, 1, :]
                cosv = cs[0:n, 0, :].rearrange("p (g d) -> p g d", g=4)
                sinv = cs[0:n, 1, :].rearrange("p (g d) -> p g d", g=4)
                rot = ROT[s][0:n].rearrange("p (g h d) -> p g h d", g=4, h=2)
                t1, t2 = T1[s][0:n], T2[s][0:n]
                tt("dve", t1, x1, cosv, ALU.mult, [("B",0), K("CS")], [K("T1")])
                tt("dve", t2, x2, sinv, ALU.mult, [("B",0), K("CS")], [K("T2")])
                tt("dve", rot[:, :, 0, :], t1, t2, ALU.subtract, [K("T1"), K("T2")], [K("ROT0")])
                tt("dve", t1, x2, cosv, ALU.mult, [("B",0), K("CS"), K("ROT0")], [K("T1")])
                tt("dve", t2, x1, sinv, ALU.mult, [("B",0), K("CS"), K("ROT0")], [K("T2")])
                tt("dve", rot[:, :, 1, :], t1, t2, ALU.add, [K("T1"), K("T2")], [K("ROT1")])
                RK = [K("ROT0"), K("ROT1")]
                dma("sp", k_dst, ROT[s][0:n, 0:128], RK, [])
                cp("act", ROTb[s][0:n], ROT[s][0:n], RK, [K("ROTb")])
                tp(B5b[:, 0:n], ROTb[s][0:n, 0:128], [K("ROTb")], [("B",5)])
                tp(B5b[:, 128:128 + n], ROTb[s][0:n, 128:256], [K("ROTb")], [("B",5)])
                cp("act", KTd, B5b[:, 0:n], [("B",5)], [("KT", tag)])
                cp("act", QTd, B5b[:, 128:128 + n], [("B",5)], [("QT", tag)])
                cp("act", Vf[s][0:n], tmA[:, 256:384], [("B",0)], [K("Vf")])
                dma("sp", v_dst, Vf[s][0:n], [K("Vf")], [])
                cp("pool", Vd, Vf[s][0:n], [K("Vf"), "VAinit"], [("V", tag)])
                cp("act", Ib[s][0:n], tmA[:, 384:512], [("B",0)], [K("Ib")])
                act(Gs[s][0:n], tmB, AF.Silu, [("B",1)], [K("Gs")])
                tt("pool", Gb[s][0:n], Gs[s][0:n], gnb[0:n], ALU.mult, [K("Gs"), "gnb"], [K("Gb")])
                cp("act", qbS[s][:, 0:n], fmq, [("B",3)], [K("qbS")])
                act(sg[s][:, 0:n], fmf, AF.Sigmoid, [FMFK], [K("sg")])
                ts("dve", sg[s][:, 0:n], sg[s][:, 0:n], lbv[:, 1:2], lbv[:, 0:1], ALU.mult, ALU.add, [K("sg"), "lbv", "lbv0"], [K("sg")])
                act(lf[s][:, 0:n], sg[s][:, 0:n], AF.Ln, [K("sg")], [K("lf")])
                ts("pool", kk[s][:, 0:n], sg[s][:, 0:n], -1.0, 1.0, ALU.mult, ALU.add, [K("sg")], [K("kk")])
                P.op("dve", lambda e: e.tensor_tensor_scan(bT[s][:, 0:n], ones[:, 0:n], lf[s][:, 0:n], 0.0, ALU.mult, ALU.add),
                     r=[K("lf"), "ones"], w=[K("bT")])
                act(eb[s][:, 0:n], bT[s][:, 0:n], AF.Exp, [K("bT")], [K("eb")])
                tt("dve", qt[s][:, 0:n], qbS[s][:, 0:n], eb[s][:, 0:n], ALU.mult, [K("qbS"), K("eb")], [K("qt")])
                act(enb[s][:, 0:n], bT[s][:, 0:n], AF.Exp, [K("bT")], [K("enb")], scale=-1.0)
                tt("pool", kt_[s][:, 0:n], kk[s][:, 0:n], enb[s][:, 0:n], ALU.mult, [K("kk"), K("enb")], [K("kt")])
                act(ehb[s][:, 0:n], bT[s][:, 0:n], AF.Exp, [K("bT")], [K("ehb")], scale=-1.0, bias=bT[s][:, n - 1:n])
                tt("pool", kh[s][:, 0:n], kk[s][:, 0:n], ehb[s][:, 0:n], ALU.mult, [K("kk"), K("ehb")], [K("kh")])
                act(dec[s][:, 0:1], bT[s][:, n - 1:n], AF.Exp, [K("bT")], [K("dec")])
                tp(B5b[0:n, 256:384], kh[s][:, 0:n], [K("kh")], [("B",5)])
                cp("act", khT[s][0:n], B5b[0:n, 256:384], [("B",5)], [K("khT")])
                attT = B[1][0:n, 128:128 + n]
                mm(attT, kt_[s][:, 0:n], qt[s][:, 0:n], True, True, [K("kt"), K("qt")], [("B",1)])
                tt("dve", attb[s][0:n, 0:n], attT, tri[0:n, 0:n], ALU.mult, [("B",1), "tri"], [K("attb")])
                def late():
                    o_ps = B[2][0:n, 0:128]
                    mm(o_ps, qt[s][:, 0:n], Sb[:], True, False, [K("qt"), "Sb"], [("B",2)])
                    mm(o_ps, attb[s][0:n, 0:n], Ib[s][0:n], False, True, [K("attb"), K("Ib")], [("B",2)])
                    U = B[7][:, 256:384]
                    mm(U, khT[s][0:n], Ib[s][0:n], True, True, [K("khT"), K("Ib")], [("B",7)])
                    stt("dve", Sf[:], Sf[:], dec[s][:, 0:1], U, ALU.mult, ALU.add, ["Sf", K("dec"), ("B",7)], ["Sf"])
                    cp("pool", Sb[:], Sf[:], ["Sf"], ["Sb"])
                    act(sq[s][0:n], o_ps, AF.Square, [("B",2)], [K("sq"), K("ss")], accum=st4[s][0:n, 0:1])
                    ts("dve", st4[s][0:n, 1:2], st4[s][0:n, 0:1], 1.0 / 128, RMS_EPS, ALU.mult, ALU.add, [K("ss")], [K("ms")])
                    act(st4[s][0:n, 2:3], st4[s][0:n, 1:2], AF.Sqrt, [K("ms")], [K("sd")])
                    P.op("dve", lambda e: e.reciprocal(st4[s][0:n, 3:4], st4[s][0:n, 2:3]), r=[K("sd")], w=[K("rstd")])
                    stt("dve", obb[s][0:n], o_ps, st4[s][0:n, 3:4], Gb[s][0:n], ALU.mult, ALU.mult, [("B",2), K("rstd"), K("Gb")], [K("obb")])
                    B6b_ = B[6][:].bitcast(BF16)
                    tp(B6b_[:, 384:384 + n], obb[s][0:n], [K("obb")], [("B",6)])
                    cp("act", OBd, B6b_[:, 384:384 + n], [("B",6)], [("OBs", tag)])
                return late

            if STOP <= 1:
                P.cut = True
            P.op("pool", lambda e: e.memset(Sf[:], 0.0), w=["Sf"])
            P.op("pool", lambda e: e.memset(Sb[:], 0.0), w=["Sb"])
            pend = [None]
            for su in range(NSUP):
                xt = XT[su % 2]
                xkey = ("XT", su % 2)
                dma("sp", xt[:], xTb.ap()[:, su * 512:(su + 1) * 512].rearrange("(k p) n -> p k n", p=128),
                    [("xTb", su)], [xkey])
                fmq, fmf = B[3][:, :], B[4][:, :]
                for kc in range(8):
                    mm(fmq, WM[:, kc, 640:768], xt[:, kc, :], kc == 0, kc == 7, [xkey, "WM"], [("B",3)])
                for kc in range(8):
                    mm(fmf, WM[:, kc, 768:896], xt[:, kc, :], kc == 0, kc == 7, [xkey, "WM"], [("B", 4)])
                for j in range(4):
                    blk = su * 4 + j
                    c0 = blk * 128
                    lt = mixer_block(128, blk, lambda kc, xt=xt, j=j: xt[:, kc, j * 128:(j + 1) * 128],
                                     fmq[:, j * 128:(j + 1) * 128], fmf[:, j * 128:(j + 1) * 128], [xkey],
                                     rope_p[c0:c0 + 128], k_p[c0:c0 + 128, :], v_p[c0:c0 + 128, :],
                                     KT[:, c0:c0 + 128], QT[:, c0:c0 + 128], VA[:, blk, 0:128],
                                     OB[:, blk // 4, (blk % 4) * 128:(blk % 4 + 1) * 128], blk)
                    if os.environ.get("KOLD") == "1":
                        lt()
                        lt = None
                    if pend[0] is not None:
                        pend[0]()
                    pend[0] = lt
            if pend[0] is not None:
                pend[0]()
            precast(wgb, wg_d.rearrange("a p c -> (a p) c"), 16 * 128)
            precast(wab, wa_d.rearrange("a p c -> (a p) c"), 8 * 128)
            precast(wbb, wb_d.rearrange("a p c -> (a p) c"), 8 * 128)
            precast(woutb, wout_d, D)
            precast(wupb, wup_d.rearrange("a p c -> (a p) c"), NFC * 128)
            precast(wdnb, wdn_d, DFF)
            dma("sp", S_p, Sf[:], ["Sf"], [])
            P.barrier()

            if STOP <= 2:
                P.cut = True
            PT = [[sb1("PT%d_%d" % (i, m), [128, 512], BF16) for m in range(2)] for i in range(2)]
            ep = [sb1("ep%d" % i, [128, 8]) for i in range(NR)]
            o0 = [sb1("o0%d" % i, [128, 128]) for i in range(NR)]; o1 = [sb1("o1%d" % i, [128, 128]) for i in range(NR)]; oab = [sb1("oab%d" % i, [128, 128], BF16) for i in range(NR)]
            groups = []
            gi = 0
            for qb in range(NBLK):
                nkb = qb + 1
                for j0 in range(0, nkb, 4):
                    groups.append((qb, j0, min(4, nkb - j0), gi % 2))
                    gi += 1

            def accs(qb):
                base = 4 + 2 * (qb % 2)
                return [B[base][:, 0:129], B[base + 1][:, 0:129]], base

            def emit_qk(g):
                qb, j0, nj, st_ = g
                q0 = qb * 128
                nkb = qb + 1
                for m in range(2):
                    stb = B[st_ * 2 + m]
                    for jj in range(nj):
                        j = j0 + jj
                        mm(stb[:, jj * 128:(jj + 1) * 128], KT[64 * m:64 * m + 64, j * 128:(j + 1) * 128],
                           QT[64 * m:64 * m + 64, q0:q0 + 128], True, True, [("KT", j), ("QT", qb)], [("B", st_ * 2 + m)])
                    act(PT[st_][m][:, 0:nj * 128], stb[:, 0:nj * 128], AF.Exp, [("B", st_ * 2 + m)], [("PT", st_, m)], scale=SCALE)
                    if j0 + nj == nkb:
                        dcol = (nj - 1) * 128
                        tt("pool", PT[st_][m][:, dcol:dcol + 128], PT[st_][m][:, dcol:dcol + 128], trib[:], ALU.mult,
                           [("PT", st_, m), "trib"], [("PT", st_, m)])

            def emit_pv(g):
                qb, j0, nj, st_ = g
                nkb = qb + 1
                acc, base = accs(qb)
                for jj in range(nj):
                    j = j0 + jj
                    for m in range(2):
                        mm(acc[m], PT[st_][m][:, jj * 128:(jj + 1) * 128], VA[:, j, 0:129], j == 0, j == nkb - 1,
                           [("PT", st_, m), ("V", j), "VAinit"], [("B", base + m)])
                if j0 + nj != nkb:
                    return
                s = qb % NR
                K = lambda nm: (nm + "_e", s)
                e_ = ep[s]
                P.op("dve", lambda e, e_=e_: e.reciprocal(e_[:, 0:1], acc[0][:, 128:129]), r=[("B", base)], w=[K("rl0")])
                P.op("dve", lambda e, e_=e_: e.reciprocal(e_[:, 1:2], acc[1][:, 128:129]), r=[("B", base + 1)], w=[K("rl1")])
                tt("dve", e_[:, 2:3], e_[:, 1:2], lam_t[:, 1:2], ALU.mult, [K("rl1"), "lam"], [K("nl1")])
                ts("dve", o0[s][:], acc[0][:, 0:128], e_[:, 0:1], None, ALU.mult, None, [("B", base), K("rl0")], [K("o0")])
                stt("dve", o1[s][:], acc[1][:, 0:128], e_[:, 2:3], o0[s][:], ALU.mult, ALU.add, [("B", base + 1), K("nl1"), K("o0")], [K("o1")])
                P.op("dve", lambda e, e_=e_, s=s: e.scalar_tensor_tensor(o0[s][:], o1[s][:], 1.0, o1[s][:], ALU.mult, ALU.mult, accum_out=e_[:, 3:4]),
                     r=[K("o1")], w=[K("o0"), K("ss")])
                ts("dve", e_[:, 4:5], e_[:, 3:4], 1.0 / 128, RMS_EPS, ALU.mult, ALU.add, [K("ss")], [K("ms")])
                act(e_[:, 5:6], e_[:, 4:5], AF.Ln, [K("ms")], [K("sd")])
                act(e_[:, 6:7], e_[:, 5:6], AF.Exp, [K("sd")], [K("rstd")], scale=-0.5)
                stt("dve", oab[s][:], o1[s][:], e_[:, 6:7], gab[:], ALU.mult, ALU.mult, [K("o1"), K("rstd"), "gab"], [K("oab")])
                tpo = B[base][:].bitcast(BF16)[:, 512:640]
                tp(tpo, oab[s][:], [K("oab")], [("B", base)])
                cp("act", OA[:, qb // 4, (qb % 4) * 128:(qb % 4 + 1) * 128], tpo, [("B", base)], [("OAs", qb)])

            for i, g in enumerate(groups):
                emit_qk(g)
                if i >= 1:
                    emit_pv(groups[i - 1])
            emit_pv(groups[-1])

            if STOP <= 3:
                P.cut = True
            P.barrier()
            ph1.close()
            P.op("pool", lambda e: e.dma_start(out=XS[:], in_=xsT.rearrange("(k p) n -> p k n", p=128)), w=["XS"], dma=True)
            a16 = sbp("a16", [128, 1])
            dma("sp", a16[:], a16_d, [], ["a16"])
            pti = sbp("pti", [128, NSB * NG8], I32)
            ptf = sbp("ptf", [128, NSB * NG8])
            gix = sbp("gix", [128, NSB * NG8], I32)
            dma("sp", pti[:], ptr_d, [], ["pti"])
            cp("dve", ptf[:], pti[:], ["pti"], ["ptf"])
            ts("dve", ptf[:], ptf[:], 16.0, a16[:, 0:1], ALU.mult, ALU.add, ["ptf", "a16"], ["ptf"])
            cp("dve", gix[:], ptf[:], ["ptf"], ["gix"])
            KG = [sbp("KG%d" % i, [128, NKT, 128], BF16) for i in range(2)]
            VG = [sbp("VG%d" % i, [128, NKT, 128], BF16) for i in range(2)]
            KTs = [sbp("KTs%d" % i, [128, 1024], BF16) for i in range(2)]
            Qbd = sbp("Qbd", [128, 8], BF16)
            P.op("pool", lambda e: e.memset(Qbd[:], 0.0), w=["Qbd"])
            KTn = sbp("KTn", [128, 4], BF16); QTn = sbp("QTn", [128, 4], BF16)
            Vn = sbp("Vn", [4, 128], BF16)
            PTs = [sbp("PTs%d" % i, [128, NKT * 8], BF16) for i in range(2)]
            PTn = sbp("PTn", [4, 8]); PTnb = sbp("PTnb", [4, 8], BF16)
            mnew = sbp("mnew", [4, 8])
            dma("sp", mnew[:], mnew_d, [], ["mnew"])
            rs = sbp("rs", [128, 8])
            OS = sbp("OS", [8, NSB, 128]); LS = sbp("LS", [8, NSB])
            OBs = sbp("OBs", [128, NSB * 4], BF16)
            ck_v = ck
            cv_v = cv
            for b in range(NSB):
                s2 = b % 2
                for G in range(NG8):
                    col = b * NG8 + G
                    P.op("pool", lambda e, G=G, col=col, s2=s2: e.indirect_dma_start(
                        out=KG[s2][:, G * 8:(G + 1) * 8, :].rearrange("p r d -> p (r d)"), out_offset=None, in_=ck_v,
                        in_offset=bass.IndirectOffsetOnAxis(ap=gix[:, col:col + 1], axis=0)),
                        r=["gix"], w=[("KG", s2, G)], dma=True)
                    P.op("pool", lambda e, G=G, col=col, s2=s2: e.indirect_dma_start(
                        out=VG[s2][:, G * 8:(G + 1) * 8, :].rearrange("p r d -> p (r d)"), out_offset=None, in_=cv_v,
                        in_offset=bass.IndirectOffsetOnAxis(ap=gix[:, col:col + 1], axis=0)),
                        r=["gix"], w=[("VG", s2, G)], dma=True)
                if STOP == 4 and os.environ.get("KSUB") == "1":
                    P.cut = True
                dma("sp", Sf[:], st_h[b], [], ["Sf"])
                cp("pool", Sb[:], Sf[:], ["Sf"], ["Sb"])
                mixer_block(4, NBLK + b, lambda kc, b=b: XS[:, kc, b * 4:(b + 1) * 4], None, None, ["XS"],
                            rope_s, k_s[b * 4:(b + 1) * 4, :], v_s[b * 4:(b + 1) * 4, :],
                            KTn[:], QTn[:], Vn[:], OBs[:, b * 4:(b + 1) * 4], ("s", b))()
                dma("sp", S_s[b], Sf[:], ["Sf"], [])
                tg = ("s", b)
                cp("dve", Qbd[0:64, 0:4], QTn[0:64, :], [("QT", tg)], ["Qbd"])
                cp("dve", Qbd[64:128, 4:8], QTn[64:128, :], [("QT", tg)], ["Qbd"])
                if STOP == 4 and os.environ.get("KSUB") == "2":
                    P.cut = True
                STb = B[7][:, 0:NKT * 8]
                for G in range(NG8):
                    kb = (b * NG8 + G) % 2
                    psb = B[6][:].bitcast(BF16)
                    for r_ in range(8):
                        tp(psb[:, r_ * 128:(r_ + 1) * 128], KG[s2][:, G * 8 + r_, :], [("KG", s2, G)], [("B", 6)])
                    cp("act" if G % 2 == 0 else "dve", KTs[kb][:], psb[:, :], [("B", 6)], [("KTs", kb)])
                    for r_ in range(8):
                        kt = G * 8 + r_
                        mm(STb[:, kt * 8:(kt + 1) * 8], KTs[kb][:, r_ * 128:(r_ + 1) * 128], Qbd[:], True, True,
                           [("KTs", kb), "Qbd"], [("B",7)])
                STn = B[1][0:4, 384:392]
                mm(STn, KTn[:], Qbd[:], True, True, [("KT", tg), "Qbd"], [("B",1)])
                act(PTs[s2][:], STb, AF.Exp, [("B",7)], [("PTs", s2)], scale=SCALE)
                act(PTn[:], STn, AF.Exp, [("B",1)], ["PTn"], scale=SCALE)
                tt("dve", PTn[:], PTn[:], mnew[:], ALU.mult, ["PTn", "mnew"], ["PTn"])
                cp("dve", PTnb[:], PTn[:], ["PTn"], ["PTnb"])
                P.op("dve", lambda e, s2=s2: e.tensor_reduce(rs[:], PTs[s2][:].rearrange("p (k q) -> p q k", q=8), AX.X, ALU.add),
                     r=[("PTs", s2)], w=["rs"])
                Lp = B[1][0:8, 400:401]
                mm(Lp, rs[:], ones[:, 0:1], True, False, ["rs", "ones"], [("B",1)])
                mm(Lp, PTn[:], ones[0:4, 0:1], False, True, ["PTn", "ones"], [("B",1)])
                if STOP == 4 and os.environ.get("KSUB") == "3":
                    P.cut = True
                Op_ = B[4][0:8, 0:128]
                for kt in range(NKT):
                    mm(Op_, PTs[s2][:, kt * 8:(kt + 1) * 8], VG[s2][:, kt, :], kt == 0, False,
                       [("PTs", s2), ("VG", s2, kt // 8)], [("B",4)])
                mm(Op_, PTnb[:], Vn[:], False, True, ["PTnb", ("V", tg)], [("B",4)])
                cp("act", OS[:, b, :], Op_, [("B",4)], ["OS"])
                cp("act", LS[:, b:b + 1], Lp, [("B",1)], ["LS"])
            if STOP == 4 and os.environ.get("KSUB") == "4":
                P.cut = True
            RL = sbp("RL", [8, NSB])
            P.op("dve", lambda e: e.reciprocal(RL[:], LS[:]), r=["LS"], w=["RL"])
            ON = sbp("ON", [8, NSB, 128])
            for b in range(NSB):
                ts("dve", ON[:, b, :], OS[:, b, :], RL[:, b:b + 1], None, ALU.mult, None, ["OS", "RL"], [("ON", b)])
            cmbp = sbp("cmbp", [8, 8])
            cmb = sbp("cmb", [8, 4])
            dma("sp", cmbp[:], cmbp_d, [], ["cmbp"])
            stt("dve", cmb[:], cmbp[:, 4:8], lam_t[0:8, 1:2], cmbp[:, 0:4], ALU.mult, ALU.add, ["cmbp", "lam"], ["cmb"])
            osb = sbp("osb", [4, NSB, 128]); osq = sbp("osq", [4, 128])
            sst = sbp("sst", [4, 4, NSB])
            oasb = sbp("oasb", [4, NSB, 128], BF16)
            B6b = B[6][:].bitcast(BF16)
            for b in range(NSB):
                bk = B[b % 3]
                mm(bk[0:4, 0:128], cmb[:], ON[:, b, :], True, True, ["cmb", ("ON", b)], [("B", b % 3)])
                cp("act", osb[:, b, :], bk[0:4, 0:128], [("B", b % 3)], [("osb", b)])
                act(osq[:], osb[:, b, :], AF.Square, [("osb", b)], ["osq", ("sst0", b)], accum=sst[:, 0, b:b + 1])
            allb = [("sst0", b) for b in range(NSB)]
            ts("dve", sst[:, 1, :], sst[:, 0, :], 1.0 / 128, RMS_EPS, ALU.mult, ALU.add, allb, ["sst1"])
            act(sst[:, 2, :], sst[:, 1, :], AF.Sqrt, ["sst1"], ["sst2"])
            P.op("dve", lambda e: e.reciprocal(sst[:, 3, :], sst[:, 2, :]), r=["sst2"], w=["sst3"])
            for b in range(NSB):
                stt("dve", oasb[:, b, :], osb[:, b, :], sst[:, 3, b:b + 1], gab[0:4, :], ALU.mult, ALU.mult, [("osb", b), "sst3", "gab"], [("oasb", b)])
                tp(B6b[:, 256 + b * 4:256 + (b + 1) * 4], oasb[:, b, :], [("oasb", b)], [("B", 6)])
            for j in range(4):
                cp("act", OA[:, NSUP + j, 0:NSL * 4], B6b[:, 256 + j * NSL * 4:256 + (j + 1) * NSL * 4], [("B", 6)], [("OAsmp", j)])
                cp("dve", OB[:, NSUP + j, 0:NSL * 4], OBs[:, j * NSL * 4:(j + 1) * NSL * 4],
                   [("OBs", ("s", b)) for b in range(NSB)], [("OBsmp", j)])
            if STOP <= 4:
                P.cut = True
            P.barrier()
            sview = snd.ap().rearrange("(c two f) n -> f two c n", two=2, f=128)
            d1 = P.op("sp", lambda e: e.dma_start(out=sview[:, 0], in_=OA[:]), dma=True)
            d2 = P.op("sp", lambda e: e.dma_start(out=sview[:, 1], in_=OB[:]), dma=True)

        NGR = NCH // 4
        for k in range(NGR):
            P.op("pool", lambda e, k=k: e.collective_compute(
                "AllGather", ALU.bypass, replica_groups=[[0, 1, 2, 3], [4, 5, 6, 7]],
                ins=[snd.ap()[k * 1024:(k + 1) * 1024, :].opt()],
                outs=[gat.ap()[k * 4096:(k + 1) * 4096, :].opt()]), cc=True, extra=[d1, d2])
        P.barrier(include_bg=True)

        if STOP <= 5:
            P.cut = True
        with ExitStack() as ph:
            def sbp(name, shape, dt=F32):
                return sb(name, shape, dt, ph)
            B = banks
            WOUT = sbp("WOUT", [128, 8, D], BF16)
            WDN = sbp("WDN", [128, NFC, D], BF16)
            for kc in range(0, 8, 2):
                P.op("sp", lambda e, kc=kc: e.dma_start(out=WOUT[:, kc:kc + 2, :], in_=woutb.ap().rearrange("(k p) c -> p k c", p=128)[:, kc:kc + 2, :]), w=["WOUT"], dma=True)
            for kc in range(0, NFC, 2):
                P.op("sp", lambda e, kc=kc: e.dma_start(out=WDN[:, kc:kc + 2, :], in_=wdnb.ap().rearrange("(k p) c -> p k c", p=128)[:, kc:kc + 2, :]), w=["WDN"], dma=True)
            LNP = sbp("LNP", [128, 4, D])
            dma("sp", LNP[:], lnp_d, [], ["LNP"])
            CVW = sbp("CVW", [128, NFC, 4])
            dma("sp", CVW[:], cvw_d, [], ["CVW"])
            SCV = sbp("SCV", [128, NFC, NSL, 2])
            dma("sp", SCV[:], scv_d, [], ["SCV"])
            hmask = sbp("hmask", [128, 1])
            dma("sp", hmask[:], hmask_d, [], ["hmask"])
            IDX = sbp("IDX", [128, (NTILE + 2) * 8], I32)
            dma("sp", IDX[:], idx_d, [], ["IDX"])
            carry = sbp("carry", [128, NFC, 2])
            CVS = sbp("CVS", [128, NFC, NSL, 2])
            XTt = sbp("XTt", [128, 8, 512], BF16)
            OAT = sbp("OAT", [128, 4, 512], BF16)
            OBT = sbp("OBT", [128, 4, 512], BF16)
            SG = sbp("SG", [128, 4, 512], BF16)
            MT = sbp("MT", [128, 8, 512], BF16)
            tA = sbp("tA", [128, 512]); tB = sbp("tB", [128, 512])
            HTM = sbp("HTM", [128, 4, D])
            HT = sbp("HT", [128, 8, 512], BF16)
            UT = sbp("UT", [128, NFC, 512], BF16)
            GT = UT[:, 0:16, :].rearrange("p (w h t) n -> p w h t n", w=2, h=4)
            WS = [sbp("WS%d" % i, [128, 2048], BF16) for i in range(4)]
            Xr = [sbp("Xr%d" % i, [128, D]) for i in range(2)]
            Z = [sbp("Z%d" % i, [128, D]) for i in range(2)]
            Hb = [sbp("Hb0", [128, D], BF16)] * 2
            bst = [sbp("bst%d" % i, [128, 16]) for i in range(2)]
            AE = [sbp("AE%d" % i, [128, 516]) for i in range(2)]
            Cc = [sbp("Cc%d" % i, [128, 512]) for i in range(2)]
            Gl = [sbp("Gl%d" % i, [128, 512]) for i in range(2)]
            wsi = [0]

            def wload(src, nel, name):
                i = wsi[0] % 4
                wsi[0] += 1
                P.op("sp", lambda e: e.dma_start(out=WS[i][:, 0:nel], in_=src), w=[("WS", i)], dma=True)
                return WS[i], ("WS", i)

            def layer_norm(zt, nt, gi_, out, rkeys, wkey, s):
                FM = 512
                nchk = D // FM
                st = bst[s]
                for c in range(nchk):
                    P.op("dve", lambda e, c=c: e.bn_stats(st[0:nt, c * 6:(c + 1) * 6], zt[0:nt, c * FM:(c + 1) * FM]),
                         r=rkeys, w=[("bst", s, c)])
                P.op("dve", lambda e: e.bn_aggr(st[0:nt, 12:14], st[0:nt, 0:12].rearrange("p (c k) -> p c k", k=6)),
                     r=[("bst", s, c) for c in range(nchk)], w=[("mv", s)])
                ts("dve", st[0:nt, 14:15], st[0:nt, 13:14], LN_EPS, None, ALU.add, None, [("mv", s)], [("ve", s)])
                act(st[0:nt, 14:15], st[0:nt, 14:15], AF.Sqrt, [("ve", s)], [("ve", s)])
                P.op("dve", lambda e: e.reciprocal(st[0:nt, 15:16], st[0:nt, 14:15]), r=[("ve", s)], w=[("rs", s)])
                ts("dve", zt[0:nt], zt[0:nt], st[0:nt, 12:13], st[0:nt, 15:16], ALU.subtract, ALU.mult, rkeys + [("mv", s), ("rs", s)], rkeys)
                tt("pool", zt[0:nt], zt[0:nt], LNP[0:nt, gi_, :], ALU.mult, rkeys + ["LNP"], rkeys)
                tt("pool", out, zt[0:nt], LNP[0:nt, gi_ + 1, :], ALU.add, rkeys + ["LNP"], [wkey])

            P.op("pool", lambda e: e.memset(carry[:], 0.0), w=["carry"])
            gview = gat.ap()

            def gather(dst, col, key):
                P.op("pool", lambda e: e.indirect_dma_start(out=dst, out_offset=None, in_=gview,
                     in_offset=bass.IndirectOffsetOnAxis(ap=IDX[:, col:col + 1], axis=0)), r=["IDX"], w=[key], dma=True)

            ybase = 0
            for ti in range(NTILE + 1):
                stile = ti == 0
                if stile:
                    n, c0, segs, seglen = NS_T + 2, 0, NSL, 4
                    ny = NS_T
                else:
                    n, c0, segs, seglen = 512, NS_T + 2 + (ti - 1) * 512, 1, 512
                    ny = 512
                tk = ("tile", ti)
                P.op("pool", lambda e, c0=c0, n=n: e.dma_start(out=XTt[:, :, 0:n], in_=xT_t[:, c0:c0 + n].rearrange("(k p) n -> p k n", p=128)),
                     w=["XTt"], dma=True)
                if stile:
                    for wh in range(2):
                        for h in range(4):
                            for pt in range(2):
                                gather(GT[:, wh, h, pt, :], wh * 8 + h * 2 + pt, ("GT", wh, h, pt))
                    gk = [("GT", wh, h, pt) for wh in range(2) for h in range(4) for pt in range(2)]
                    cp("dve", OAT[:, :, 0:NS_T], GT[:, 0, :, 0, 0:NS_T], gk, ["OAT"])
                    cp("dve", OAT[:, :, NS_T:NS_T + 2], GT[:, 1, :, 0, 510:512], gk, ["OAT"])
                    cp("dve", OBT[:, :, 0:NS_T], GT[:, 0, :, 1, 0:NS_T], gk, ["OBT"])
                    cp("dve", OBT[:, :, NS_T:NS_T + 2], GT[:, 1, :, 1, 510:512], gk, ["OBT"])
                else:
                    for h in range(4):
                        gather(OAT[:, h, :], (ti + 1) * 8 + h * 2, "OAT")
                        gather(OBT[:, h, :], (ti + 1) * 8 + h * 2 + 1, "OBT")
                for cb in range(8):
                    sl = (cb % 2) * 2
                    for gi2 in range(2):
                        wsb, wk = wload(wgb.ap()[(gi2 * 8 + cb) * 128:(gi2 * 8 + cb + 1) * 128, :], 1024, "wg")
                        ps = B[gi2][:, 0:n]
                        for kc in range(8):
                            mm(ps, wsb[:, kc * 128:(kc + 1) * 128], XTt[:, kc, 0:n], kc == 0, kc == 7, [wk, "XTt"], [("B", gi2)])
                        act(SG[:, sl + gi2, 0:n], ps, AF.Sigmoid, [("B", gi2)], [("SG", sl + gi2)])
                    wsa, wka = wload(wab.ap()[cb * 128:(cb + 1) * 128, :], 512, "wa")
                    wsb2, wkb = wload(wbb.ap()[cb * 128:(cb + 1) * 128, :], 512, "wb")
                    pa = B[2 + cb % 2][:, 0:n]
                    pb = B[4 + cb % 2][:, 0:n]
                    for kc in range(4):
                        mm(pa, wsa[:, kc * 128:(kc + 1) * 128], OAT[:, kc, 0:n], kc == 0, kc == 3, [wka, "OAT"], [("B", 2 + cb % 2)])
                    for kc in range(4):
                        mm(pb, wsb2[:, kc * 128:(kc + 1) * 128], OBT[:, kc, 0:n], kc == 0, kc == 3, [wkb, "OBT"], [("B", 4 + cb % 2)])
                    tt("dve", tA[:, 0:n], pa, SG[:, sl, 0:n], ALU.mult, [("B", 2 + cb % 2), ("SG", sl)], ["tA"])
                    tt("dve", tB[:, 0:n], pb, SG[:, sl + 1, 0:n], ALU.mult, [("B", 4 + cb % 2), ("SG", sl + 1)], ["tB"])
                    tt("pool", MT[:, cb, 0:n], tA[:, 0:n], tB[:, 0:n], ALU.add, ["tA", "tB"], [("MT", cb)])
                nblk = (n + 127) // 128
                MTk = [("MT", cb) for cb in range(8)]
                for tb in range(nblk):
                    t0 = tb * 128
                    nt = min(128, n - t0)
                    s = tb % 2
                    dma("sp", Xr[s][0:nt], x_t[c0 + t0:c0 + t0 + nt, :], [], [("Xr", s)])
                    for hf in range(2):
                        ps = B[6 + hf][0:nt, :]
                        for kc in range(8):
                            mm(ps, MT[:, kc, t0:t0 + nt], WOUT[:, kc, hf * 512:(hf + 1) * 512], kc == 0, kc == 7, MTk + ["WOUT"], [("B", 6 + hf)])
                        stt("dve", Z[s][0:nt, hf * 512:(hf + 1) * 512], Xr[s][0:nt, hf * 512:(hf + 1) * 512], ALPHA, ps, ALU.mult, ALU.add,
                            [("Xr", s), ("B", 6 + hf)], [("Z", s)])
                    layer_norm(Z[s], nt, 0, HTM[0:nt, tb, :], [("Z", s)], ("HTM", tb), s)
                    cp("act", Hb[s][0:nt], HTM[0:nt, tb, :], [("HTM", tb)], ["Hb"])
                    psb = B[tb % 2][:].bitcast(BF16)
                    for kc in range(8):
                        tp(psb[:, kc * 128:kc * 128 + nt], Hb[s][0:nt, kc * 128:(kc + 1) * 128], ["Hb"], [("B", tb % 2)])
                    cp("act", HT[:, :, t0:t0 + nt], psb[:, :].rearrange("p (k t) -> p k t", k=8)[:, :, 0:nt], [("B", tb % 2)], [("HT", tb)])
                HTk = [("HT", tb) for tb in range(nblk)]
                for fc in range(NFC):
                    wsu, wku = wload(wupb.ap()[fc * 128:(fc + 1) * 128, :], 2048, "wup")
                    s = fc % 2
                    pa = B[2 + s][:, 0:n]
                    pg = B[4 + s][:, 0:n]
                    for kc in range(8):
                        mm(pa, wsu[:, kc * 128:(kc + 1) * 128], HT[:, kc, 0:n], kc == 0, kc == 7, [wku] + HTk, [("B", 2 + s)])
                    for kc in range(8):
                        mm(pg, wsu[:, 1024 + kc * 128:1024 + (kc + 1) * 128], HT[:, kc, 0:n], kc == 0, kc == 7, [wku] + HTk, [("B", 4 + s)])
                    nsg = segs * seglen
                    ae = AE[s][:, 0:segs * (seglen + 2)].rearrange("p (g t) -> p g t", g=segs)
                    cp("act", ae[:, :, 2:], pa[:, 0:nsg].rearrange("p (g t) -> p g t", g=segs), [("B", 2 + s)], [("AE", s)])
                    if stile:
                        cp("pool", ae[:, :, 0:2], SCV[:, fc, :, :], ["SCV", ("AE", s)], [("AE", s)])
                        ts("dve", carry[:, fc, :], pa[:, NS_T:NS_T + 2], hmask[:, 0:1], None, ALU.mult, None, [("B", 2 + s), "hmask"], ["carry"])
                        cp("pool", CVS[:, fc, :, :], ae[:, :, seglen:seglen + 2], [("AE", s)], ["CVS"])
                    else:
                        cp("pool", ae[:, :, 0:2], carry[:, fc, :].unsqueeze(1), ["carry", ("AE", s)], [("AE", s)])
                        cp("pool", carry[:, fc, :].unsqueeze(1), ae[:, :, seglen:seglen + 2], [("AE", s)], ["carry"])
                    cc_ = Cc[s][:, 0:nsg].rearrange("p (g t) -> p g t", g=segs)
                    ts("dve", cc_, ae[:, :, 2:], CVW[:, fc, 2:3], CVW[:, fc, 3:4], ALU.mult, ALU.add, [("AE", s), "CVW"], [("Cc", s)])
                    stt("dve", cc_, ae[:, :, 1:seglen + 1], CVW[:, fc, 1:2], cc_, ALU.mult, ALU.add, [("AE", s), "CVW", ("Cc", s)], [("Cc", s)])
                    stt("dve", cc_, ae[:, :, 0:seglen], CVW[:, fc, 0:1], cc_, ALU.mult, ALU.add, [("AE", s), "CVW", ("Cc", s)], [("Cc", s)])
                    act(Gl[s][:, 0:nsg], Cc[s][:, 0:nsg], AF.Gelu, [("Cc", s)], [("Gl", s)])
                    tt("dve", UT[:, fc, 0:nsg], pg[:, 0:nsg], Gl[s][:, 0:nsg], ALU.mult, [("B", 4 + s), ("Gl", s)], [("UT", fc)])
                UTk = [("UT", fc) for fc in range(NFC)]
                nblk4 = (ny + 127) // 128
                for tb in range(nblk4):
                    t0 = tb * 128
                    nt = min(128, ny - t0)
                    s = tb % 2
                    for hf in range(2):
                        ps = B[6 + hf][0:nt, :]
                        for fc in range(NFC):
                            mm(ps, UT[:, fc, t0:t0 + nt], WDN[:, fc, hf * 512:(hf + 1) * 512], fc == 0, fc == NFC - 1, UTk + ["WDN"], [("B", 6 + hf)])
                        stt("dve", Z[s][0:nt, hf * 512:(hf + 1) * 512], HTM[0:nt, tb, hf * 512:(hf + 1) * 512], ALPHA, ps, ALU.mult, ALU.add,
                            [("HTM", tb), ("B", 6 + hf)], [("Z", s)])
                    layer_norm(Z[s], nt, 2, Xr[s][0:nt], [("Z", s)], ("Xr", s), s)
                    dma("sp", y_o[ybase + t0:ybase + t0 + nt, :], Xr[s][0:nt], [("Xr", s)], [])
                ybase += ny
            dma("sp", cv_s, CVS[:], ["CVS"], [])
            dma("sp", cv_p, carry[:], ["carry"], [])

        with ExitStack() as fin:
            P.finalize(fin)
    return nc


def _prep(inputs):
    g = lambda k: np.asarray(inputs[k])
    xp, xs = g("x_prompt"), g("x_sample")
    Bp, T, _ = xp.shape
    Bs, TS, _ = xs.shape
    ck, cvv = g("cache_k"), g("cache_v")
    NPHYS, _, PG, H, _, DH = ck.shape
    pt = g("page_table")
    PAST = pt.shape[1] * PG
    NSB = Bs // 2
    cfg = dict(T=T, NSB=NSB, PAST=PAST, NPHYS=NPHYS)
    NG8 = PAST // 1024
    TT = T // 4
    NTILE = TT // 512
    NSUP = T // 512
    NCH = NSUP + 4
    NSL = NSB // 4
    NS_T = NSL * 4
    w_in = g("w_in")[0]
    f32 = np.float32
    half = DH // 2
    inv = (10000.0 ** (-np.arange(half, dtype=f32) * 2.0 / DH)).astype(f32)

    def rope_tab(pos):
        ang = pos.astype(f32)[:, None] * inv[None, :]
        c, s = np.cos(ang).astype(f32), np.sin(ang).astype(f32)
        return np.ascontiguousarray(np.stack([np.tile(c, (1, 4)), np.tile(s, (1, 4))], axis=1))
    rope_p = rope_tab(np.arange(T))
    rope_s = rope_tab(PAST + np.arange(TS))
    tri = np.triu(np.ones((128, 128), f32))
    ident = np.eye(128, dtype=f32)
    mnew = np.tile(np.triu(np.ones((4, 4), f32)), (1, 2))
    cmbp = np.zeros((8, 8), f32)
    cmbp[0:4, 0:4] = np.eye(4)
    cmbp[4:8, 4:8] = np.eye(4)
    a16 = (np.arange(128) % 16).astype(f32).reshape(128, 1)
    lamv = np.tile(np.concatenate([g("lambda_q1")[0], g("lambda_q2")[0], g("lambda_k1")[0], g("lambda_k2")[0]])[None, :], (128, 1)).astype(f32)
    ga_b = np.tile(g("subln_g")[0][None, :], (128, 1)).astype(f32)
    gn_b = np.tile(g("hgrn_norm_g")[0][None, :], (128, 1)).astype(f32)
    wgt = w_in[:, 3584:5632].reshape(8, 128, 16, 128).transpose(2, 1, 0, 3).reshape(16, 128, 1024)
    wa = g("w_branch_a")[0].reshape(4, 128, 8, 128).transpose(2, 1, 0, 3).reshape(8, 128, 512)
    wb = g("w_branch_b")[0].reshape(4, 128, 8, 128).transpose(2, 1, 0, 3).reshape(8, 128, 512)
    wup = g("w_up")[0].reshape(8, 128, 2, NFC, 128).transpose(3, 1, 2, 0, 4).reshape(NFC, 128, 2048)
    lnp = np.stack([g("ln1_g")[0], g("ln1_b")[0], g("ln2_g")[0], g("ln2_b")[0]], 0)
    lnp = np.ascontiguousarray(np.tile(lnp[None], (128, 1, 1))).astype(f32)
    cvw = np.concatenate([g("conv_w")[0], g("conv_b")], 0).reshape(4, NFC, 128).transpose(2, 1, 0)
    shared = dict(rope_p=rope_p, rope_s=rope_s, lamv=lamv, ga_b=ga_b, gn_b=gn_b, tri=tri, ident=ident, mnew=mnew,
                  cmbp=cmbp, a16=a16, wg_t=np.ascontiguousarray(wgt), wa_t=np.ascontiguousarray(wa),
                  wb_t=np.ascontiguousarray(wb), wout=np.ascontiguousarray(g("w_out")[0]),
                  wup_t=np.ascontiguousarray(wup), wdn=np.ascontiguousarray(g("w_down")[0]), lnp=lnp,
                  cvw=np.ascontiguousarray(cvw))
    cols = {"ka": 512, "qa": 0, "va": 1024, "ib": 2560, "gb": 3072, "qb": 1536, "fb": 2048}
    order = ["ka", "qa", "va", "ib", "gb", "qb", "fb"]
    sconv = g("state_conv")[:, 0]
    st_all = g("state_hgrn")[:, 0]
    lbl_all = g("lb_logits")
    in_maps = []
    for c in range(8):
        gI, h = c // 4, c % 4
        j = h
        m = dict(shared)
        m["xT_seq"] = np.ascontiguousarray(xp[gI].T)
        sb_ids = np.arange(gI * NSB, (gI + 1) * NSB)
        m["xsT"] = np.ascontiguousarray(xs[sb_ids].reshape(NSB * TS, D).T)
        m["wm"] = np.ascontiguousarray(np.concatenate([w_in[:, cols[k] + h * 128: cols[k] + (h + 1) * 128] for k in order], 1))
        m["lbl"] = np.ascontiguousarray(lbl_all[:, h * 128:(h + 1) * 128].T)
        ptb = pt[sb_ids].reshape(NSB, NG8, 8)
        m["ptr"] = np.ascontiguousarray(np.repeat(ptb.transpose(2, 0, 1), 16, axis=0).reshape(128, NSB * NG8)).astype(np.int32)
        m["ck"] = np.ascontiguousarray(ck[:, 0, :, h]).reshape(NPHYS * 16, 1024)
        m["cv"] = np.ascontiguousarray(cvv[:, 0, :, h]).reshape(NPHYS * 16, 1024)
        m["st_h"] = np.ascontiguousarray(st_all[sb_ids, h])
        tb_ids = sb_ids[j * NSL:(j + 1) * NSL]
        xs_t = xs[tb_ids].reshape(NS_T, D)
        p0 = j * TT
        halo = xp[gI, max(p0 - 2, 0):max(p0 - 2, 0) + 2]
        xcols = np.concatenate([xs_t, halo, xp[gI, p0:p0 + TT]], 0)
        m["x_t"] = np.ascontiguousarray(xcols)
        m["xT_t"] = np.ascontiguousarray(xcols.T)
        m["scv"] = np.ascontiguousarray(sconv[tb_ids].reshape(NSL, 2, NFC, 128).transpose(3, 2, 0, 1))
        m["hmask"] = np.full((128, 1), 0.0 if j == 0 else 1.0, f32)
        idx = np.zeros((128, NTILE + 2, 4, 2), np.int64)
        pr = np.arange(128)
        chunks = [NSUP + j, max(j * NTILE - 1, 0)] + [j * NTILE + k for k in range(NTILE)]
        for ci, chn in enumerate(chunks):
            for hh in range(4):
                for part in range(2):
                    idx[:, ci, hh, part] = (chn // 4) * 4096 + hh * 1024 + (chn % 4) * 256 + part * 128 + pr
        m["idx"] = np.ascontiguousarray(idx.reshape(128, -1)).astype(np.int32)
        in_maps.append({k: np.ascontiguousarray(v) for k, v in m.items()})
    return cfg, in_maps


_CACHE = {}


def kernel(**inputs):
    cfg, in_maps = _prep(inputs)
    key = tuple(sorted(cfg.items()))
    if key not in _CACHE:
        _CACHE[key] = build(cfg)
    nc = _CACHE[key]
    res = run_bass_kernel_spmd(nc, in_maps, core_ids=list(range(8))).results
    T, NSB = cfg["T"], cfg["NSB"]
    TT = T // 4
    NSL = NSB // 4
    NS_T = NSL * 4
    f32 = np.float32
    Bs = NSB * 2
    y_p = np.zeros((2, T, D), f32); y_s = np.zeros((Bs, 4, D), f32)
    k_p = np.zeros((2, 1, T, 4, 2, 64), f32); v_p = np.zeros((2, 1, T, 4, 128), f32)
    h_p = np.zeros((2, 1, 4, 128, 128), f32); c_p = np.zeros((2, 1, 2, DFF), f32)
    k_s = np.zeros((Bs, 1, 4, 4, 2, 64), f32); v_s = np.zeros((Bs, 1, 4, 4, 128), f32)
    h_s = np.zeros((Bs, 1, 4, 128, 128), f32); c_s = np.zeros((Bs, 1, 2, DFF), f32)
    for c in range(8):
        r = res[c]
        gI, h = c // 4, c % 4
        j = h
        k_p[gI, 0, :, h] = r["k_p"].reshape(T, 2, 64)
        v_p[gI, 0, :, h] = r["v_p"]
        h_p[gI, 0, h] = r["S_p"]
        sl = slice(gI * NSB, (gI + 1) * NSB)
        k_s[sl, 0, :, h] = r["k_s"].reshape(NSB, 4, 2, 64)
        v_s[sl, 0, :, h] = r["v_s"].reshape(NSB, 4, 128)
        h_s[sl, 0, h] = r["S_s"]
        tb = slice(gI * NSB + j * NSL, gI * NSB + (j + 1) * NSL)
        y_s[tb] = r["y_o"][0:NS_T].reshape(NSL, 4, D)
        y_p[gI, j * TT:(j + 1) * TT] = r["y_o"][NS_T:]
        c_s[tb, 0] = r["cv_s"].transpose(2, 3, 1, 0).reshape(NSL, 2, DFF)
        if j == 3:
            c_p[gI, 0] = r["cv_p"].transpose(2, 1, 0).reshape(2, DFF)
    return (y_p, y_s, k_p, v_p, h_p, c_p, k_s, v_s, h_s, c_s)
```

```python
import math
import os
from contextlib import ExitStack
import numpy as np
import concourse.bass as bass
import concourse.mybir as mybir
from concourse.bass_utils import run_bass_kernel_spmd

F32 = mybir.dt.float32
BF16 = mybir.dt.bfloat16
I32 = mybir.dt.int32
ALU = mybir.AluOpType
AF = mybir.ActivationFunctionType
AX = mybir.AxisListType

D = 1024
DFF = 2816
NFC = DFF // 128
LN_EPS = 1e-5
RMS_EPS = 1e-5
ALPHA = 2.0 ** 0.25
LAM_INIT = 0.8 - 0.6 * math.exp(0.0)
SCALE = 64 ** -0.5
RING = 12


class Op:
    __slots__ = ("eng", "fn", "dma", "deps", "needs_inc", "sem", "val", "idx", "cc", "bg")

    def __init__(self, eng, fn, dma=False, cc=False):
        self.eng, self.fn, self.dma, self.cc = eng, fn, dma, cc
        self.deps = []
        self.needs_inc = dma or cc
        self.sem = None
        self.val = 0
        self.bg = False


class Prog:
    ENGS = ("pe", "act", "dve", "pool", "sp")

    def __init__(self, nc):
        self.nc = nc
        self.ops = {e: [] for e in self.ENGS}
        self.last_w = {}
        self.readers = {}
        self.all_dma = []
        self.cut = False

    def op(self, eng, fn, r=(), w=(), dma=False, cc=False, extra=()):
        o = Op(eng, fn, dma, cc)
        if self.cut:
            return o
        deps = set(extra)
        for k in r:
            d = self.last_w.get(k)
            if d is not None:
                deps.add(d)
        for k in w:
            d = self.last_w.get(k)
            if d is not None:
                deps.add(d)
            for rd in self.readers.get(k, ()):
                deps.add(rd)
        async_o = dma or cc
        for d in deps:
            if d is o:
                continue
            if d.eng == "pe" and eng == "pe" and not async_o:
                continue
            o.deps.append(d)
            d.needs_inc = True
        for k in w:
            self.last_w[k] = o
            self.readers[k] = []
        for k in r:
            lst = self.readers.setdefault(k, [])
            if not async_o:
                lst[:] = [x for x in lst if x.eng != eng or x.dma or x.cc]
            lst.append(o)
        self.ops[eng].append(o)
        if async_o:
            self.all_dma.append(o)
        return o

    def barrier(self, include_bg=False):
        if self.cut:
            return
        lasts = []
        for e in self.ENGS:
            for o_ in reversed(self.ops[e]):
                if include_bg or not o_.bg:
                    lasts.append(o_)
                    break
        dm = [d for d in self.all_dma if include_bg or not getattr(d, "bg", False)]
        self.all_dma = [d for d in self.all_dma if not (include_bg or not getattr(d, "bg", False))]
        for e in self.ENGS:
            self.op(e, None, extra=lasts + dm)
        self.last_w = {}
        self.readers = {}

    def finalize(self, stack):
        nc = self.nc
        esem = {e: stack.enter_context(nc.semaphore("es_" + e)) for e in self.ENGS}
        rings = {e: [stack.enter_context(nc.semaphore("r%s%d" % (e, i))) for i in range(RING)]
                 for e in ("sp", "pool")}
        ccsem = stack.enter_context(nc.semaphore("ccsem"))
        cnt = {e: 0 for e in self.ENGS}
        dcnt = {"sp": 0, "pool": 0}
        cccnt = 0
        for e in self.ENGS:
            for o in self.ops[e]:
                if o.cc:
                    cccnt += 1
                    o.sem, o.val = ccsem, cccnt
                elif o.dma:
                    i = dcnt[e]
                    dcnt[e] += 1
                    o.idx = i
                    o.sem, o.val = rings[e][i % RING], 16 * (i // RING + 1)
                elif o.needs_inc and o.fn is not None:
                    cnt[e] += 1
                    o.sem, o.val = esem[e], cnt[e]
        block = stack.enter_context(nc.Block())

        def run(e, eng):
            waited = {}
            for o in self.ops[e]:
                ws = []
                for d in o.deps:
                    if d.sem is None:
                        continue
                    ws.append((d.sem, d.val))
                if o.dma and o.idx >= RING:
                    ws.append((o.sem, o.val - 16))
                for s, v in ws:
                    key = id(s)
                    if waited.get(key, 0) >= v:
                        continue
                    waited[key] = v
                    eng.wait_ge(s, v)
                if o.fn is None:
                    continue
                ins = o.fn(eng)
                if o.cc:
                    ins.then_inc(o.sem)
                elif o.dma:
                    ins.then_inc(o.sem, 16)
                elif o.sem is not None:
                    ins.then_inc(o.sem, 1)
            if e in ("sp", "pool"):
                for i in range(min(RING, dcnt[e])):
                    last = ((dcnt[e] - 1 - i) // RING) * RING + i
                    eng.wait_ge(rings[e][i], 16 * (last // RING + 1))
            if e == "pool" and cccnt:
                eng.wait_ge(ccsem, cccnt)

        @block.tensor
        def _(eng):
            run("pe", eng)

        @block.scalar
        def _(eng):
            run("act", eng)

        @block.vector
        def _(eng):
            run("dve", eng)

        @block.gpsimd
        def _(eng):
            run("pool", eng)

        @block.sync
        def _(eng):
            run("sp", eng)


def build(cfg):
    T, NSB, PAST, NPHYS = cfg["T"], cfg["NSB"], cfg["PAST"], cfg["NPHYS"]
    NBLK = T // 128
    NSUP = T // 512
    NG8 = PAST // 1024
    NKT = NG8 * 8
    TT = T // 4
    NTILE = TT // 512
    NSL = NSB // 4
    NS_T = NSL * 4
    NCH = NSUP + 4
    NCOL = NS_T + 2 + TT
    nc = bass.Bass("TRN2", target_bir_lowering=False)
    P = Prog(nc)
    STOP = int(os.environ.get("KSTOP", "99"))

    def din(name, shape, dt=F32):
        return nc.dram_tensor(name, list(shape), dt, kind="ExternalInput").ap()

    def dout(name, shape, dt=F32):
        return nc.dram_tensor(name, list(shape), dt, kind="ExternalOutput").ap()

    xT_seq = din("xT_seq", [D, T])
    xsT = din("xsT", [D, NSB * 4])
    wm = din("wm", [D, 896])
    rope_p = din("rope_p", [T, 2, 128])
    rope_s = din("rope_s", [4, 2, 128])
    lbl = din("lbl", [128, 2])
    lamv = din("lamv", [128, 256])
    ga_b = din("ga_b", [128, 128])
    gn_b = din("gn_b", [128, 128])
    tri_d = din("tri", [128, 128])
    ident_d = din("ident", [128, 128])
    mnew_d = din("mnew", [4, 8])
    cmbp_d = din("cmbp", [8, 8])
    a16_d = din("a16", [128, 1])
    ptr_d = din("ptr", [128, NSB * NG8], I32)
    ck = din("ck", [NPHYS * 16, 1024])
    cv = din("cv", [NPHYS * 16, 1024])
    st_h = din("st_h", [NSB, 128, 128])
    xT_t = din("xT_t", [D, NCOL])
    x_t = din("x_t", [NCOL, D])
    wg_d = din("wg_t", [16, 128, 8 * 128])
    wa_d = din("wa_t", [8, 128, 4 * 128])
    wb_d = din("wb_t", [8, 128, 4 * 128])
    wout_d = din("wout", [D, D])
    wup_d = din("wup_t", [NFC, 128, 2 * 8 * 128])
    wdn_d = din("wdn", [DFF, D])
    lnp_d = din("lnp", [128, 4, D])
    cvw_d = din("cvw", [128, NFC, 4])
    scv_d = din("scv", [128, NFC, NSL, 2])
    hmask_d = din("hmask", [128, 1])
    idx_d = din("idx", [128, (NTILE + 2) * 8], I32)
    k_p = dout("k_p", [T, 128])
    v_p = dout("v_p", [T, 128])
    k_s = dout("k_s", [NSB * 4, 128])
    v_s = dout("v_s", [NSB * 4, 128])
    S_p = dout("S_p", [128, 128])
    S_s = dout("S_s", [NSB, 128, 128])
    y_o = dout("y_o", [NS_T + TT, D])
    cv_s = dout("cv_s", [128, NFC, NSL, 2])
    cv_p = dout("cv_p", [128, NFC, 2])
    xTb = nc.dram_tensor("xTb", [D, T], BF16)
    snd = nc.dram_tensor("snd", [NCH * 256, 512], BF16)
    gat = nc.dram_tensor("gat", [4 * NCH * 256, 512], BF16)
    wgb = nc.dram_tensor("wgb", [16 * 128, 1024], BF16)
    wab = nc.dram_tensor("wab", [8 * 128, 512], BF16)
    wbb = nc.dram_tensor("wbb", [8 * 128, 512], BF16)
    wupb = nc.dram_tensor("wupb", [NFC * 128, 2048], BF16)
    woutb = nc.dram_tensor("woutb", [D, D], BF16)
    wdnb = nc.dram_tensor("wdnb", [DFF, D], BF16)

    with ExitStack() as top:
        def sb(name, shape, dt=F32, st=top):
            return st.enter_context(nc.sbuf_tensor("s_" + name, list(shape), dt))

        banks = [top.enter_context(nc.psum_tensor("ps%d" % i, [128, 512], F32)) for i in range(8)]

        ident = sb("ident", [128, 128], BF16)
        tri = sb("tri", [128, 128], F32)
        ones = sb("ones", [128, 128], F32)
        trib = sb("trib", [128, 128], BF16)
        lam_t = sb("lam_t", [128, 4], F32)

        def mm(out, lhsT, rhs, start, stop, r, w):
            return P.op("pe", lambda e: e.matmul(out, lhsT, rhs, start=start, stop=stop), r=r, w=w)

        def tp(out, in_, r, w):
            n = in_.shape[0]
            return P.op("pe", lambda e: e.transpose(out, in_, ident[0:n, 0:n]), r=list(r) + ["ident"], w=w)

        def act(out, in_, func, r, w, bias=None, scale=None, accum=None):
            kw = {}
            if bias is not None:
                kw["bias"] = bias
            if scale is not None:
                kw["scale"] = scale
            if accum is not None:
                kw["accum_out"] = accum
            return P.op("act", lambda e: e.activation(out, in_, func, **kw), r=r, w=w)

        def tt(eng, out, a, b, op, r, w):
            return P.op(eng, lambda e: e.tensor_tensor(out, a, b, op), r=r, w=w)

        def ts(eng, out, a, s1, s2, op0, op1, r, w):
            if op1 is None:
                return P.op(eng, lambda e: e.tensor_scalar(out, a, s1, None, op0), r=r, w=w)
            return P.op(eng, lambda e: e.tensor_scalar(out, a, s1, s2, op0, op1), r=r, w=w)

        def stt(eng, out, a, s, b, op0, op1, r, w):
            return P.op(eng, lambda e: e.scalar_tensor_tensor(out, a, s, b, op0, op1), r=r, w=w)

        def cp(eng, out, in_, r, w):
            if eng == "act":
                return P.op("act", lambda e: e.copy(out, in_), r=r, w=w)
            return P.op(eng, lambda e: e.tensor_copy(out, in_), r=r, w=w)

        def dma(q, out, in_, r, w):
            return P.op(q, lambda e: e.dma_start(out=out, in_=in_), r=r, w=w, dma=True)

        identf = sb("identf", [128, 128], F32)
        dma("sp", identf[:], ident_d, [], ["identf"])
        cp("dve", ident[:], identf[:], ["identf"], ["ident"])
        dma("sp", tri[:], tri_d, [], ["tri"])
        cp("dve", trib[:], tri[:], ["tri"], ["trib"])
        P.op("pool", lambda e: e.memset(ones[:], 1.0), w=["ones"])
        lamt = sb("lamt", [128, 256], F32)
        dma("sp", lamt[:], lamv, [], ["lamt"])
        lj = sb("lj", [128, 128], F32)
        ls = sb("ls", [128, 4], F32)
        tt("dve", lj[:], lamt[:, 0:128], lamt[:, 128:256], ALU.mult, ["lamt"], ["lj"])
        P.op("dve", lambda e: e.tensor_reduce(ls[:, 0:2], lj[:].rearrange("p (a b) -> p a b", a=2), AX.X, ALU.add),
             r=["lj"], w=["ls"])
        act(ls[:, 2:4], ls[:, 0:2], AF.Exp, ["ls"], ["ls2"])
        tt("dve", lam_t[:, 0:1], ls[:, 2:3], ls[:, 3:4], ALU.subtract, ["ls2"], ["lam0"])
        ts("dve", lam_t[:, 0:1], lam_t[:, 0:1], LAM_INIT, None, ALU.add, None, ["lam0"], ["lam0"])
        ts("dve", lam_t[:, 1:2], lam_t[:, 0:1], -1.0, None, ALU.mult, None, ["lam0"], ["lam"])

        for i in range(0, T, 2048):
            w_ = min(2048, T - i)
            P.op("pool", lambda e, i=i, w_=w_: e.dma_start(out=xTb.ap()[:, i:i + w_], in_=xT_seq[:, i:i + w_]),
                 w=[("xTb", i // 512 + j) for j in range(w_ // 512)], dma=True)

        def precast(dst, src2d, rows):
            for r0 in range(0, rows, 512):
                r1 = min(rows, r0 + 512)
                o_ = P.op("pool", lambda e, r0=r0, r1=r1: e.dma_start(out=dst.ap()[r0:r1, :], in_=src2d[r0:r1, :]), dma=True)
                o_.bg = True
        with ExitStack() as ph:
            def sbp(name, shape, dt=F32):
                return sb(name, shape, dt, ph)
            OA = sbp("OA", [128, NCH, 512], BF16)
            OB = sbp("OB", [128, NCH, 512], BF16)
            WM = sbp("WM", [128, 8, 896], BF16)
            P.op("pool", lambda e: e.dma_start(out=WM[:], in_=wm.rearrange("(k p) c -> p k c", p=128)), w=["WM"], dma=True)
            gab = sbp("gab", [128, 128])
            gnb = sbp("gnb", [128, 128])
            dma("sp", gab[:], ga_b, [], ["gab"])
            dma("sp", gnb[:], gn_b, [], ["gnb"])
            P.op("act", lambda e: e.mul(gab[:], gab[:], 1.0 - LAM_INIT), r=["gab"], w=["gab"])
            lbt = sbp("lbt", [128, 2])
            dma("sp", lbt[:], lbl, [], ["lbt"])
            lbv = sbp("lbv", [128, 2])
            tt("dve", lbv[:, 0:1], lbt[:, 0:1], lbt[:, 1:2], ALU.subtract, ["lbt"], ["lbv0"])
            act(lbv[:, 0:1], lbv[:, 0:1], AF.Sigmoid, ["lbv0"], ["lbv0"])
            ts("dve", lbv[:, 1:2], lbv[:, 0:1], -1.0, 1.0, ALU.mult, ALU.add, ["lbv0"], ["lbv"])
            Sf = sbp("Sf", [128, 128])
            Sb = sbp("Sb", [128, 128], BF16)
            XS = sbp("XS", [128, 8, NSB * 4], BF16)
            NR = 3
            ring_names = {}

            def rt(name, shape, dt=F32):
                tl = [sbp("%s_%d" % (name, i), shape, dt) for i in range(NR)]
                ring_names[name] = tl
                return tl
            CS = rt("CS", [128, 2, 128])
            T1 = rt("T1", [128, 4, 32]); T2 = rt("T2", [128, 4, 32])
            ROT = rt("ROT", [128, 256]); ROTb = rt("ROTb", [128, 256], BF16)
            Vf = rt("Vf", [128, 128]); Ib = rt("Ib", [128, 128], BF16)
            Gs = rt("Gs", [128, 128]); Gb = rt("Gb", [128, 128], BF16)
            sg = rt("sg", [128, 128]); lf = rt("lf", [128, 128]); kk = rt("kk", [128, 128])
            bT = rt("bT", [128, 128]); eb = rt("eb", [128, 128]); enb = rt("enb", [128, 128]); ehb = rt("ehb", [128, 128])
            qt = rt("qt", [128, 128], BF16); kt_ = rt("kt", [128, 128], BF16); kh = rt("kh", [128, 128], BF16)
            khT = rt("khT", [128, 128], BF16); attb = rt("attb", [128, 128], BF16)
            dec = rt("dec", [128, 2]); sq = rt("sq", [128, 128]); st4 = rt("st4", [128, 4])
            obb = rt("obb", [128, 128], BF16)
            qbS = rt("qbS", [128, 128])

            ph1 = ExitStack()
            ph1.__enter__()
            def sb1(name, shape, dt=F32):
                return sb(name, shape, dt, ph1)
            KT = sb1("KT", [128, T], BF16)
            QT = sb1("QT", [128, T], BF16)
            VA = sb1("VA", [128, NBLK, 132], BF16)
            P.op("pool", lambda e: e.memset(VA[:], 1.0), w=["VAinit"])
            XT = [sb1("XT%d" % i, [128, 8, 512], BF16) for i in range(2)]
            SUPF = [[sb1("sup%s%d" % (nm, i), [128, 512]) for nm in ("q", "g", "l", "k", "b", "e", "n")] for i in range(2)]
            SUPB = [[sb1("supb%s%d" % (nm, i), [128, 512], BF16) for nm in ("q", "k")] for i in range(2)]
            B = banks
            B5b = B[5][:].bitcast(BF16)

            def mixer_block(n, bi, xcols, fmq, fmf, fm_keys, rope_src, k_dst, v_dst, KTd, QTd, Vd, OBd, tag, pre=None):
                s = bi % NR
                K = lambda nm: (nm, s)
                xk = fm_keys
                tmA = B[0][0:n, :]
                tmB = B[1][0:n, 0:128]
                for kc in range(8):
                    mm(tmA, xcols(kc), WM[:, kc, 0:512], kc == 0, kc == 7, xk + ["WM"], [("B",0)])
                for kc in range(8):
                    mm(tmB, xcols(kc), WM[:, kc, 512:640], kc == 0, kc == 7, xk + ["WM"], [("B",1)])
                FMFK = ("B", 4)
                if fmq is None:
                    FMFK = ("B", 3)
                    fmq = B[3][:, 0:n]
                    fmf = B[3][:, 8:8 + n]
                    for kc in range(8):
                        mm(fmq, WM[:, kc, 640:768], xcols(kc), kc == 0, kc == 7, xk + ["WM"], [("B",3)])
                    for kc in range(8):
                        mm(fmf, WM[:, kc, 768:896], xcols(kc), kc == 0, kc == 7, xk + ["WM"], [FMFK])
                cs = CS[s]
                dma("sp", cs[0:n], rope_src, [], [K("CS")])
                xa = tmA[:, 0:256].rearrange("p (g h d) -> p g h d", g=4, h=2)
                x1, x2 = xa[:, :, 0, :], xa[:, :, 1, :]
                cosv = cs[0:n, 0, :].rearrange("p (g d) -> p g d", g=4)
                sinv = cs[0:n, 1, :].rearrange("p (g d) -> p g d", g=4)
                rot = ROT[s][0:n].rearrange("p (g h d) -> p g h d", g=4, h=2)
                t1, t2 = T1[s][0:n], T2[s][0:n]
                tt("dve", t1, x1, cosv, ALU.mult, [("B",0), K("CS")], [K("T1")])
                tt("dve", t2, x2, sinv, ALU.mult, [("B",0), K("CS")], [K("T2")])
                tt("dve", rot[:, :, 0, :], t1, t2, ALU.subtract, [K("T1"), K("T2")], [K("ROT0")])
                tt("dve", t1, x2, cosv, ALU.mult, [("B",0), K("CS"), K("ROT0")], [K("T1")])
                tt("dve", t2, x1, sinv, ALU.mult, [("B",0), K("CS"), K("ROT0")], [K("T2")])
                tt("dve", rot[:, :, 1, :], t1, t2, ALU.add, [K("T1"), K("T2")], [K("ROT1")])
                RK = [K("ROT0"), K("ROT1")]
                dma("sp", k_dst, ROT[s][0:n, 0:128], RK, [])
                cp("act", ROTb[s][0:n], ROT[s][0:n], RK, [K("ROTb")])
                tp(B5b[:, 0:n], ROTb[s][0:n, 0:128], [K("ROTb")], [("B",5)])
                tp(B5b[:, 128:128 + n], ROTb[s][0:n, 128:256], [K("ROTb")], [("B",5)])
                cp("act", KTd, B5b[:, 0:n], [("B",5)], [("KT", tag)])
                cp("act", QTd, B5b[:, 128:128 + n], [("B",5)], [("QT", tag)])
                cp("act", Vf[s][0:n], tmA[:, 256:384], [("B",0)], [K("Vf")])
                dma("sp", v_dst, Vf[s][0:n], [K("Vf")], [])
                cp("pool", Vd, Vf[s][0:n], [K("Vf"), "VAinit"], [("V", tag)])
                cp("act", Ib[s][0:n], tmA[:, 384:512], [("B",0)], [K("Ib")])
                act(Gs[s][0:n], tmB, AF.Silu, [("B",1)], [K("Gs")])
                tt("pool", Gb[s][0:n], Gs[s][0:n], gnb[0:n], ALU.mult, [K("Gs"), "gnb"], [K("Gb")])
                if pre is None:
                    cp("act", qbS[s][:, 0:n], fmq, [("B",3)], [K("qbS")])
                    act(sg[s][:, 0:n], fmf, AF.Sigmoid, [FMFK], [K("sg")])
                    ts("dve", sg[s][:, 0:n], sg[s][:, 0:n], lbv[:, 1:2], lbv[:, 0:1], ALU.mult, ALU.add, [K("sg"), "lbv", "lbv0"], [K("sg")])
                    act(lf[s][:, 0:n], sg[s][:, 0:n], AF.Ln, [K("sg")], [K("lf")])
                    ts("pool", kk[s][:, 0:n], sg[s][:, 0:n], -1.0, 1.0, ALU.mult, ALU.add, [K("sg")], [K("kk")])
                    P.op("dve", lambda e: e.tensor_tensor_scan(bT[s][:, 0:n], ones[:, 0:n], lf[s][:, 0:n], 0.0, ALU.mult, ALU.add),
                         r=[K("lf"), "ones"], w=[K("bT")])
                    act(eb[s][:, 0:n], bT[s][:, 0:n], AF.Exp, [K("bT")], [K("eb")])
                    tt("dve", qt[s][:, 0:n], qbS[s][:, 0:n], eb[s][:, 0:n], ALU.mult, [K("qbS"), K("eb")], [K("qt")])
                    act(enb[s][:, 0:n], bT[s][:, 0:n], AF.Exp, [K("bT")], [K("enb")], scale=-1.0)
                    tt("pool", kt_[s][:, 0:n], kk[s][:, 0:n], enb[s][:, 0:n], ALU.mult, [K("kk"), K("enb")], [K("kt")])
                    qt_ap, kt_ap, kk_ap, bT_ap = qt[s][:, 0:n], kt_[s][:, 0:n], kk[s][:, 0:n], bT[s][:, 0:n]
                    kq, kkt, kkk, kbt = K("qt"), K("kt"), K("kk"), K("bT")
                else:
                    qt_ap, kt_ap, kk_ap, bT_ap, pk = pre
                    kq, kkt, kkk, kbt = pk
                act(ehb[s][:, 0:n], bT_ap, AF.Exp, [kbt], [K("ehb")], scale=-1.0, bias=bT_ap[:, n - 1:n])
                tt("pool", kh[s][:, 0:n], kk_ap, ehb[s][:, 0:n], ALU.mult, [kkk, K("ehb")], [K("kh")])
                act(dec[s][:, 0:1], bT_ap[:, n - 1:n], AF.Exp, [kbt], [K("dec")])
                tp(B5b[0:n, 256:384], kh[s][:, 0:n], [K("kh")], [("B",5)])
                cp("act", khT[s][0:n], B5b[0:n, 256:384], [("B",5)], [K("khT")])
                attT = B[1][0:n, 128:128 + n]
                mm(attT, kt_ap, qt_ap, True, True, [kkt, kq], [("B",1)])
                tt("dve", attb[s][0:n, 0:n], attT, tri[0:n, 0:n], ALU.mult, [("B",1), "tri"], [K("attb")])
                def late():
                    o_ps = B[2][0:n, 0:128]
                    mm(o_ps, qt_ap, Sb[:], True, False, [kq, "Sb"], [("B",2)])
                    mm(o_ps, attb[s][0:n, 0:n], Ib[s][0:n], False, True, [K("attb"), K("Ib")], [("B",2)])
                    U = B[7][:, 256:384]
                    mm(U, khT[s][0:n], Ib[s][0:n], True, True, [K("khT"), K("Ib")], [("B",7)])
                    stt("dve", Sf[:], Sf[:], dec[s][:, 0:1], U, ALU.mult, ALU.add, ["Sf", K("dec"), ("B",7)], ["Sf"])
                    cp("pool", Sb[:], Sf[:], ["Sf"], ["Sb"])
                    act(sq[s][0:n], o_ps, AF.Square, [("B",2)], [K("sq"), K("ss")], accum=st4[s][0:n, 0:1])
                    ts("dve", st4[s][0:n, 1:2], st4[s][0:n, 0:1], 1.0 / 128, RMS_EPS, ALU.mult, ALU.add, [K("ss")], [K("ms")])
                    act(st4[s][0:n, 2:3], st4[s][0:n, 1:2], AF.Sqrt, [K("ms")], [K("sd")])
                    P.op("dve", lambda e: e.reciprocal(st4[s][0:n, 3:4], st4[s][0:n, 2:3]), r=[K("sd")], w=[K("rstd")])
                    stt("dve", obb[s][0:n], o_ps, st4[s][0:n, 3:4], Gb[s][0:n], ALU.mult, ALU.mult, [("B",2), K("rstd"), K("Gb")], [K("obb")])
                    B6b_ = B[6][:].bitcast(BF16)
                    tp(B6b_[:, 384:384 + n], obb[s][0:n], [K("obb")], [("B",6)])
                    cp("act", OBd, B6b_[:, 384:384 + n], [("B",6)], [("OBs", tag)])
                return late

            if STOP <= 1:
                P.cut = True
            P.op("pool", lambda e: e.memset(Sf[:], 0.0), w=["Sf"])
            P.op("pool", lambda e: e.memset(Sb[:], 0.0), w=["Sb"])
            pend = [None]
            for su in range(NSUP):
                xt = XT[su % 2]
                xkey = ("XT", su % 2)
                dma("sp", xt[:], xTb.ap()[:, su * 512:(su + 1) * 512].rearrange("(k p) n -> p k n", p=128),
                    [("xTb", su)], [xkey])
                fmq, fmf = B[3][:, :], B[4][:, :]
                for kc in range(8):
                    mm(fmq, WM[:, kc, 640:768], xt[:, kc, :], kc == 0, kc == 7, [xkey, "WM"], [("B",3)])
                for kc in range(8):
                    mm(fmf, WM[:, kc, 768:896], xt[:, kc, :], kc == 0, kc == 7, [xkey, "WM"], [("B", 4)])
                p2 = su % 2
                sq_, sg_, sl_, sk_, sbt, se_, sn_ = SUPF[p2]
                sqt, skt = SUPB[p2]
                SK = ("sup", p2)
                cp("act", sq_[:], fmq, [("B", 3)], [("supq", p2)])
                act(sg_[:], fmf, AF.Sigmoid, [("B", 4)], [("supg", p2)])
                ts("dve", sg_[:], sg_[:], lbv[:, 1:2], lbv[:, 0:1], ALU.mult, ALU.add, [("supg", p2), "lbv", "lbv0"], [("supg", p2)])
                act(sl_[:], sg_[:], AF.Ln, [("supg", p2)], [("supl", p2)])
                ts("pool", sk_[:], sg_[:], -1.0, 1.0, ALU.mult, ALU.add, [("supg", p2)], [("supk", p2)])
                for j in range(4):
                    P.op("dve", lambda e, j=j, sbt=sbt, sl_=sl_: e.tensor_tensor_scan(sbt[:, j * 128:(j + 1) * 128], ones[:, 0:128], sl_[:, j * 128:(j + 1) * 128], 0.0, ALU.mult, ALU.add),
                         r=[("supl", p2), "ones"], w=[("supb", p2, j)])
                sbk = [("supb", p2, j) for j in range(4)]
                act(se_[:], sbt[:], AF.Exp, sbk, [("supe", p2)])
                tt("dve", sqt[:], sq_[:], se_[:], ALU.mult, [("supq", p2), ("supe", p2)], [("supqt", p2)])
                act(sn_[:], sbt[:], AF.Exp, sbk, [("supn", p2)], scale=-1.0)
                tt("pool", skt[:], sk_[:], sn_[:], ALU.mult, [("supk", p2), ("supn", p2)], [("supkt", p2)])
                for j in range(4):
                    blk = su * 4 + j
                    c0 = blk * 128
                    lt = mixer_block(128, blk, lambda kc, xt=xt, j=j: xt[:, kc, j * 128:(j + 1) * 128],
                                     fmq[:, j * 128:(j + 1) * 128], fmf[:, j * 128:(j + 1) * 128], [xkey],
                                     rope_p[c0:c0 + 128], k_p[c0:c0 + 128, :], v_p[c0:c0 + 128, :],
                                     KT[:, c0:c0 + 128], QT[:, c0:c0 + 128], VA[:, blk, 0:128],
                                     OB[:, blk // 4, (blk % 4) * 128:(blk % 4 + 1) * 128], blk,
                                     pre=(sqt[:, j * 128:(j + 1) * 128], skt[:, j * 128:(j + 1) * 128], sk_[:, j * 128:(j + 1) * 128],
                                          sbt[:, j * 128:(j + 1) * 128],
                                          (("supqt", p2), ("supkt", p2), ("supk", p2), ("supb", p2, j))))
                    if os.environ.get("KOLD") == "1":
                        lt()
                        lt = None
                    if pend[0] is not None:
                        pend[0]()
                    pend[0] = lt
            if pend[0] is not None:
                pend[0]()
            precast(wgb, wg_d.rearrange("a p c -> (a p) c"), 16 * 128)
            precast(wab, wa_d.rearrange("a p c -> (a p) c"), 8 * 128)
            precast(wbb, wb_d.rearrange("a p c -> (a p) c"), 8 * 128)
            precast(woutb, wout_d, D)
            precast(wupb, wup_d.rearrange("a p c -> (a p) c"), NFC * 128)
            precast(wdnb, wdn_d, DFF)
            dma("sp", S_p, Sf[:], ["Sf"], [])
            P.barrier()

            if STOP <= 2:
                P.cut = True
            PT = [[sb1("PT%d_%d" % (i, m), [128, 512], BF16) for m in range(2)] for i in range(2)]
            ep = [sb1("ep%d" % i, [128, 8]) for i in range(NR)]
            o0 = [sb1("o0%d" % i, [128, 128]) for i in range(NR)]; o1 = [sb1("o1%d" % i, [128, 128]) for i in range(NR)]; oab = [sb1("oab%d" % i, [128, 128], BF16) for i in range(NR)]
            groups = []
            gi = 0
            for qb in range(NBLK):
                nkb = qb + 1
                for j0 in range(0, nkb, 4):
                    groups.append((qb, j0, min(4, nkb - j0), gi % 2))
                    gi += 1

            def accs(qb):
                base = 4 + 2 * (qb % 2)
                return [B[base][:, 0:129], B[base + 1][:, 0:129]], base

            def emit_qk(g):
                qb, j0, nj, st_ = g
                q0 = qb * 128
                nkb = qb + 1
                for m in range(2):
                    stb = B[st_ * 2 + m]
                    for jj in range(nj):
                        j = j0 + jj
                        mm(stb[:, jj * 128:(jj + 1) * 128], KT[64 * m:64 * m + 64, j * 128:(j + 1) * 128],
                           QT[64 * m:64 * m + 64, q0:q0 + 128], True, True, [("KT", j), ("QT", qb)], [("B", st_ * 2 + m)])
                    act(PT[st_][m][:, 0:nj * 128], stb[:, 0:nj * 128], AF.Exp, [("B", st_ * 2 + m)], [("PT", st_, m)], scale=SCALE)
                    if j0 + nj == nkb:
                        dcol = (nj - 1) * 128
                        tt("pool", PT[st_][m][:, dcol:dcol + 128], PT[st_][m][:, dcol:dcol + 128], trib[:], ALU.mult,
                           [("PT", st_, m), "trib"], [("PT", st_, m)])

            def emit_pv(g):
                qb, j0, nj, st_ = g
                nkb = qb + 1
                acc, base = accs(qb)
                for jj in range(nj):
                    j = j0 + jj
                    for m in range(2):
                        mm(acc[m], PT[st_][m][:, jj * 128:(jj + 1) * 128], VA[:, j, 0:129], j == 0, j == nkb - 1,
                           [("PT", st_, m), ("V", j), "VAinit"], [("B", base + m)])
                if j0 + nj != nkb:
                    return
                s = qb % NR
                K = lambda nm: (nm + "_e", s)
                e_ = ep[s]
                P.op("dve", lambda e, e_=e_: e.reciprocal(e_[:, 0:1], acc[0][:, 128:129]), r=[("B", base)], w=[K("rl0")])
                P.op("dve", lambda e, e_=e_: e.reciprocal(e_[:, 1:2], acc[1][:, 128:129]), r=[("B", base + 1)], w=[K("rl1")])
                tt("dve", e_[:, 2:3], e_[:, 1:2], lam_t[:, 1:2], ALU.mult, [K("rl1"), "lam"], [K("nl1")])
                ts("dve", o0[s][:], acc[0][:, 0:128], e_[:, 0:1], None, ALU.mult, None, [("B", base), K("rl0")], [K("o0")])
                stt("dve", o1[s][:], acc[1][:, 0:128], e_[:, 2:3], o0[s][:], ALU.mult, ALU.add, [("B", base + 1), K("nl1"), K("o0")], [K("o1")])
                P.op("dve", lambda e, e_=e_, s=s: e.scalar_tensor_tensor(o0[s][:], o1[s][:], 1.0, o1[s][:], ALU.mult, ALU.mult, accum_out=e_[:, 3:4]),
                     r=[K("o1")], w=[K("o0"), K("ss")])
                ts("dve", e_[:, 4:5], e_[:, 3:4], 1.0 / 128, RMS_EPS, ALU.mult, ALU.add, [K("ss")], [K("ms")])
                act(e_[:, 5:6], e_[:, 4:5], AF.Ln, [K("ms")], [K("sd")])
                act(e_[:, 6:7], e_[:, 5:6], AF.Exp, [K("sd")], [K("rstd")], scale=-0.5)
                stt("dve", oab[s][:], o1[s][:], e_[:, 6:7], gab[:], ALU.mult, ALU.mult, [K("o1"), K("rstd"), "gab"], [K("oab")])
                tpo = B[base][:].bitcast(BF16)[:, 512:640]
                tp(tpo, oab[s][:], [K("oab")], [("B", base)])
                cp("act", OA[:, qb // 4, (qb % 4) * 128:(qb % 4 + 1) * 128], tpo, [("B", base)], [("OAs", qb)])

            for i, g in enumerate(groups):
                emit_qk(g)
                if i >= 1:
                    emit_pv(groups[i - 1])
            emit_pv(groups[-1])

            if STOP <= 3:
                P.cut = True
            P.barrier()
            ph1.close()
            P.op("pool", lambda e: e.dma_start(out=XS[:], in_=xsT.rearrange("(k p) n -> p k n", p=128)), w=["XS"], dma=True)
            a16 = sbp("a16", [128, 1])
            dma("sp", a16[:], a16_d, [], ["a16"])
            pti = sbp("pti", [128, NSB * NG8], I32)
            ptf = sbp("ptf", [128, NSB * NG8])
            gix = sbp("gix", [128, NSB * NG8], I32)
            dma("sp", pti[:], ptr_d, [], ["pti"])
            cp("dve", ptf[:], pti[:], ["pti"], ["ptf"])
            ts("dve", ptf[:], ptf[:], 16.0, a16[:, 0:1], ALU.mult, ALU.add, ["ptf", "a16"], ["ptf"])
            cp("dve", gix[:], ptf[:], ["ptf"], ["gix"])
            KG = [sbp("KG%d" % i, [128, NKT, 128], BF16) for i in range(2)]
            VG = [sbp("VG%d" % i, [128, NKT, 128], BF16) for i in range(2)]
            KTs = [sbp("KTs%d" % i, [128, 1024], BF16) for i in range(2)]
            Qbd = sbp("Qbd", [128, 8], BF16)
            P.op("pool", lambda e: e.memset(Qbd[:], 0.0), w=["Qbd"])
            KTn = sbp("KTn", [128, 4], BF16); QTn = sbp("QTn", [128, 4], BF16)
            Vn = sbp("Vn", [4, 128], BF16)
            PTs = [sbp("PTs%d" % i, [128, NKT * 8], BF16) for i in range(2)]
            PTn = sbp("PTn", [4, 8]); PTnb = sbp("PTnb", [4, 8], BF16)
            mnew = sbp("mnew", [4, 8])
            dma("sp", mnew[:], mnew_d, [], ["mnew"])
            rs = sbp("rs", [128, 8])
            OS = sbp("OS", [8, NSB, 128]); LS = sbp("LS", [8, NSB])
            OBs = sbp("OBs", [128, NSB * 4], BF16)
            ck_v = ck
            cv_v = cv
            for b in range(NSB):
                s2 = b % 2
                for G in range(NG8):
                    col = b * NG8 + G
                    P.op("pool", lambda e, G=G, col=col, s2=s2: e.indirect_dma_start(
                        out=KG[s2][:, G * 8:(G + 1) * 8, :].rearrange("p r d -> p (r d)"), out_offset=None, in_=ck_v,
                        in_offset=bass.IndirectOffsetOnAxis(ap=gix[:, col:col + 1], axis=0)),
                        r=["gix"], w=[("KG", s2, G)], dma=True)
                    P.op("pool", lambda e, G=G, col=col, s2=s2: e.indirect_dma_start(
                        out=VG[s2][:, G * 8:(G + 1) * 8, :].rearrange("p r d -> p (r d)"), out_offset=None, in_=cv_v,
                        in_offset=bass.IndirectOffsetOnAxis(ap=gix[:, col:col + 1], axis=0)),
                        r=["gix"], w=[("VG", s2, G)], dma=True)
                if STOP == 4 and os.environ.get("KSUB") == "1":
                    P.cut = True
                dma("sp", Sf[:], st_h[b], [], ["Sf"])
                cp("pool", Sb[:], Sf[:], ["Sf"], ["Sb"])
                mixer_block(4, NBLK + b, lambda kc, b=b: XS[:, kc, b * 4:(b + 1) * 4], None, None, ["XS"],
                            rope_s, k_s[b * 4:(b + 1) * 4, :], v_s[b * 4:(b + 1) * 4, :],
                            KTn[:], QTn[:], Vn[:], OBs[:, b * 4:(b + 1) * 4], ("s", b))()
                dma("sp", S_s[b], Sf[:], ["Sf"], [])
                tg = ("s", b)
                cp("dve", Qbd[0:64, 0:4], QTn[0:64, :], [("QT", tg)], ["Qbd"])
                cp("dve", Qbd[64:128, 4:8], QTn[64:128, :], [("QT", tg)], ["Qbd"])
                if STOP == 4 and os.environ.get("KSUB") == "2":
                    P.cut = True
                STb = B[7][:, 0:NKT * 8]
                for G in range(NG8):
                    kb = (b * NG8 + G) % 2
                    psb = B[6][:].bitcast(BF16)
                    for r_ in range(8):
                        tp(psb[:, r_ * 128:(r_ + 1) * 128], KG[s2][:, G * 8 + r_, :], [("KG", s2, G)], [("B", 6)])
                    cp("act" if G % 2 == 0 else "dve", KTs[kb][:], psb[:, :], [("B", 6)], [("KTs", kb)])
                    for r_ in range(8):
                        kt = G * 8 + r_
                        mm(STb[:, kt * 8:(kt + 1) * 8], KTs[kb][:, r_ * 128:(r_ + 1) * 128], Qbd[:], True, True,
                           [("KTs", kb), "Qbd"], [("B",7)])
                STn = B[1][0:4, 384:392]
                mm(STn, KTn[:], Qbd[:], True, True, [("KT", tg), "Qbd"], [("B",1)])
                act(PTs[s2][:], STb, AF.Exp, [("B",7)], [("PTs", s2)], scale=SCALE)
                act(PTn[:], STn, AF.Exp, [("B",1)], ["PTn"], scale=SCALE)
                tt("dve", PTn[:], PTn[:], mnew[:], ALU.mult, ["PTn", "mnew"], ["PTn"])
                cp("dve", PTnb[:], PTn[:], ["PTn"], ["PTnb"])
                P.op("dve", lambda e, s2=s2: e.tensor_reduce(rs[:], PTs[s2][:].rearrange("p (k q) -> p q k", q=8), AX.X, ALU.add),
                     r=[("PTs", s2)], w=["rs"])
                Lp = B[1][0:8, 400:401]
                mm(Lp, rs[:], ones[:, 0:1], True, False, ["rs", "ones"], [("B",1)])
                mm(Lp, PTn[:], ones[0:4, 0:1], False, True, ["PTn", "ones"], [("B",1)])
                if STOP == 4 and os.environ.get("KSUB") == "3":
                    P.cut = True
                Op_ = B[4][0:8, 0:128]
                for kt in range(NKT):
                    mm(Op_, PTs[s2][:, kt * 8:(kt + 1) * 8], VG[s2][:, kt, :], kt == 0, False,
                       [("PTs", s2), ("VG", s2, kt // 8)], [("B",4)])
                mm(Op_, PTnb[:], Vn[:], False, True, ["PTnb", ("V", tg)], [("B",4)])
                cp("act", OS[:, b, :], Op_, [("B",4)], ["OS"])
                cp("act", LS[:, b:b + 1], Lp, [("B",1)], ["LS"])
            if STOP == 4 and os.environ.get("KSUB") == "4":
                P.cut = True
            RL = sbp("RL", [8, NSB])
            P.op("dve", lambda e: e.reciprocal(RL[:], LS[:]), r=["LS"], w=["RL"])
            ON = sbp("ON", [8, NSB, 128])
            for b in range(NSB):
                ts("dve", ON[:, b, :], OS[:, b, :], RL[:, b:b + 1], None, ALU.mult, None, ["OS", "RL"], [("ON", b)])
            cmbp = sbp("cmbp", [8, 8])
            cmb = sbp("cmb", [8, 4])
            dma("sp", cmbp[:], cmbp_d, [], ["cmbp"])
            stt("dve", cmb[:], cmbp[:, 4:8], lam_t[0:8, 1:2], cmbp[:, 0:4], ALU.mult, ALU.add, ["cmbp", "lam"], ["cmb"])
            osb = sbp("osb", [4, NSB, 128]); osq = sbp("osq", [4, 128])
            sst = sbp("sst", [4, 4, NSB])
            oasb = sbp("oasb", [4, NSB, 128], BF16)
            B6b = B[6][:].bitcast(BF16)
            for b in range(NSB):
                bk = B[b % 3]
                mm(bk[0:4, 0:128], cmb[:], ON[:, b, :], True, True, ["cmb", ("ON", b)], [("B", b % 3)])
                cp("act", osb[:, b, :], bk[0:4, 0:128], [("B", b % 3)], [("osb", b)])
                act(osq[:], osb[:, b, :], AF.Square, [("osb", b)], ["osq", ("sst0", b)], accum=sst[:, 0, b:b + 1])
            allb = [("sst0", b) for b in range(NSB)]
            ts("dve", sst[:, 1, :], sst[:, 0, :], 1.0 / 128, RMS_EPS, ALU.mult, ALU.add, allb, ["sst1"])
            act(sst[:, 2, :], sst[:, 1, :], AF.Sqrt, ["sst1"], ["sst2"])
            P.op("dve", lambda e: e.reciprocal(sst[:, 3, :], sst[:, 2, :]), r=["sst2"], w=["sst3"])
            for b in range(NSB):
                stt("dve", oasb[:, b, :], osb[:, b, :], sst[:, 3, b:b + 1], gab[0:4, :], ALU.mult, ALU.mult, [("osb", b), "sst3", "gab"], [("oasb", b)])
                tp(B6b[:, 256 + b * 4:256 + (b + 1) * 4], oasb[:, b, :], [("oasb", b)], [("B", 6)])
            for j in range(4):
                cp("act", OA[:, NSUP + j, 0:NSL * 4], B6b[:, 256 + j * NSL * 4:256 + (j + 1) * NSL * 4], [("B", 6)], [("OAsmp", j)])
                cp("dve", OB[:, NSUP + j, 0:NSL * 4], OBs[:, j * NSL * 4:(j + 1) * NSL * 4],
                   [("OBs", ("s", b)) for b in range(NSB)], [("OBsmp", j)])
            if STOP <= 4:
                P.cut = True
            P.barrier()
            sview = snd.ap().rearrange("(c two f) n -> f two c n", two=2, f=128)
            d1 = P.op("sp", lambda e: e.dma_start(out=sview[:, 0], in_=OA[:]), dma=True)
            d2 = P.op("sp", lambda e: e.dma_start(out=sview[:, 1], in_=OB[:]), dma=True)

        NGR = NCH // 4
        for k in range(NGR):
            P.op("pool", lambda e, k=k: e.collective_compute(
                "AllGather", ALU.bypass, replica_groups=[[0, 1, 2, 3], [4, 5, 6, 7]],
                ins=[snd.ap()[k * 1024:(k + 1) * 1024, :].opt()],
                outs=[gat.ap()[k * 4096:(k + 1) * 4096, :].opt()]), cc=True, extra=[d1, d2])
        P.barrier(include_bg=True)

        if STOP <= 5:
            P.cut = True
        with ExitStack() as ph:
            def sbp(name, shape, dt=F32):
                return sb(name, shape, dt, ph)
            B = banks
            WOUT = sbp("WOUT", [128, 8, D], BF16)
            WDN = sbp("WDN", [128, NFC, D], BF16)
            for kc in range(0, 8, 2):
                P.op("sp", lambda e, kc=kc: e.dma_start(out=WOUT[:, kc:kc + 2, :], in_=woutb.ap().rearrange("(k p) c -> p k c", p=128)[:, kc:kc + 2, :]), w=["WOUT"], dma=True)
            for kc in range(0, NFC, 2):
                P.op("sp", lambda e, kc=kc: e.dma_start(out=WDN[:, kc:kc + 2, :], in_=wdnb.ap().rearrange("(k p) c -> p k c", p=128)[:, kc:kc + 2, :]), w=["WDN"], dma=True)
            LNP = sbp("LNP", [128, 4, D])
            dma("sp", LNP[:], lnp_d, [], ["LNP"])
            CVW = sbp("CVW", [128, NFC, 4])
            dma("sp", CVW[:], cvw_d, [], ["CVW"])
            SCV = sbp("SCV", [128, NFC, NSL, 2])
            dma("sp", SCV[:], scv_d, [], ["SCV"])
            hmask = sbp("hmask", [128, 1])
            dma("sp", hmask[:], hmask_d, [], ["hmask"])
            IDX = sbp("IDX", [128, (NTILE + 2) * 8], I32)
            dma("sp", IDX[:], idx_d, [], ["IDX"])
            carry = sbp("carry", [128, NFC, 2])
            CVS = sbp("CVS", [128, NFC, NSL, 2])
            XTt = sbp("XTt", [128, 8, 512], BF16)
            OAT = sbp("OAT", [128, 4, 512], BF16)
            OBT = sbp("OBT", [128, 4, 512], BF16)
            SG = sbp("SG", [128, 4, 512], BF16)
            MT = sbp("MT", [128, 8, 512], BF16)
            tA = sbp("tA", [128, 512]); tB = sbp("tB", [128, 512])
            HTM = sbp("HTM", [128, 4, D])
            HT = sbp("HT", [128, 8, 512], BF16)
            UT = sbp("UT", [128, NFC, 512], BF16)
            GT = UT[:, 0:16, :].rearrange("p (w h t) n -> p w h t n", w=2, h=4)
            WS = [sbp("WS%d" % i, [128, 2048], BF16) for i in range(4)]
            Xr = [sbp("Xr%d" % i, [128, D]) for i in range(2)]
            Z = [sbp("Z%d" % i, [128, D]) for i in range(2)]
            Hb = [sbp("Hb0", [128, D], BF16)] * 2
            bst = [sbp("bst%d" % i, [128, 16]) for i in range(2)]
            AE = [sbp("AE%d" % i, [128, 516]) for i in range(2)]
            Cc = [sbp("Cc%d" % i, [128, 512]) for i in range(2)]
            Gl = [sbp("Gl%d" % i, [128, 512]) for i in range(2)]
            wsi = [0]

            def wload(src, nel, name):
                i = wsi[0] % 4
                wsi[0] += 1
                P.op("sp", lambda e: e.dma_start(out=WS[i][:, 0:nel], in_=src), w=[("WS", i)], dma=True)
                return WS[i], ("WS", i)

            def layer_norm(zt, nt, gi_, out, rkeys, wkey, s):
                FM = 512
                nchk = D // FM
                st = bst[s]
                for c in range(nchk):
                    P.op("dve", lambda e, c=c: e.bn_stats(st[0:nt, c * 6:(c + 1) * 6], zt[0:nt, c * FM:(c + 1) * FM]),
                         r=rkeys, w=[("bst", s, c)])
                P.op("dve", lambda e: e.bn_aggr(st[0:nt, 12:14], st[0:nt, 0:12].rearrange("p (c k) -> p c k", k=6)),
                     r=[("bst", s, c) for c in range(nchk)], w=[("mv", s)])
                ts("dve", st[0:nt, 14:15], st[0:nt, 13:14], LN_EPS, None, ALU.add, None, [("mv", s)], [("ve", s)])
                act(st[0:nt, 14:15], st[0:nt, 14:15], AF.Sqrt, [("ve", s)], [("ve", s)])
                P.op("dve", lambda e: e.reciprocal(st[0:nt, 15:16], st[0:nt, 14:15]), r=[("ve", s)], w=[("rs", s)])
                ts("dve", zt[0:nt], zt[0:nt], st[0:nt, 12:13], st[0:nt, 15:16], ALU.subtract, ALU.mult, rkeys + [("mv", s), ("rs", s)], rkeys)
                tt("pool", zt[0:nt], zt[0:nt], LNP[0:nt, gi_, :], ALU.mult, rkeys + ["LNP"], rkeys)
                tt("pool", out, zt[0:nt], LNP[0:nt, gi_ + 1, :], ALU.add, rkeys + ["LNP"], [wkey])

            P.op("pool", lambda e: e.memset(carry[:], 0.0), w=["carry"])
            gview = gat.ap()

            def gather(dst, col, key):
                P.op("pool", lambda e: e.indirect_dma_start(out=dst, out_offset=None, in_=gview,
                     in_offset=bass.IndirectOffsetOnAxis(ap=IDX[:, col:col + 1], axis=0)), r=["IDX"], w=[key], dma=True)

            ybase = 0
            for ti in range(NTILE + 1):
                stile = ti == 0
                if stile:
                    n, c0, segs, seglen = NS_T + 2, 0, NSL, 4
                    ny = NS_T
                else:
                    n, c0, segs, seglen = 512, NS_T + 2 + (ti - 1) * 512, 1, 512
                    ny = 512
                tk = ("tile", ti)
                P.op("pool", lambda e, c0=c0, n=n: e.dma_start(out=XTt[:, :, 0:n], in_=xT_t[:, c0:c0 + n].rearrange("(k p) n -> p k n", p=128)),
                     w=["XTt"], dma=True)
                if stile:
                    for wh in range(2):
                        for h in range(4):
                            for pt in range(2):
                                gather(GT[:, wh, h, pt, :], wh * 8 + h * 2 + pt, ("GT", wh, h, pt))
                    gk = [("GT", wh, h, pt) for wh in range(2) for h in range(4) for pt in range(2)]
                    cp("dve", OAT[:, :, 0:NS_T], GT[:, 0, :, 0, 0:NS_T], gk, ["OAT"])
                    cp("dve", OAT[:, :, NS_T:NS_T + 2], GT[:, 1, :, 0, 510:512], gk, ["OAT"])
                    cp("dve", OBT[:, :, 0:NS_T], GT[:, 0, :, 1, 0:NS_T], gk, ["OBT"])
                    cp("dve", OBT[:, :, NS_T:NS_T + 2], GT[:, 1, :, 1, 510:512], gk, ["OBT"])
                else:
                    for h in range(4):
                        gather(OAT[:, h, :], (ti + 1) * 8 + h * 2, "OAT")
                        gather(OBT[:, h, :], (ti + 1) * 8 + h * 2 + 1, "OBT")
                for cb in range(8):
                    sl = (cb % 2) * 2
                    for gi2 in range(2):
                        wsb, wk = wload(wgb.ap()[(gi2 * 8 + cb) * 128:(gi2 * 8 + cb + 1) * 128, :], 1024, "wg")
                        ps = B[gi2][:, 0:n]
                        for kc in range(8):
                            mm(ps, wsb[:, kc * 128:(kc + 1) * 128], XTt[:, kc, 0:n], kc == 0, kc == 7, [wk, "XTt"], [("B", gi2)])
                        act(SG[:, sl + gi2, 0:n], ps, AF.Sigmoid, [("B", gi2)], [("SG", sl + gi2)])
                    wsa, wka = wload(wab.ap()[cb * 128:(cb + 1) * 128, :], 512, "wa")
                    wsb2, wkb = wload(wbb.ap()[cb * 128:(cb + 1) * 128, :], 512, "wb")
                    pa = B[2 + cb % 2][:, 0:n]
                    pb = B[4 + cb % 2][:, 0:n]
                    for kc in range(4):
                        mm(pa, wsa[:, kc * 128:(kc + 1) * 128], OAT[:, kc, 0:n], kc == 0, kc == 3, [wka, "OAT"], [("B", 2 + cb % 2)])
                    for kc in range(4):
                        mm(pb, wsb2[:, kc * 128:(kc + 1) * 128], OBT[:, kc, 0:n], kc == 0, kc == 3, [wkb, "OBT"], [("B", 4 + cb % 2)])
                    tt("dve", tA[:, 0:n], pa, SG[:, sl, 0:n], ALU.mult, [("B", 2 + cb % 2), ("SG", sl)], ["tA"])
                    tt("dve", tB[:, 0:n], pb, SG[:, sl + 1, 0:n], ALU.mult, [("B", 4 + cb % 2), ("SG", sl + 1)], ["tB"])
                    tt("pool", MT[:, cb, 0:n], tA[:, 0:n], tB[:, 0:n], ALU.add, ["tA", "tB"], [("MT", cb)])
                nblk = (n + 127) // 128
                MTk = [("MT", cb) for cb in range(8)]
                for tb in range(nblk):
                    t0 = tb * 128
                    nt = min(128, n - t0)
                    s = tb % 2
                    dma("sp", Xr[s][0:nt], x_t[c0 + t0:c0 + t0 + nt, :], [], [("Xr", s)])
                    for hf in range(2):
                        ps = B[6 + hf][0:nt, :]
                        for kc in range(8):
                            mm(ps, MT[:, kc, t0:t0 + nt], WOUT[:, kc, hf * 512:(hf + 1) * 512], kc == 0, kc == 7, MTk + ["WOUT"], [("B", 6 + hf)])
                        stt("dve", Z[s][0:nt, hf * 512:(hf + 1) * 512], Xr[s][0:nt, hf * 512:(hf + 1) * 512], ALPHA, ps, ALU.mult, ALU.add,
                            [("Xr", s), ("B", 6 + hf)], [("Z", s)])
                    layer_norm(Z[s], nt, 0, HTM[0:nt, tb, :], [("Z", s)], ("HTM", tb), s)
                    cp("act", Hb[s][0:nt], HTM[0:nt, tb, :], [("HTM", tb)], ["Hb"])
                    psb = B[tb % 2][:].bitcast(BF16)
                    for kc in range(8):
                        tp(psb[:, kc * 128:kc * 128 + nt], Hb[s][0:nt, kc * 128:(kc + 1) * 128], ["Hb"], [("B", tb % 2)])
                    cp("act", HT[:, :, t0:t0 + nt], psb[:, :].rearrange("p (k t) -> p k t", k=8)[:, :, 0:nt], [("B", tb % 2)], [("HT", tb)])
                HTk = [("HT", tb) for tb in range(nblk)]
                for fc in range(NFC):
                    wsu, wku = wload(wupb.ap()[fc * 128:(fc + 1) * 128, :], 2048, "wup")
                    s = fc % 2
                    pa = B[2 + s][:, 0:n]
                    pg = B[4 + s][:, 0:n]
                    for kc in range(8):
                        mm(pa, wsu[:, kc * 128:(kc + 1) * 128], HT[:, kc, 0:n], kc == 0, kc == 7, [wku] + HTk, [("B", 2 + s)])
                    for kc in range(8):
                        mm(pg, wsu[:, 1024 + kc * 128:1024 + (kc + 1) * 128], HT[:, kc, 0:n], kc == 0, kc == 7, [wku] + HTk, [("B", 4 + s)])
                    nsg = segs * seglen
                    ae = AE[s][:, 0:segs * (seglen + 2)].rearrange("p (g t) -> p g t", g=segs)
                    cp("act", ae[:, :, 2:], pa[:, 0:nsg].rearrange("p (g t) -> p g t", g=segs), [("B", 2 + s)], [("AE", s)])
                    if stile:
                        cp("pool", ae[:, :, 0:2], SCV[:, fc, :, :], ["SCV", ("AE", s)], [("AE", s)])
                        ts("dve", carry[:, fc, :], pa[:, NS_T:NS_T + 2], hmask[:, 0:1], None, ALU.mult, None, [("B", 2 + s), "hmask"], ["carry"])
                        cp("pool", CVS[:, fc, :, :], ae[:, :, seglen:seglen + 2], [("AE", s)], ["CVS"])
                    else:
                        cp("pool", ae[:, :, 0:2], carry[:, fc, :].unsqueeze(1), ["carry", ("AE", s)], [("AE", s)])
                        cp("pool", carry[:, fc, :].unsqueeze(1), ae[:, :, seglen:seglen + 2], [("AE", s)], ["carry"])
                    cc_ = Cc[s][:, 0:nsg].rearrange("p (g t) -> p g t", g=segs)
                    ts("dve", cc_, ae[:, :, 2:], CVW[:, fc, 2:3], CVW[:, fc, 3:4], ALU.mult, ALU.add, [("AE", s), "CVW"], [("Cc", s)])
                    stt("dve", cc_, ae[:, :, 1:seglen + 1], CVW[:, fc, 1:2], cc_, ALU.mult, ALU.add, [("AE", s), "CVW", ("Cc", s)], [("Cc", s)])
                    stt("dve", cc_, ae[:, :, 0:seglen], CVW[:, fc, 0:1], cc_, ALU.mult, ALU.add, [("AE", s), "CVW", ("Cc", s)], [("Cc", s)])
                    act(Gl[s][:, 0:nsg], Cc[s][:, 0:nsg], AF.Gelu, [("Cc", s)], [("Gl", s)])
                    tt("dve", UT[:, fc, 0:nsg], pg[:, 0:nsg], Gl[s][:, 0:nsg], ALU.mult, [("B", 4 + s), ("Gl", s)], [("UT", fc)])
                UTk = [("UT", fc) for fc in range(NFC)]
                nblk4 = (ny + 127) // 128
                for tb in range(nblk4):
                    t0 = tb * 128
                    nt = min(128, ny - t0)
                    s = tb % 2
                    for hf in range(2):
                        ps = B[6 + hf][0:nt, :]
                        for fc in range(NFC):
                            mm(ps, UT[:, fc, t0:t0 + nt], WDN[:, fc, hf * 512:(hf + 1) * 512], fc == 0, fc == NFC - 1, UTk + ["WDN"], [("B", 6 + hf)])
                        stt("dve", Z[s][0:nt, hf * 512:(hf + 1) * 512], HTM[0:nt, tb, hf * 512:(hf + 1) * 512], ALPHA, ps, ALU.mult, ALU.add,
                            [("HTM", tb), ("B", 6 + hf)], [("Z", s)])
                    layer_norm(Z[s], nt, 2, Xr[s][0:nt], [("Z", s)], ("Xr", s), s)
                    dma("sp", y_o[ybase + t0:ybase + t0 + nt, :], Xr[s][0:nt], [("Xr", s)], [])
                ybase += ny
            dma("sp", cv_s, CVS[:], ["CVS"], [])
            dma("sp", cv_p, carry[:], ["carry"], [])

        with ExitStack() as fin:
            P.finalize(fin)
    return nc


def _prep(inputs):
    g = lambda k: np.asarray(inputs[k])
    xp, xs = g("x_prompt"), g("x_sample")
    Bp, T, _ = xp.shape
    Bs, TS, _ = xs.shape
    ck, cvv = g("cache_k"), g("cache_v")
    NPHYS, _, PG, H, _, DH = ck.shape
    pt = g("page_table")
    PAST = pt.shape[1] * PG
    NSB = Bs // 2
    cfg = dict(T=T, NSB=NSB, PAST=PAST, NPHYS=NPHYS)
    NG8 = PAST // 1024
    TT = T // 4
    NTILE = TT // 512
    NSUP = T // 512
    NCH = NSUP + 4
    NSL = NSB // 4
    NS_T = NSL * 4
    w_in = g("w_in")[0]
    f32 = np.float32
    half = DH // 2
    inv = (10000.0 ** (-np.arange(half, dtype=f32) * 2.0 / DH)).astype(f32)

    def rope_tab(pos):
        ang = pos.astype(f32)[:, None] * inv[None, :]
        c, s = np.cos(ang).astype(f32), np.sin(ang).astype(f32)
        return np.ascontiguousarray(np.stack([np.tile(c, (1, 4)), np.tile(s, (1, 4))], axis=1))
    rope_p = rope_tab(np.arange(T))
    rope_s = rope_tab(PAST + np.arange(TS))
    tri = np.triu(np.ones((128, 128), f32))
    ident = np.eye(128, dtype=f32)
    mnew = np.tile(np.triu(np.ones((4, 4), f32)), (1, 2))
    cmbp = np.zeros((8, 8), f32)
    cmbp[0:4, 0:4] = np.eye(4)
    cmbp[4:8, 4:8] = np.eye(4)
    a16 = (np.arange(128) % 16).astype(f32).reshape(128, 1)
    lamv = np.tile(np.concatenate([g("lambda_q1")[0], g("lambda_q2")[0], g("lambda_k1")[0], g("lambda_k2")[0]])[None, :], (128, 1)).astype(f32)
    ga_b = np.tile(g("subln_g")[0][None, :], (128, 1)).astype(f32)
    gn_b = np.tile(g("hgrn_norm_g")[0][None, :], (128, 1)).astype(f32)
    wgt = w_in[:, 3584:5632].reshape(8, 128, 16, 128).transpose(2, 1, 0, 3).reshape(16, 128, 1024)
    wa = g("w_branch_a")[0].reshape(4, 128, 8, 128).transpose(2, 1, 0, 3).reshape(8, 128, 512)
    wb = g("w_branch_b")[0].reshape(4, 128, 8, 128).transpose(2, 1, 0, 3).reshape(8, 128, 512)
    wup = g("w_up")[0].reshape(8, 128, 2, NFC, 128).transpose(3, 1, 2, 0, 4).reshape(NFC, 128, 2048)
    lnp = np.stack([g("ln1_g")[0], g("ln1_b")[0], g("ln2_g")[0], g("ln2_b")[0]], 0)
    lnp = np.ascontiguousarray(np.tile(lnp[None], (128, 1, 1))).astype(f32)
    cvw = np.concatenate([g("conv_w")[0], g("conv_b")], 0).reshape(4, NFC, 128).transpose(2, 1, 0)
    shared = dict(rope_p=rope_p, rope_s=rope_s, lamv=lamv, ga_b=ga_b, gn_b=gn_b, tri=tri, ident=ident, mnew=mnew,
                  cmbp=cmbp, a16=a16, wg_t=np.ascontiguousarray(wgt), wa_t=np.ascontiguousarray(wa),
                  wb_t=np.ascontiguousarray(wb), wout=np.ascontiguousarray(g("w_out")[0]),
                  wup_t=np.ascontiguousarray(wup), wdn=np.ascontiguousarray(g("w_down")[0]), lnp=lnp,
                  cvw=np.ascontiguousarray(cvw))
    cols = {"ka": 512, "qa": 0, "va": 1024, "ib": 2560, "gb": 3072, "qb": 1536, "fb": 2048}
    order = ["ka", "qa", "va", "ib", "gb", "qb", "fb"]
    sconv = g("state_conv")[:, 0]
    st_all = g("state_hgrn")[:, 0]
    lbl_all = g("lb_logits")
    in_maps = []
    for c in range(8):
        gI, h = c // 4, c % 4
        j = h
        m = dict(shared)
        m["xT_seq"] = np.ascontiguousarray(xp[gI].T)
        sb_ids = np.arange(gI * NSB, (gI + 1) * NSB)
        m["xsT"] = np.ascontiguousarray(xs[sb_ids].reshape(NSB * TS, D).T)
        m["wm"] = np.ascontiguousarray(np.concatenate([w_in[:, cols[k] + h * 128: cols[k] + (h + 1) * 128] for k in order], 1))
        m["lbl"] = np.ascontiguousarray(lbl_all[:, h * 128:(h + 1) * 128].T)
        ptb = pt[sb_ids].reshape(NSB, NG8, 8)
        m["ptr"] = np.ascontiguousarray(np.repeat(ptb.transpose(2, 0, 1), 16, axis=0).reshape(128, NSB * NG8)).astype(np.int32)
        m["ck"] = np.ascontiguousarray(ck[:, 0, :, h]).reshape(NPHYS * 16, 1024)
        m["cv"] = np.ascontiguousarray(cvv[:, 0, :, h]).reshape(NPHYS * 16, 1024)
        m["st_h"] = np.ascontiguousarray(st_all[sb_ids, h])
        tb_ids = sb_ids[j * NSL:(j + 1) * NSL]
        xs_t = xs[tb_ids].reshape(NS_T, D)
        p0 = j * TT
        halo = xp[gI, max(p0 - 2, 0):max(p0 - 2, 0) + 2]
        xcols = np.concatenate([xs_t, halo, xp[gI, p0:p0 + TT]], 0)
        m["x_t"] = np.ascontiguousarray(xcols)
        m["xT_t"] = np.ascontiguousarray(xcols.T)
        m["scv"] = np.ascontiguousarray(sconv[tb_ids].reshape(NSL, 2, NFC, 128).transpose(3, 2, 0, 1))
        m["hmask"] = np.full((128, 1), 0.0 if j == 0 else 1.0, f32)
        idx = np.zeros((128, NTILE + 2, 4, 2), np.int64)
        pr = np.arange(128)
        chunks = [NSUP + j, max(j * NTILE - 1, 0)] + [j * NTILE + k for k in range(NTILE)]
        for ci, chn in enumerate(chunks):
            for hh in range(4):
                for part in range(2):
                    idx[:, ci, hh, part] = (chn // 4) * 4096 + hh * 1024 + (chn % 4) * 256 + part * 128 + pr
        m["idx"] = np.ascontiguousarray(idx.reshape(128, -1)).astype(np.int32)
        in_maps.append({k: np.ascontiguousarray(v) for k, v in m.items()})
    return cfg, in_maps


_CACHE = {}


def kernel(**inputs):
    cfg, in_maps = _prep(inputs)
    key = tuple(sorted(cfg.items()))
    if key not in _CACHE:
        _CACHE[key] = build(cfg)
    nc = _CACHE[key]
    res = run_bass_kernel_spmd(nc, in_maps, core_ids=list(range(8))).results
    T, NSB = cfg["T"], cfg["NSB"]
    TT = T // 4
    NSL = NSB // 4
    NS_T = NSL * 4
    f32 = np.float32
    Bs = NSB * 2
    y_p = np.zeros((2, T, D), f32); y_s = np.zeros((Bs, 4, D), f32)
    k_p = np.zeros((2, 1, T, 4, 2, 64), f32); v_p = np.zeros((2, 1, T, 4, 128), f32)
    h_p = np.zeros((2, 1, 4, 128, 128), f32); c_p = np.zeros((2, 1, 2, DFF), f32)
    k_s = np.zeros((Bs, 1, 4, 4, 2, 64), f32); v_s = np.zeros((Bs, 1, 4, 4, 128), f32)
    h_s = np.zeros((Bs, 1, 4, 128, 128), f32); c_s = np.zeros((Bs, 1, 2, DFF), f32)
    for c in range(8):
        r = res[c]
        gI, h = c // 4, c % 4
        j = h
        k_p[gI, 0, :, h] = r["k_p"].reshape(T, 2, 64)
        v_p[gI, 0, :, h] = r["v_p"]
        h_p[gI, 0, h] = r["S_p"]
        sl = slice(gI * NSB, (gI + 1) * NSB)
        k_s[sl, 0, :, h] = r["k_s"].reshape(NSB, 4, 2, 64)
        v_s[sl, 0, :, h] = r["v_s"].reshape(NSB, 4, 128)
        h_s[sl, 0, h] = r["S_s"]
        tb = slice(gI * NSB + j * NSL, gI * NSB + (j + 1) * NSL)
        y_s[tb] = r["y_o"][0:NS_T].reshape(NSL, 4, D)
        y_p[gI, j * TT:(j + 1) * TT] = r["y_o"][NS_T:]
        c_s[tb, 0] = r["cv_s"].transpose(2, 3, 1, 0).reshape(NSL, 2, DFF)
        if j == 3:
            c_p[gI, 0] = r["cv_p"].transpose(2, 1, 0).reshape(2, DFF)
    return (y_p, y_s, k_p, v_p, h_p, c_p, k_s, v_s, h_s, c_s)
```

```python
import math
import os
from contextlib import ExitStack
import numpy as np
import concourse.bass as bass
import concourse.mybir as mybir
from concourse.bass_utils import run_bass_kernel_spmd

F32 = mybir.dt.float32
BF16 = mybir.dt.bfloat16
I32 = mybir.dt.int32
ALU = mybir.AluOpType
AF = mybir.ActivationFunctionType
AX = mybir.AxisListType

D = 1024
DFF = 2816
NFC = DFF // 128
LN_EPS = 1e-5
RMS_EPS = 1e-5
ALPHA = 2.0 ** 0.25
LAM_INIT = 0.8 - 0.6 * math.exp(0.0)
SCALE = 64 ** -0.5
RING = 12


class Op:
    __slots__ = ("eng", "fn", "dma", "deps", "needs_inc", "sem", "val", "idx", "cc", "bg")

    def __init__(self, eng, fn, dma=False, cc=False):
        self.eng, self.fn, self.dma, self.cc = eng, fn, dma, cc
        self.deps = []
        self.needs_inc = dma or cc
        self.sem = None
        self.val = 0
        self.bg = False


class Prog:
    ENGS = ("pe", "act", "dve", "pool", "sp")

    def __init__(self, nc):
        self.nc = nc
        self.ops = {e: [] for e in self.ENGS}
        self.last_w = {}
        self.readers = {}
        self.all_dma = []
        self.cut = False

    def op(self, eng, fn, r=(), w=(), dma=False, cc=False, extra=()):
        o = Op(eng, fn, dma, cc)
        if self.cut:
            return o
        deps = set(extra)
        for k in r:
            d = self.last_w.get(k)
            if d is not None:
                deps.add(d)
        for k in w:
            d = self.last_w.get(k)
            if d is not None:
                deps.add(d)
            for rd in self.readers.get(k, ()):
                deps.add(rd)
        async_o = dma or cc
        for d in deps:
            if d is o:
                continue
            if d.eng == "pe" and eng == "pe" and not async_o:
                continue
            o.deps.append(d)
            d.needs_inc = True
        for k in w:
            self.last_w[k] = o
            self.readers[k] = []
        for k in r:
            lst = self.readers.setdefault(k, [])
            if not async_o:
                lst[:] = [x for x in lst if x.eng != eng or x.dma or x.cc]
            lst.append(o)
        self.ops[eng].append(o)
        if async_o:
            self.all_dma.append(o)
        return o

    def barrier(self, include_bg=False):
        if self.cut:
            return
        lasts = []
        for e in self.ENGS:
            for o_ in reversed(self.ops[e]):
                if include_bg or not o_.bg:
                    lasts.append(o_)
                    break
        dm = [d for d in self.all_dma if include_bg or not getattr(d, "bg", False)]
        self.all_dma = [d for d in self.all_dma if not (include_bg or not getattr(d, "bg", False))]
        for e in self.ENGS:
            self.op(e, None, extra=lasts + dm)
        self.last_w = {}
        self.readers = {}

    def finalize(self, stack):
        nc = self.nc
        esem = {e: stack.enter_context(nc.semaphore("es_" + e)) for e in self.ENGS}
        rings = {e: [stack.enter_context(nc.semaphore("r%s%d" % (e, i))) for i in range(RING)]
                 for e in ("sp", "pool")}
        ccsem = stack.enter_context(nc.semaphore("ccsem"))
        cnt = {e: 0 for e in self.ENGS}
        dcnt = {"sp": 0, "pool": 0}
        cccnt = 0
        for e in self.ENGS:
            for o in self.ops[e]:
                if o.cc:
                    cccnt += 1
                    o.sem, o.val = ccsem, cccnt
                elif o.dma:
                    i = dcnt[e]
                    dcnt[e] += 1
                    o.idx = i
                    o.sem, o.val = rings[e][i % RING], 16 * (i // RING + 1)
                elif o.needs_inc and o.fn is not None:
                    cnt[e] += 1
                    o.sem, o.val = esem[e], cnt[e]
        block = stack.enter_context(nc.Block())

        def run(e, eng):
            waited = {}
            for o in self.ops[e]:
                ws = []
                for d in o.deps:
                    if d.sem is None:
                        continue
                    ws.append((d.sem, d.val))
                if o.dma and o.idx >= RING:
                    ws.append((o.sem, o.val - 16))
                for s, v in ws:
                    key = id(s)
                    if waited.get(key, 0) >= v:
                        continue
                    waited[key] = v
                    eng.wait_ge(s, v)
                if o.fn is None:
                    continue
                ins = o.fn(eng)
                if o.cc:
                    ins.then_inc(o.sem)
                elif o.dma:
                    ins.then_inc(o.sem, 16)
                elif o.sem is not None:
                    ins.then_inc(o.sem, 1)
            if e in ("sp", "pool"):
                for i in range(min(RING, dcnt[e])):
                    last = ((dcnt[e] - 1 - i) // RING) * RING + i
                    eng.wait_ge(rings[e][i], 16 * (last // RING + 1))
            if e == "pool" and cccnt:
                eng.wait_ge(ccsem, cccnt)

        @block.tensor
        def _(eng):
            run("pe", eng)

        @block.scalar
        def _(eng):
            run("act", eng)

        @block.vector
        def _(eng):
            run("dve", eng)

        @block.gpsimd
        def _(eng):
            run("pool", eng)

        @block.sync
        def _(eng):
            run("sp", eng)


def build(cfg):
    T, NSB, PAST, NPHYS = cfg["T"], cfg["NSB"], cfg["PAST"], cfg["NPHYS"]
    NBLK = T // 128
    NSUP = T // 512
    NG8 = PAST // 1024
    NKT = NG8 * 8
    TT = T // 4
    NTILE = TT // 512
    NSL = NSB // 4
    NS_T = NSL * 4
    NCH = NSUP + 4
    NCOL = NS_T + 2 + TT
    nc = bass.Bass("TRN2", target_bir_lowering=False)
    P = Prog(nc)
    STOP = int(os.environ.get("KSTOP", "99"))

    def din(name, shape, dt=F32):
        return nc.dram_tensor(name, list(shape), dt, kind="ExternalInput").ap()

    def dout(name, shape, dt=F32):
        return nc.dram_tensor(name, list(shape), dt, kind="ExternalOutput").ap()

    xT_seq = din("xT_seq", [D, T])
    xsT = din("xsT", [D, NSB * 4])
    wm = din("wm", [D, 896])
    rope_p = din("rope_p", [T, 2, 128])
    rope_s = din("rope_s", [4, 2, 128])
    lbl = din("lbl", [128, 2])
    lamv = din("lamv", [128, 256])
    ga_b = din("ga_b", [128, 128])
    gn_b = din("gn_b", [128, 128])
    tri_d = din("tri", [128, 128])
    ident_d = din("ident", [128, 128])
    mnew_d = din("mnew", [4, 8])
    cmbp_d = din("cmbp", [8, 8])
    a16_d = din("a16", [128, 1])
    ptr_d = din("ptr", [128, NSB * NG8], I32)
    ck = din("ck", [NPHYS * 16, 1024])
    cv = din("cv", [NPHYS * 16, 1024])
    st_h = din("st_h", [NSB, 128, 128])
    xT_t = din("xT_t", [D, NCOL])
    x_t = din("x_t", [NCOL, D])
    wg_d = din("wg_t", [16, 128, 8 * 128])
    wa_d = din("wa_t", [8, 128, 4 * 128])
    wb_d = din("wb_t", [8, 128, 4 * 128])
    wout_d = din("wout", [D, D])
    wup_d = din("wup_t", [NFC, 128, 2 * 8 * 128])
    wdn_d = din("wdn", [DFF, D])
    lnp_d = din("lnp", [128, 4, D])
    cvw_d = din("cvw", [128, NFC, 4])
    scv_d = din("scv", [128, NFC, NSL, 2])
    hmask_d = din("hmask", [128, 1])
    idx_d = din("idx", [128, (NTILE + 2) * 8], I32)
    k_p = dout("k_p", [T, 128])
    v_p = dout("v_p", [T, 128])
    k_s = dout("k_s", [NSB * 4, 128])
    v_s = dout("v_s", [NSB * 4, 128])
    S_p = dout("S_p", [128, 128])
    S_s = dout("S_s", [NSB, 128, 128])
    y_o = dout("y_o", [NS_T + TT, D])
    cv_s = dout("cv_s", [128, NFC, NSL, 2])
    cv_p = dout("cv_p", [128, NFC, 2])
    xTb = nc.dram_tensor("xTb", [D, T], BF16)
    snd = nc.dram_tensor("snd", [NCH * 256, 512], BF16)
    gat = nc.dram_tensor("gat", [4 * NCH * 256, 512], BF16)
    wgb = nc.dram_tensor("wgb", [16 * 128, 1024], BF16)
    wab = nc.dram_tensor("wab", [8 * 128, 512], BF16)
    wbb = nc.dram_tensor("wbb", [8 * 128, 512], BF16)
    wupb = nc.dram_tensor("wupb", [NFC * 128, 2048], BF16)
    woutb = nc.dram_tensor("woutb", [D, D], BF16)
    wdnb = nc.dram_tensor("wdnb", [DFF, D], BF16)

    with ExitStack() as top:
        def sb(name, shape, dt=F32, st=top):
            return st.enter_context(nc.sbuf_tensor("s_" + name, list(shape), dt))

        banks = [top.enter_context(nc.psum_tensor("ps%d" % i, [128, 512], F32)) for i in range(8)]

        ident = sb("ident", [128, 128], BF16)
        tri = sb("tri", [128, 128], F32)
        ones = sb("ones", [128, 128], F32)
        trib = sb("trib", [128, 128], BF16)
        mhalf = sb("mhalf", [128, 1], F32)
        lam_t = sb("lam_t", [128, 4], F32)

        def mm(out, lhsT, rhs, start, stop, r, w):
            return P.op("pe", lambda e: e.matmul(out, lhsT, rhs, start=start, stop=stop), r=r, w=w)

        def tp(out, in_, r, w):
            n = in_.shape[0]
            return P.op("pe", lambda e: e.transpose(out, in_, ident[0:n, 0:n]), r=list(r) + ["ident"], w=w)

        def act(out, in_, func, r, w, bias=None, scale=None, accum=None):
            kw = {}
            if bias is not None:
                kw["bias"] = bias
            if scale is not None:
                kw["scale"] = scale
            if accum is not None:
                kw["accum_out"] = accum
            return P.op("act", lambda e: e.activation(out, in_, func, **kw), r=r, w=w)

        def tt(eng, out, a, b, op, r, w):
            return P.op(eng, lambda e: e.tensor_tensor(out, a, b, op), r=r, w=w)

        def ts(eng, out, a, s1, s2, op0, op1, r, w):
            if op1 is None:
                return P.op(eng, lambda e: e.tensor_scalar(out, a, s1, None, op0), r=r, w=w)
            return P.op(eng, lambda e: e.tensor_scalar(out, a, s1, s2, op0, op1), r=r, w=w)

        def stt(eng, out, a, s, b, op0, op1, r, w):
            return P.op(eng, lambda e: e.scalar_tensor_tensor(out, a, s, b, op0, op1), r=r, w=w)

        def cp(eng, out, in_, r, w):
            if eng == "act":
                return P.op("act", lambda e: e.copy(out, in_), r=r, w=w)
            return P.op(eng, lambda e: e.tensor_copy(out, in_), r=r, w=w)

        def dma(q, out, in_, r, w):
            return P.op(q, lambda e: e.dma_start(out=out, in_=in_), r=r, w=w, dma=True)

        identf = sb("identf", [128, 128], F32)
        dma("sp", identf[:], ident_d, [], ["identf"])
        cp("dve", ident[:], identf[:], ["identf"], ["ident"])
        dma("sp", tri[:], tri_d, [], ["tri"])
        cp("dve", trib[:], tri[:], ["tri"], ["trib"])
        P.op("pool", lambda e: e.memset(ones[:], 1.0), w=["ones"])
        P.op("pool", lambda e: e.memset(mhalf[:], -0.5), w=["mhalf"])
        lamt = sb("lamt", [128, 256], F32)
        dma("sp", lamt[:], lamv, [], ["lamt"])
        lj = sb("lj", [128, 128], F32)
        ls = sb("ls", [128, 4], F32)
        tt("dve", lj[:], lamt[:, 0:128], lamt[:, 128:256], ALU.mult, ["lamt"], ["lj"])
        P.op("dve", lambda e: e.tensor_reduce(ls[:, 0:2], lj[:].rearrange("p (a b) -> p a b", a=2), AX.X, ALU.add),
             r=["lj"], w=["ls"])
        act(ls[:, 2:4], ls[:, 0:2], AF.Exp, ["ls"], ["ls2"])
        tt("dve", lam_t[:, 0:1], ls[:, 2:3], ls[:, 3:4], ALU.subtract, ["ls2"], ["lam0"])
        ts("dve", lam_t[:, 0:1], lam_t[:, 0:1], LAM_INIT, None, ALU.add, None, ["lam0"], ["lam0"])
        ts("dve", lam_t[:, 1:2], lam_t[:, 0:1], -1.0, None, ALU.mult, None, ["lam0"], ["lam"])

        for i in range(0, T, 2048):
            w_ = min(2048, T - i)
            P.op("pool", lambda e, i=i, w_=w_: e.dma_start(out=xTb.ap()[:, i:i + w_], in_=xT_seq[:, i:i + w_]),
                 w=[("xTb", i // 512 + j) for j in range(w_ // 512)], dma=True)

        def precast(dst, src2d, rows):
            for r0 in range(0, rows, 512):
                r1 = min(rows, r0 + 512)
                o_ = P.op("pool", lambda e, r0=r0, r1=r1: e.dma_start(out=dst.ap()[r0:r1, :], in_=src2d[r0:r1, :]), dma=True)
                o_.bg = True
        with ExitStack() as ph:
            def sbp(name, shape, dt=F32):
                return sb(name, shape, dt, ph)
            OA = sbp("OA", [128, NCH, 512], BF16)
            OB = sbp("OB", [128, NCH, 512], BF16)
            WM = sbp("WM", [128, 8, 896], BF16)
            P.op("pool", lambda e: e.dma_start(out=WM[:], in_=wm.rearrange("(k p) c -> p k c", p=128)), w=["WM"], dma=True)
            gab = sbp("gab", [128, 128])
            gnb = sbp("gnb", [128, 128])
            dma("sp", gab[:], ga_b, [], ["gab"])
            dma("sp", gnb[:], gn_b, [], ["gnb"])
            P.op("act", lambda e: e.mul(gab[:], gab[:], 1.0 - LAM_INIT), r=["gab"], w=["gab"])
            lbt = sbp("lbt", [128, 2])
            dma("sp", lbt[:], lbl, [], ["lbt"])
            lbv = sbp("lbv", [128, 2])
            tt("dve", lbv[:, 0:1], lbt[:, 0:1], lbt[:, 1:2], ALU.subtract, ["lbt"], ["lbv0"])
            act(lbv[:, 0:1], lbv[:, 0:1], AF.Sigmoid, ["lbv0"], ["lbv0"])
            ts("dve", lbv[:, 1:2], lbv[:, 0:1], -1.0, 1.0, ALU.mult, ALU.add, ["lbv0"], ["lbv"])
            Sf = sbp("Sf", [128, 128])
            Sb = sbp("Sb", [128, 128], BF16)
            XS = sbp("XS", [128, 8, NSB * 4], BF16)
            NR = 3
            ring_names = {}

            def rt(name, shape, dt=F32):
                tl = [sbp("%s_%d" % (name, i), shape, dt) for i in range(NR)]
                ring_names[name] = tl
                return tl
            CS = rt("CS", [128, 2, 128])
            T1 = rt("T1", [128, 4, 32]); T2 = rt("T2", [128, 4, 32])
            ROT = rt("ROT", [128, 256]); ROTb = rt("ROTb", [128, 256], BF16)
            Vf = rt("Vf", [128, 128]); Ib = rt("Ib", [128, 128], BF16)
            Gs = rt("Gs", [128, 128]); Gb = rt("Gb", [128, 128], BF16)
            sg = rt("sg", [128, 128]); lf = rt("lf", [128, 128]); kk = rt("kk", [128, 128])
            bT = rt("bT", [128, 128]); eb = rt("eb", [128, 128]); enb = rt("enb", [128, 128]); ehb = rt("ehb", [128, 128])
            qt = rt("qt", [128, 128], BF16); kt_ = rt("kt", [128, 128], BF16); kh = rt("kh", [128, 128], BF16)
            khT = rt("khT", [128, 128], BF16); attb = rt("attb", [128, 128], BF16)
            dec = rt("dec", [128, 2]); sq = rt("sq", [128, 128]); st4 = rt("st4", [128, 4])
            obb = rt("obb", [128, 128], BF16)
            qbS = rt("qbS", [128, 128])

            ph1 = ExitStack()
            ph1.__enter__()
            def sb1(name, shape, dt=F32):
                return sb(name, shape, dt, ph1)
            KT = sb1("KT", [128, T], BF16)
            QT = sb1("QT", [128, T], BF16)
            VA = sb1("VA", [128, NBLK, 132], BF16)
            P.op("pool", lambda e: e.memset(VA[:], 1.0), w=["VAinit"])
            XT = [sb1("XT%d" % i, [128, 8, 512], BF16) for i in range(2)]
            SUPF = [[sb1("sup%s%d" % (nm, i), [128, 512]) for nm in ("q", "g", "l", "k", "b", "e", "n")] for i in range(2)]
            SUPB = [[sb1("supb%s%d" % (nm, i), [128, 512], BF16) for nm in ("q", "k")] for i in range(2)]
            B = banks
            B5b = B[5][:].bitcast(BF16)

            def mixer_block(n, bi, xcols, fmq, fmf, fm_keys, rope_src, k_dst, v_dst, KTd, QTd, Vd, OBd, tag, pre=None):
                s = bi % NR
                K = lambda nm: (nm, s)
                xk = fm_keys
                tmA = B[0][0:n, :]
                tmB = B[1][0:n, 0:128]
                for kc in range(8):
                    mm(tmA, xcols(kc), WM[:, kc, 0:512], kc == 0, kc == 7, xk + ["WM"], [("B",0)])
                for kc in range(8):
                    mm(tmB, xcols(kc), WM[:, kc, 512:640], kc == 0, kc == 7, xk + ["WM"], [("B",1)])
                FMFK = ("B", 4)
                if fmq is None:
                    FMFK = ("B", 3)
                    fmq = B[3][:, 0:n]
                    fmf = B[3][:, 8:8 + n]
                    for kc in range(8):
                        mm(fmq, WM[:, kc, 640:768], xcols(kc), kc == 0, kc == 7, xk + ["WM"], [("B",3)])
                    for kc in range(8):
                        mm(fmf, WM[:, kc, 768:896], xcols(kc), kc == 0, kc == 7, xk + ["WM"], [FMFK])
                cs = CS[s]
                dma("sp", cs[0:n], rope_src, [], [K("CS")])
                xa = tmA[:, 0:256].rearrange("p (g h d) -> p g h d", g=4, h=2)
                x1, x2 = xa[:, :, 0, :], xa[:, :, 1, :]
                cosv = cs[0:n, 0, :].rearrange("p (g d) -> p g d", g=4)
                sinv = cs[0:n, 1, :].rearrange("p (g d) -> p g d", g=4)
                rot = ROT[s][0:n].rearrange("p (g h d) -> p g h d", g=4, h=2)
                t1, t2 = T1[s][0:n], T2[s][0:n]
                tt("dve", t1, x1, cosv, ALU.mult, [("B",0), K("CS")], [K("T1")])
                tt("dve", t2, x2, sinv, ALU.mult, [("B",0), K("CS")], [K("T2")])
                tt("dve", rot[:, :, 0, :], t1, t2, ALU.subtract, [K("T1"), K("T2")], [K("ROT0")])
                tt("dve", t1, x2, cosv, ALU.mult, [("B",0), K("CS"), K("ROT0")], [K("T1")])
                tt("dve", t2, x1, sinv, ALU.mult, [("B",0), K("CS"), K("ROT0")], [K("T2")])
                tt("dve", rot[:, :, 1, :], t1, t2, ALU.add, [K("T1"), K("T2")], [K("ROT1")])
                RK = [K("ROT0"), K("ROT1")]
                dma("sp", k_dst, ROT[s][0:n, 0:128], RK, [])
                cp("act", ROTb[s][0:n], ROT[s][0:n], RK, [K("ROTb")])
                tp(B5b[:, 0:n], ROTb[s][0:n, 0:128], [K("ROTb")], [("B",5)])
                tp(B5b[:, 128:128 + n], ROTb[s][0:n, 128:256], [K("ROTb")], [("B",5)])
                cp("act", KTd, B5b[:, 0:n], [("B",5)], [("KT", tag)])
                cp("act", QTd, B5b[:, 128:128 + n], [("B",5)], [("QT", tag)])
                cp("act", Vf[s][0:n], tmA[:, 256:384], [("B",0)], [K("Vf")])
                dma("sp", v_dst, Vf[s][0:n], [K("Vf")], [])
                cp("pool", Vd, Vf[s][0:n], [K("Vf"), "VAinit"], [("V", tag)])
                cp("act", Ib[s][0:n], tmA[:, 384:512], [("B",0)], [K("Ib")])
                act(Gs[s][0:n], tmB, AF.Exp, [("B",1)], [K("Gs")], scale=-1.0)
                ts("dve", Gs[s][0:n], Gs[s][0:n], 1.0, None, ALU.add, None, [K("Gs")], [K("Gs")])
                P.op("dve", lambda e: e.reciprocal(Gs[s][0:n], Gs[s][0:n]), r=[K("Gs")], w=[K("Gs")])
                tt("dve", Gs[s][0:n], tmB, Gs[s][0:n], ALU.mult, [("B",1), K("Gs")], [K("Gs")])
                tt("pool", Gb[s][0:n], Gs[s][0:n], gnb[0:n], ALU.mult, [K("Gs"), "gnb"], [K("Gb")])
                if pre is None:
                    cp("act", qbS[s][:, 0:n], fmq, [("B",3)], [K("qbS")])
                    act(sg[s][:, 0:n], fmf, AF.Sigmoid, [FMFK], [K("sg")])
                    ts("dve", sg[s][:, 0:n], sg[s][:, 0:n], lbv[:, 1:2], lbv[:, 0:1], ALU.mult, ALU.add, [K("sg"), "lbv", "lbv0"], [K("sg")])
                    act(lf[s][:, 0:n], sg[s][:, 0:n], AF.Ln, [K("sg")], [K("lf")])
                    ts("pool", kk[s][:, 0:n], sg[s][:, 0:n], -1.0, 1.0, ALU.mult, ALU.add, [K("sg")], [K("kk")])
                    P.op("dve", lambda e: e.tensor_tensor_scan(bT[s][:, 0:n], ones[:, 0:n], lf[s][:, 0:n], 0.0, ALU.mult, ALU.add),
                         r=[K("lf"), "ones"], w=[K("bT")])
                    act(eb[s][:, 0:n], bT[s][:, 0:n], AF.Exp, [K("bT")], [K("eb")])
                    tt("dve", qt[s][:, 0:n], qbS[s][:, 0:n], eb[s][:, 0:n], ALU.mult, [K("qbS"), K("eb")], [K("qt")])
                    act(enb[s][:, 0:n], bT[s][:, 0:n], AF.Exp, [K("bT")], [K("enb")], scale=-1.0)
                    tt("pool", kt_[s][:, 0:n], kk[s][:, 0:n], enb[s][:, 0:n], ALU.mult, [K("kk"), K("enb")], [K("kt")])
                    qt_ap, kt_ap, kk_ap, bT_ap = qt[s][:, 0:n], kt_[s][:, 0:n], kk[s][:, 0:n], bT[s][:, 0:n]
                    kq, kkt, kkk, kbt = K("qt"), K("kt"), K("kk"), K("bT")
                else:
                    qt_ap, kt_ap, kk_ap, bT_ap, pk = pre
                    kq, kkt, kkk, kbt = pk
                act(ehb[s][:, 0:n], bT_ap, AF.Exp, [kbt], [K("ehb")], scale=-1.0, bias=bT_ap[:, n - 1:n])
                tt("pool", kh[s][:, 0:n], kk_ap, ehb[s][:, 0:n], ALU.mult, [kkk, K("ehb")], [K("kh")])
                act(dec[s][:, 0:1], bT_ap[:, n - 1:n], AF.Exp, [kbt], [K("dec")])
                tp(B5b[0:n, 256:384], kh[s][:, 0:n], [K("kh")], [("B",5)])
                cp("act", khT[s][0:n], B5b[0:n, 256:384], [("B",5)], [K("khT")])
                attT = B[1][0:n, 128:128 + n]
                mm(attT, kt_ap, qt_ap, True, True, [kkt, kq], [("B",1)])
                tt("dve", attb[s][0:n, 0:n], attT, tri[0:n, 0:n], ALU.mult, [("B",1), "tri"], [K("attb")])
                def late():
                    o_ps = B[2][0:n, 0:128]
                    mm(o_ps, qt_ap, Sb[:], True, False, [kq, "Sb"], [("B",2)])
                    mm(o_ps, attb[s][0:n, 0:n], Ib[s][0:n], False, True, [K("attb"), K("Ib")], [("B",2)])
                    U = B[7][:, 256:384]
                    mm(U, khT[s][0:n], Ib[s][0:n], True, True, [K("khT"), K("Ib")], [("B",7)])
                    stt("dve", Sf[:], Sf[:], dec[s][:, 0:1], U, ALU.mult, ALU.add, ["Sf", K("dec"), ("B",7)], ["Sf"])
                    cp("pool", Sb[:], Sf[:], ["Sf"], ["Sb"])
                    act(sq[s][0:n], o_ps, AF.Square, [("B",2)], [K("sq"), K("ss")], accum=st4[s][0:n, 0:1])
                    ts("dve", st4[s][0:n, 1:2], st4[s][0:n, 0:1], 1.0 / 128, RMS_EPS, ALU.mult, ALU.add, [K("ss")], [K("ms")])
                    tt("pool", st4[s][0:n, 3:4], st4[s][0:n, 1:2], mhalf[0:n, :], ALU.pow, [K("ms"), "mhalf"], [K("rstd")])
                    stt("dve", obb[s][0:n], o_ps, st4[s][0:n, 3:4], Gb[s][0:n], ALU.mult, ALU.mult, [("B",2), K("rstd"), K("Gb")], [K("obb")])
                    B6b_ = B[6][:].bitcast(BF16)
                    tp(B6b_[:, 384:384 + n], obb[s][0:n], [K("obb")], [("B",6)])
                    cp("act", OBd, B6b_[:, 384:384 + n], [("B",6)], [("OBs", tag)])
                return late

            if STOP <= 1:
                P.cut = True
            P.op("pool", lambda e: e.memset(Sf[:], 0.0), w=["Sf"])
            P.op("pool", lambda e: e.memset(Sb[:], 0.0), w=["Sb"])
            pend = [None]
            for su in range(NSUP):
                xt = XT[su % 2]
                xkey = ("XT", su % 2)
                dma("sp", xt[:], xTb.ap()[:, su * 512:(su + 1) * 512].rearrange("(k p) n -> p k n", p=128),
                    [("xTb", su)], [xkey])
                fmq, fmf = B[3][:, :], B[4][:, :]
                for kc in range(8):
                    mm(fmq, WM[:, kc, 640:768], xt[:, kc, :], kc == 0, kc == 7, [xkey, "WM"], [("B",3)])
                for kc in range(8):
                    mm(fmf, WM[:, kc, 768:896], xt[:, kc, :], kc == 0, kc == 7, [xkey, "WM"], [("B", 4)])
                p2 = su % 2
                sq_, sg_, sl_, sk_, sbt, se_, sn_ = SUPF[p2]
                sqt, skt = SUPB[p2]
                SK = ("sup", p2)
                cp("act", sq_[:], fmq, [("B", 3)], [("supq", p2)])
                act(sg_[:], fmf, AF.Sigmoid, [("B", 4)], [("supg", p2)])
                ts("dve", sg_[:], sg_[:], lbv[:, 1:2], lbv[:, 0:1], ALU.mult, ALU.add, [("supg", p2), "lbv", "lbv0"], [("supg", p2)])
                act(sl_[:], sg_[:], AF.Ln, [("supg", p2)], [("supl", p2)])
                ts("pool", sk_[:], sg_[:], -1.0, 1.0, ALU.mult, ALU.add, [("supg", p2)], [("supk", p2)])
                for j in range(4):
                    P.op("dve", lambda e, j=j, sbt=sbt, sl_=sl_: e.tensor_tensor_scan(sbt[:, j * 128:(j + 1) * 128], ones[:, 0:128], sl_[:, j * 128:(j + 1) * 128], 0.0, ALU.mult, ALU.add),
                         r=[("supl", p2), "ones"], w=[("supb", p2, j)])
                sbk = [("supb", p2, j) for j in range(4)]
                act(se_[:], sbt[:], AF.Exp, sbk, [("supe", p2)])
                tt("dve", sqt[:], sq_[:], se_[:], ALU.mult, [("supq", p2), ("supe", p2)], [("supqt", p2)])
                act(sn_[:], sbt[:], AF.Exp, sbk, [("supn", p2)], scale=-1.0)
                tt("pool", skt[:], sk_[:], sn_[:], ALU.mult, [("supk", p2), ("supn", p2)], [("supkt", p2)])
                for j in range(4):
                    blk = su * 4 + j
                    c0 = blk * 128
                    lt = mixer_block(128, blk, lambda kc, xt=xt, j=j: xt[:, kc, j * 128:(j + 1) * 128],
                                     fmq[:, j * 128:(j + 1) * 128], fmf[:, j * 128:(j + 1) * 128], [xkey],
                                     rope_p[c0:c0 + 128], k_p[c0:c0 + 128, :], v_p[c0:c0 + 128, :],
                                     KT[:, c0:c0 + 128], QT[:, c0:c0 + 128], VA[:, blk, 0:128],
                                     OB[:, blk // 4, (blk % 4) * 128:(blk % 4 + 1) * 128], blk,
                                     pre=(sqt[:, j * 128:(j + 1) * 128], skt[:, j * 128:(j + 1) * 128], sk_[:, j * 128:(j + 1) * 128],
                                          sbt[:, j * 128:(j + 1) * 128],
                                          (("supqt", p2), ("supkt", p2), ("supk", p2), ("supb", p2, j))))
                    if os.environ.get("KOLD") == "1":
                        lt()
                        lt = None
                    if pend[0] is not None:
                        pend[0]()
                    pend[0] = lt
            if pend[0] is not None:
                pend[0]()
            precast(wgb, wg_d.rearrange("a p c -> (a p) c"), 16 * 128)
            precast(wab, wa_d.rearrange("a p c -> (a p) c"), 8 * 128)
            precast(wbb, wb_d.rearrange("a p c -> (a p) c"), 8 * 128)
            precast(woutb, wout_d, D)
            precast(wupb, wup_d.rearrange("a p c -> (a p) c"), NFC * 128)
            precast(wdnb, wdn_d, DFF)
            dma("sp", S_p, Sf[:], ["Sf"], [])
            P.barrier()

            if STOP <= 2:
                P.cut = True
            PT = [[sb1("PT%d_%d" % (i, m), [128, 512], BF16) for m in range(2)] for i in range(2)]
            ep = [sb1("ep%d" % i, [128, 8]) for i in range(NR)]
            o0 = [sb1("o0%d" % i, [128, 128]) for i in range(NR)]; o1 = [sb1("o1%d" % i, [128, 128]) for i in range(NR)]; oab = [sb1("oab%d" % i, [128, 128], BF16) for i in range(NR)]
            groups = []
            gi = 0
            for qb in range(NBLK):
                nkb = qb + 1
                for j0 in range(0, nkb, 4):
                    groups.append((qb, j0, min(4, nkb - j0), gi % 2))
                    gi += 1

            def accs(qb):
                base = 4 + 2 * (qb % 2)
                return [B[base][:, 0:129], B[base + 1][:, 0:129]], base

            def emit_qk(g):
                qb, j0, nj, st_ = g
                q0 = qb * 128
                nkb = qb + 1
                for m in range(2):
                    stb = B[st_ * 2 + m]
                    for jj in range(nj):
                        j = j0 + jj
                        mm(stb[:, jj * 128:(jj + 1) * 128], KT[64 * m:64 * m + 64, j * 128:(j + 1) * 128],
                           QT[64 * m:64 * m + 64, q0:q0 + 128], True, True, [("KT", j), ("QT", qb)], [("B", st_ * 2 + m)])
                    act(PT[st_][m][:, 0:nj * 128], stb[:, 0:nj * 128], AF.Exp, [("B", st_ * 2 + m)], [("PT", st_, m)], scale=SCALE)
                    if j0 + nj == nkb:
                        dcol = (nj - 1) * 128
                        tt("pool", PT[st_][m][:, dcol:dcol + 128], PT[st_][m][:, dcol:dcol + 128], trib[:], ALU.mult,
                           [("PT", st_, m), "trib"], [("PT", st_, m)])

            def emit_pv(g):
                qb, j0, nj, st_ = g
                nkb = qb + 1
                acc, base = accs(qb)
                for jj in range(nj):
                    j = j0 + jj
                    for m in range(2):
                        mm(acc[m], PT[st_][m][:, jj * 128:(jj + 1) * 128], VA[:, j, 0:129], j == 0, j == nkb - 1,
                           [("PT", st_, m), ("V", j), "VAinit"], [("B", base + m)])
                if j0 + nj != nkb:
                    return
                s = qb % NR
                K = lambda nm: (nm + "_e", s)
                e_ = ep[s]
                P.op("dve", lambda e, e_=e_: e.reciprocal(e_[:, 0:1], acc[0][:, 128:129]), r=[("B", base)], w=[K("rl0")])
                P.op("dve", lambda e, e_=e_: e.reciprocal(e_[:, 1:2], acc[1][:, 128:129]), r=[("B", base + 1)], w=[K("rl1")])
                tt("dve", e_[:, 2:3], e_[:, 1:2], lam_t[:, 1:2], ALU.mult, [K("rl1"), "lam"], [K("nl1")])
                ts("dve", o0[s][:], acc[0][:, 0:128], e_[:, 0:1], None, ALU.mult, None, [("B", base), K("rl0")], [K("o0")])
                stt("dve", o1[s][:], acc[1][:, 0:128], e_[:, 2:3], o0[s][:], ALU.mult, ALU.add, [("B", base + 1), K("nl1"), K("o0")], [K("o1")])
                P.op("dve", lambda e, e_=e_, s=s: e.scalar_tensor_tensor(o0[s][:], o1[s][:], 1.0, o1[s][:], ALU.mult, ALU.mult, accum_out=e_[:, 3:4]),
                     r=[K("o1")], w=[K("o0"), K("ss")])
                ts("dve", e_[:, 4:5], e_[:, 3:4], 1.0 / 128, RMS_EPS, ALU.mult, ALU.add, [K("ss")], [K("ms")])
                tt("pool", e_[:, 6:7], e_[:, 4:5], mhalf[:], ALU.pow, [K("ms"), "mhalf"], [K("rstd")])
                stt("dve", oab[s][:], o1[s][:], e_[:, 6:7], gab[:], ALU.mult, ALU.mult, [K("o1"), K("rstd"), "gab"], [K("oab")])
                tpo = B[base][:].bitcast(BF16)[:, 512:640]
                tp(tpo, oab[s][:], [K("oab")], [("B", base)])
                cp("act", OA[:, qb // 4, (qb % 4) * 128:(qb % 4 + 1) * 128], tpo, [("B", base)], [("OAs", qb)])

            for i, g in enumerate(groups):
                emit_qk(g)
                if i >= 1:
                    emit_pv(groups[i - 1])
            emit_pv(groups[-1])

            if STOP <= 3:
                P.cut = True
            P.barrier()
            ph1.close()
            sview = snd.ap().rearrange("(c two f) n -> f two c n", two=2, f=128)
            NGR = NCH // 4
            d1p = P.op("sp", lambda e: e.dma_start(out=sview[:, 0, 0:NSUP], in_=OA[:, 0:NSUP, :]), dma=True)
            d2p = P.op("sp", lambda e: e.dma_start(out=sview[:, 1, 0:NSUP], in_=OB[:, 0:NSUP, :]), dma=True)
            for k in range(NGR - 1):
                c_ = P.op("pool", lambda e, k=k: e.collective_compute(
                    "AllGather", ALU.bypass, replica_groups=[[0, 1, 2, 3], [4, 5, 6, 7]],
                    ins=[snd.ap()[k * 1024:(k + 1) * 1024, :].opt()],
                    outs=[gat.ap()[k * 4096:(k + 1) * 4096, :].opt()]), cc=True, extra=[d1p, d2p])
                c_.bg = True
            P.op("pool", lambda e: e.dma_start(out=XS[:], in_=xsT.rearrange("(k p) n -> p k n", p=128)), w=["XS"], dma=True)
            a16 = sbp("a16", [128, 1])
            dma("sp", a16[:], a16_d, [], ["a16"])
            pti = sbp("pti", [128, NSB * NG8], I32)
            ptf = sbp("ptf", [128, NSB * NG8])
            gix = sbp("gix", [128, NSB * NG8], I32)
            dma("sp", pti[:], ptr_d, [], ["pti"])
            cp("dve", ptf[:], pti[:], ["pti"], ["ptf"])
            ts("dve", ptf[:], ptf[:], 16.0, a16[:, 0:1], ALU.mult, ALU.add, ["ptf", "a16"], ["ptf"])
            cp("dve", gix[:], ptf[:], ["ptf"], ["gix"])
            KG = [sbp("KG%d" % i, [128, NKT, 128], BF16) for i in range(2)]
            VG = [sbp("VG%d" % i, [128, NKT, 128], BF16) for i in range(2)]
            KTs = [sbp("KTs%d" % i, [128, 1024], BF16) for i in range(2)]
            Qbd = sbp("Qbd", [128, 8], BF16)
            P.op("pool", lambda e: e.memset(Qbd[:], 0.0), w=["Qbd"])
            KTn = sbp("KTn", [128, 4], BF16); QTn = sbp("QTn", [128, 4], BF16)
            Vn = sbp("Vn", [4, 128], BF16)
            PTs = [sbp("PTs%d" % i, [128, NKT * 8], BF16) for i in range(2)]
            PTn = sbp("PTn", [4, 8]); PTnb = sbp("PTnb", [4, 8], BF16)
            mnew = sbp("mnew", [4, 8])
            dma("sp", mnew[:], mnew_d, [], ["mnew"])
            rs = sbp("rs", [128, 8])
            OS = sbp("OS", [8, NSB, 128]); LS = sbp("LS", [8, NSB])
            OBs = sbp("OBs", [128, NSB * 4], BF16)
            ck_v = ck
            cv_v = cv
            for b in range(NSB):
                s2 = b % 2
                for G in range(NG8):
                    col = b * NG8 + G
                    P.op("pool", lambda e, G=G, col=col, s2=s2: e.indirect_dma_start(
                        out=KG[s2][:, G * 8:(G + 1) * 8, :].rearrange("p r d -> p (r d)"), out_offset=None, in_=ck_v,
                        in_offset=bass.IndirectOffsetOnAxis(ap=gix[:, col:col + 1], axis=0)),
                        r=["gix"], w=[("KG", s2, G)], dma=True)
                    P.op("pool", lambda e, G=G, col=col, s2=s2: e.indirect_dma_start(
                        out=VG[s2][:, G * 8:(G + 1) * 8, :].rearrange("p r d -> p (r d)"), out_offset=None, in_=cv_v,
                        in_offset=bass.IndirectOffsetOnAxis(ap=gix[:, col:col + 1], axis=0)),
                        r=["gix"], w=[("VG", s2, G)], dma=True)
                if STOP == 4 and os.environ.get("KSUB") == "1":
                    P.cut = True
                dma("sp", Sf[:], st_h[b], [], ["Sf"])
                cp("pool", Sb[:], Sf[:], ["Sf"], ["Sb"])
                mixer_block(4, NBLK + b, lambda kc, b=b: XS[:, kc, b * 4:(b + 1) * 4], None, None, ["XS"],
                            rope_s, k_s[b * 4:(b + 1) * 4, :], v_s[b * 4:(b + 1) * 4, :],
                            KTn[:], QTn[:], Vn[:], OBs[:, b * 4:(b + 1) * 4], ("s", b))()
                dma("sp", S_s[b], Sf[:], ["Sf"], [])
                tg = ("s", b)
                cp("dve", Qbd[0:64, 0:4], QTn[0:64, :], [("QT", tg)], ["Qbd"])
                cp("dve", Qbd[64:128, 4:8], QTn[64:128, :], [("QT", tg)], ["Qbd"])
                if STOP == 4 and os.environ.get("KSUB") == "2":
                    P.cut = True
                STb = B[7][:, 0:NKT * 8]
                for G in range(NG8):
                    kb = (b * NG8 + G) % 2
                    psb = B[6][:].bitcast(BF16)
                    for r_ in range(8):
                        tp(psb[:, r_ * 128:(r_ + 1) * 128], KG[s2][:, G * 8 + r_, :], [("KG", s2, G)], [("B", 6)])
                    cp("act" if G % 2 == 0 else "dve", KTs[kb][:], psb[:, :], [("B", 6)], [("KTs", kb)])
                    for r_ in range(8):
                        kt = G * 8 + r_
                        mm(STb[:, kt * 8:(kt + 1) * 8], KTs[kb][:, r_ * 128:(r_ + 1) * 128], Qbd[:], True, True,
                           [("KTs", kb), "Qbd"], [("B",7)])
                STn = B[1][0:4, 384:392]
                mm(STn, KTn[:], Qbd[:], True, True, [("KT", tg), "Qbd"], [("B",1)])
                act(PTs[s2][:], STb, AF.Exp, [("B",7)], [("PTs", s2)], scale=SCALE)
                act(PTn[:], STn, AF.Exp, [("B",1)], ["PTn"], scale=SCALE)
                tt("dve", PTn[:], PTn[:], mnew[:], ALU.mult, ["PTn", "mnew"], ["PTn"])
                cp("dve", PTnb[:], PTn[:], ["PTn"], ["PTnb"])
                P.op("dve", lambda e, s2=s2: e.tensor_reduce(rs[:], PTs[s2][:].rearrange("p (k q) -> p q k", q=8), AX.X, ALU.add),
                     r=[("PTs", s2)], w=["rs"])
                Lp = B[1][0:8, 400:401]
                mm(Lp, rs[:], ones[:, 0:1], True, False, ["rs", "ones"], [("B",1)])
                mm(Lp, PTn[:], ones[0:4, 0:1], False, True, ["PTn", "ones"], [("B",1)])
                if STOP == 4 and os.environ.get("KSUB") == "3":
                    P.cut = True
                Op_ = B[4][0:8, 0:128]
                for kt in range(NKT):
                    mm(Op_, PTs[s2][:, kt * 8:(kt + 1) * 8], VG[s2][:, kt, :], kt == 0, False,
                       [("PTs", s2), ("VG", s2, kt // 8)], [("B",4)])
                mm(Op_, PTnb[:], Vn[:], False, True, ["PTnb", ("V", tg)], [("B",4)])
                cp("act", OS[:, b, :], Op_, [("B",4)], ["OS"])
                cp("act", LS[:, b:b + 1], Lp, [("B",1)], ["LS"])
            if STOP == 4 and os.environ.get("KSUB") == "4":
                P.cut = True
            RL = sbp("RL", [8, NSB])
            P.op("dve", lambda e: e.reciprocal(RL[:], LS[:]), r=["LS"], w=["RL"])
            ON = sbp("ON", [8, NSB, 128])
            for b in range(NSB):
                ts("dve", ON[:, b, :], OS[:, b, :], RL[:, b:b + 1], None, ALU.mult, None, ["OS", "RL"], [("ON", b)])
            cmbp = sbp("cmbp", [8, 8])
            cmb = sbp("cmb", [8, 4])
            dma("sp", cmbp[:], cmbp_d, [], ["cmbp"])
            stt("dve", cmb[:], cmbp[:, 4:8], lam_t[0:8, 1:2], cmbp[:, 0:4], ALU.mult, ALU.add, ["cmbp", "lam"], ["cmb"])
            osb = sbp("osb", [4, NSB, 128]); osq = sbp("osq", [4, 128])
            sst = sbp("sst", [4, 4, NSB])
            oasb = sbp("oasb", [4, NSB, 128], BF16)
            B6b = B[6][:].bitcast(BF16)
            for b in range(NSB):
                bk = B[b % 3]
                mm(bk[0:4, 0:128], cmb[:], ON[:, b, :], True, True, ["cmb", ("ON", b)], [("B", b % 3)])
                cp("act", osb[:, b, :], bk[0:4, 0:128], [("B", b % 3)], [("osb", b)])
                act(osq[:], osb[:, b, :], AF.Square, [("osb", b)], ["osq", ("sst0", b)], accum=sst[:, 0, b:b + 1])
            allb = [("sst0", b) for b in range(NSB)]
            ts("dve", sst[:, 1, :], sst[:, 0, :], 1.0 / 128, RMS_EPS, ALU.mult, ALU.add, allb, ["sst1"])
            act(sst[:, 2, :], sst[:, 1, :], AF.Sqrt, ["sst1"], ["sst2"])
            P.op("dve", lambda e: e.reciprocal(sst[:, 3, :], sst[:, 2, :]), r=["sst2"], w=["sst3"])
            for b in range(NSB):
                stt("dve", oasb[:, b, :], osb[:, b, :], sst[:, 3, b:b + 1], gab[0:4, :], ALU.mult, ALU.mult, [("osb", b), "sst3", "gab"], [("oasb", b)])
                tp(B6b[:, 256 + b * 4:256 + (b + 1) * 4], oasb[:, b, :], [("oasb", b)], [("B", 6)])
            for j in range(4):
                cp("act", OA[:, NSUP + j, 0:NSL * 4], B6b[:, 256 + j * NSL * 4:256 + (j + 1) * NSL * 4], [("B", 6)], [("OAsmp", j)])
                cp("dve", OB[:, NSUP + j, 0:NSL * 4], OBs[:, j * NSL * 4:(j + 1) * NSL * 4],
                   [("OBs", ("s", b)) for b in range(NSB)], [("OBsmp", j)])
            if STOP <= 4:
                P.cut = True
            P.barrier()
            d1 = P.op("sp", lambda e: e.dma_start(out=sview[:, 0, NSUP:NCH], in_=OA[:, NSUP:NCH, :]), dma=True)
            d2 = P.op("sp", lambda e: e.dma_start(out=sview[:, 1, NSUP:NCH], in_=OB[:, NSUP:NCH, :]), dma=True)

        for k in range(NGR - 1, NGR):
            P.op("pool", lambda e, k=k: e.collective_compute(
                "AllGather", ALU.bypass, replica_groups=[[0, 1, 2, 3], [4, 5, 6, 7]],
                ins=[snd.ap()[k * 1024:(k + 1) * 1024, :].opt()],
                outs=[gat.ap()[k * 4096:(k + 1) * 4096, :].opt()]), cc=True, extra=[d1, d2])
        P.barrier(include_bg=True)

        if STOP <= 5:
            P.cut = True
        with ExitStack() as ph:
            def sbp(name, shape, dt=F32):
                return sb(name, shape, dt, ph)
            B = banks
            WOUT = sbp("WOUT", [128, 8, D], BF16)
            WDN = sbp("WDN", [128, NFC, D], BF16)
            for kc in range(0, 8, 2):
                P.op("sp", lambda e, kc=kc: e.dma_start(out=WOUT[:, kc:kc + 2, :], in_=woutb.ap().rearrange("(k p) c -> p k c", p=128)[:, kc:kc + 2, :]), w=["WOUT"], dma=True)
            for kc in range(0, NFC, 2):
                P.op("sp", lambda e, kc=kc: e.dma_start(out=WDN[:, kc:kc + 2, :], in_=wdnb.ap().rearrange("(k p) c -> p k c", p=128)[:, kc:kc + 2, :]), w=["WDN"], dma=True)
            LNP = sbp("LNP", [128, 4, D])
            dma("sp", LNP[:], lnp_d, [], ["LNP"])
            CVW = sbp("CVW", [128, NFC, 4])
            dma("sp", CVW[:], cvw_d, [], ["CVW"])
            SCV = sbp("SCV", [128, NFC, NSL, 2])
            dma("sp", SCV[:], scv_d, [], ["SCV"])
            hmask = sbp("hmask", [128, 1])
            dma("sp", hmask[:], hmask_d, [], ["hmask"])
            IDX = sbp("IDX", [128, (NTILE + 2) * 8], I32)
            dma("sp", IDX[:], idx_d, [], ["IDX"])
            carry = sbp("carry", [128, NFC, 2])
            CVS = sbp("CVS", [128, NFC, NSL, 2])
            XTt = sbp("XTt", [128, 8, 512], BF16)
            OAT = sbp("OAT", [128, 4, 512], BF16)
            OBT = sbp("OBT", [128, 4, 512], BF16)
            SG = sbp("SG", [128, 4, 512], BF16)
            MT = sbp("MT", [128, 8, 512], BF16)
            tA = sbp("tA", [128, 512]); tB = sbp("tB", [128, 512])
            HTM = sbp("HTM", [128, 4, D])
            HT = sbp("HT", [128, 8, 512], BF16)
            UT = sbp("UT", [128, NFC, 512], BF16)
            GT = UT[:, 0:16, :].rearrange("p (w h t) n -> p w h t n", w=2, h=4)
            WS = [sbp("WS%d" % i, [128, 2048], BF16) for i in range(4)]
            Xr = [sbp("Xr%d" % i, [128, D]) for i in range(2)]
            Z = [sbp("Z%d" % i, [128, D]) for i in range(2)]
            Hb = [sbp("Hb0", [128, D], BF16)] * 2
            bst = [sbp("bst%d" % i, [128, 16]) for i in range(2)]
            AE = [sbp("AE%d" % i, [128, 516]) for i in range(2)]
            Cc = [sbp("Cc%d" % i, [128, 512]) for i in range(2)]
            Gl = [sbp("Gl%d" % i, [128, 512]) for i in range(2)]
            wsi = [0]

            def wload(src, nel, name):
                i = wsi[0] % 4
                wsi[0] += 1
                P.op("sp", lambda e: e.dma_start(out=WS[i][:, 0:nel], in_=src), w=[("WS", i)], dma=True)
                return WS[i], ("WS", i)

            def layer_norm(zt, nt, gi_, out, rkeys, wkey, s):
                FM = 512
                nchk = D // FM
                st = bst[s]
                for c in range(nchk):
                    P.op("dve", lambda e, c=c: e.bn_stats(st[0:nt, c * 6:(c + 1) * 6], zt[0:nt, c * FM:(c + 1) * FM]),
                         r=rkeys, w=[("bst", s, c)])
                P.op("dve", lambda e: e.bn_aggr(st[0:nt, 12:14], st[0:nt, 0:12].rearrange("p (c k) -> p c k", k=6)),
                     r=[("bst", s, c) for c in range(nchk)], w=[("mv", s)])
                ts("dve", st[0:nt, 14:15], st[0:nt, 13:14], LN_EPS, None, ALU.add, None, [("mv", s)], [("ve", s)])
                act(st[0:nt, 14:15], st[0:nt, 14:15], AF.Sqrt, [("ve", s)], [("ve", s)])
                P.op("dve", lambda e: e.reciprocal(st[0:nt, 15:16], st[0:nt, 14:15]), r=[("ve", s)], w=[("rs", s)])
                ts("dve", zt[0:nt], zt[0:nt], st[0:nt, 12:13], st[0:nt, 15:16], ALU.subtract, ALU.mult, rkeys + [("mv", s), ("rs", s)], rkeys)
                tt("pool", zt[0:nt], zt[0:nt], LNP[0:nt, gi_, :], ALU.mult, rkeys + ["LNP"], rkeys)
                tt("pool", out, zt[0:nt], LNP[0:nt, gi_ + 1, :], ALU.add, rkeys + ["LNP"], [wkey])

            P.op("pool", lambda e: e.memset(carry[:], 0.0), w=["carry"])
            gview = gat.ap()

            def gather(dst, col, key):
                P.op("pool", lambda e: e.indirect_dma_start(out=dst, out_offset=None, in_=gview,
                     in_offset=bass.IndirectOffsetOnAxis(ap=IDX[:, col:col + 1], axis=0)), r=["IDX"], w=[key], dma=True)

            ybase = 0
            for ti in range(NTILE + 1):
                stile = ti == 0
                if stile:
                    n, c0, segs, seglen = NS_T + 2, 0, NSL, 4
                    ny = NS_T
                else:
                    n, c0, segs, seglen = 512, NS_T + 2 + (ti - 1) * 512, 1, 512
                    ny = 512
                tk = ("tile", ti)
                P.op("pool", lambda e, c0=c0, n=n: e.dma_start(out=XTt[:, :, 0:n], in_=xT_t[:, c0:c0 + n].rearrange("(k p) n -> p k n", p=128)),
                     w=["XTt"], dma=True)
                if stile:
                    for wh in range(2):
                        for h in range(4):
                            for pt in range(2):
                                gather(GT[:, wh, h, pt, :], wh * 8 + h * 2 + pt, ("GT", wh, h, pt))
                    gk = [("GT", wh, h, pt) for wh in range(2) for h in range(4) for pt in range(2)]
                    cp("dve", OAT[:, :, 0:NS_T], GT[:, 0, :, 0, 0:NS_T], gk, ["OAT"])
                    cp("dve", OAT[:, :, NS_T:NS_T + 2], GT[:, 1, :, 0, 510:512], gk, ["OAT"])
                    cp("dve", OBT[:, :, 0:NS_T], GT[:, 0, :, 1, 0:NS_T], gk, ["OBT"])
                    cp("dve", OBT[:, :, NS_T:NS_T + 2], GT[:, 1, :, 1, 510:512], gk, ["OBT"])
                else:
                    for h in range(4):
                        gather(OAT[:, h, :], (ti + 1) * 8 + h * 2, "OAT")
                        gather(OBT[:, h, :], (ti + 1) * 8 + h * 2 + 1, "OBT")
                for cb in range(8):
                    sl = (cb % 2) * 2
                    for gi2 in range(2):
                        wsb, wk = wload(wgb.ap()[(gi2 * 8 + cb) * 128:(gi2 * 8 + cb + 1) * 128, :], 1024, "wg")
                        ps = B[gi2][:, 0:n]
                        for kc in range(8):
                            mm(ps, wsb[:, kc * 128:(kc + 1) * 128], XTt[:, kc, 0:n], kc == 0, kc == 7, [wk, "XTt"], [("B", gi2)])
                        act(SG[:, sl + gi2, 0:n], ps, AF.Sigmoid, [("B", gi2)], [("SG", sl + gi2)])
                    wsa, wka = wload(wab.ap()[cb * 128:(cb + 1) * 128, :], 512, "wa")
                    wsb2, wkb = wload(wbb.ap()[cb * 128:(cb + 1) * 128, :], 512, "wb")
                    pa = B[2 + cb % 2][:, 0:n]
                    pb = B[4 + cb % 2][:, 0:n]
                    for kc in range(4):
                        mm(pa, wsa[:, kc * 128:(kc + 1) * 128], OAT[:, kc, 0:n], kc == 0, kc == 3, [wka, "OAT"], [("B", 2 + cb % 2)])
                    for kc in range(4):
                        mm(pb, wsb2[:, kc * 128:(kc + 1) * 128], OBT[:, kc, 0:n], kc == 0, kc == 3, [wkb, "OBT"], [("B", 4 + cb % 2)])
                    tt("dve", tA[:, 0:n], pa, SG[:, sl, 0:n], ALU.mult, [("B", 2 + cb % 2), ("SG", sl)], ["tA"])
                    tt("dve", tB[:, 0:n], pb, SG[:, sl + 1, 0:n], ALU.mult, [("B", 4 + cb % 2), ("SG", sl + 1)], ["tB"])
                    tt("pool", MT[:, cb, 0:n], tA[:, 0:n], tB[:, 0:n], ALU.add, ["tA", "tB"], [("MT", cb)])
                nblk = (n + 127) // 128
                MTk = [("MT", cb) for cb in range(8)]
                for tb in range(nblk):
                    t0 = tb * 128
                    nt = min(128, n - t0)
                    s = tb % 2
                    dma("sp", Xr[s][0:nt], x_t[c0 + t0:c0 + t0 + nt, :], [], [("Xr", s)])
                    for hf in range(2):
                        ps = B[6 + hf][0:nt, :]
                        for kc in range(8):
                            mm(ps, MT[:, kc, t0:t0 + nt], WOUT[:, kc, hf * 512:(hf + 1) * 512], kc == 0, kc == 7, MTk + ["WOUT"], [("B", 6 + hf)])
                        stt("dve", Z[s][0:nt, hf * 512:(hf + 1) * 512], Xr[s][0:nt, hf * 512:(hf + 1) * 512], ALPHA, ps, ALU.mult, ALU.add,
                            [("Xr", s), ("B", 6 + hf)], [("Z", s)])
                    layer_norm(Z[s], nt, 0, HTM[0:nt, tb, :], [("Z", s)], ("HTM", tb), s)
                    cp("act", Hb[s][0:nt], HTM[0:nt, tb, :], [("HTM", tb)], ["Hb"])
                    psb = B[tb % 2][:].bitcast(BF16)
                    for kc in range(8):
                        tp(psb[:, kc * 128:kc * 128 + nt], Hb[s][0:nt, kc * 128:(kc + 1) * 128], ["Hb"], [("B", tb % 2)])
                    cp("act", HT[:, :, t0:t0 + nt], psb[:, :].rearrange("p (k t) -> p k t", k=8)[:, :, 0:nt], [("B", tb % 2)], [("HT", tb)])
                HTk = [("HT", tb) for tb in range(nblk)]
                for fc in range(NFC):
                    wsu, wku = wload(wupb.ap()[fc * 128:(fc + 1) * 128, :], 2048, "wup")
                    s = fc % 2
                    pa = B[2 + s][:, 0:n]
                    pg = B[4 + s][:, 0:n]
                    for kc in range(8):
                        mm(pa, wsu[:, kc * 128:(kc + 1) * 128], HT[:, kc, 0:n], kc == 0, kc == 7, [wku] + HTk, [("B", 2 + s)])
                    for kc in range(8):
                        mm(pg, wsu[:, 1024 + kc * 128:1024 + (kc + 1) * 128], HT[:, kc, 0:n], kc == 0, kc == 7, [wku] + HTk, [("B", 4 + s)])
                    nsg = segs * seglen
                    ae = AE[s][:, 0:segs * (seglen + 2)].rearrange("p (g t) -> p g t", g=segs)
                    cp("act", ae[:, :, 2:], pa[:, 0:nsg].rearrange("p (g t) -> p g t", g=segs), [("B", 2 + s)], [("AE", s)])
                    if stile:
                        cp("pool", ae[:, :, 0:2], SCV[:, fc, :, :], ["SCV", ("AE", s)], [("AE", s)])
                        ts("dve", carry[:, fc, :], pa[:, NS_T:NS_T + 2], hmask[:, 0:1], None, ALU.mult, None, [("B", 2 + s), "hmask"], ["carry"])
                        cp("pool", CVS[:, fc, :, :], ae[:, :, seglen:seglen + 2], [("AE", s)], ["CVS"])
                    else:
                        cp("pool", ae[:, :, 0:2], carry[:, fc, :].unsqueeze(1), ["carry", ("AE", s)], [("AE", s)])
                        cp("pool", carry[:, fc, :].unsqueeze(1), ae[:, :, seglen:seglen + 2], [("AE", s)], ["carry"])
                    cc_ = Cc[s][:, 0:nsg].rearrange("p (g t) -> p g t", g=segs)
                    ts("dve", cc_, ae[:, :, 2:], CVW[:, fc, 2:3], CVW[:, fc, 3:4], ALU.mult, ALU.add, [("AE", s), "CVW"], [("Cc", s)])
                    stt("dve", cc_, ae[:, :, 1:seglen + 1], CVW[:, fc, 1:2], cc_, ALU.mult, ALU.add, [("AE", s), "CVW", ("Cc", s)], [("Cc", s)])
                    stt("dve", cc_, ae[:, :, 0:seglen], CVW[:, fc, 0:1], cc_, ALU.mult, ALU.add, [("AE", s), "CVW", ("Cc", s)], [("Cc", s)])
                    act(Gl[s][:, 0:nsg], Cc[s][:, 0:nsg], AF.Gelu, [("Cc", s)], [("Gl", s)])
                    tt("dve", UT[:, fc, 0:nsg], pg[:, 0:nsg], Gl[s][:, 0:nsg], ALU.mult, [("B", 4 + s), ("Gl", s)], [("UT", fc)])
                UTk = [("UT", fc) for fc in range(NFC)]
                nblk4 = (ny + 127) // 128
                for tb in range(nblk4):
                    t0 = tb * 128
                    nt = min(128, ny - t0)
                    s = tb % 2
                    for hf in range(2):
                        ps = B[6 + hf][0:nt, :]
                        for fc in range(NFC):
                            mm(ps, UT[:, fc, t0:t0 + nt], WDN[:, fc, hf * 512:(hf + 1) * 512], fc == 0, fc == NFC - 1, UTk + ["WDN"], [("B", 6 + hf)])
                        stt("dve", Z[s][0:nt, hf * 512:(hf + 1) * 512], HTM[0:nt, tb, hf * 512:(hf + 1) * 512], ALPHA, ps, ALU.mult, ALU.add,
                            [("HTM", tb), ("B", 6 + hf)], [("Z", s)])
                    layer_norm(Z[s], nt, 2, Xr[s][0:nt], [("Z", s)], ("Xr", s), s)
                    dma("sp", y_o[ybase + t0:ybase + t0 + nt, :], Xr[s][0:nt], [("Xr", s)], [])
                ybase += ny
            dma("sp", cv_s, CVS[:], ["CVS"], [])
            dma("sp", cv_p, carry[:], ["carry"], [])

        with ExitStack() as fin:
            P.finalize(fin)
    return nc


def _prep(inputs):
    g = lambda k: np.asarray(inputs[k])
    xp, xs = g("x_prompt"), g("x_sample")
    Bp, T, _ = xp.shape
    Bs, TS, _ = xs.shape
    ck, cvv = g("cache_k"), g("cache_v")
    NPHYS, _, PG, H, _, DH = ck.shape
    pt = g("page_table")
    PAST = pt.shape[1] * PG
    NSB = Bs // 2
    cfg = dict(T=T, NSB=NSB, PAST=PAST, NPHYS=NPHYS)
    NG8 = PAST // 1024
    TT = T // 4
    NTILE = TT // 512
    NSUP = T // 512
    NCH = NSUP + 4
    NSL = NSB // 4
    NS_T = NSL * 4
    w_in = g("w_in")[0]
    f32 = np.float32
    half = DH // 2
    inv = (10000.0 ** (-np.arange(half, dtype=f32) * 2.0 / DH)).astype(f32)

    def rope_tab(pos):
        ang = pos.astype(f32)[:, None] * inv[None, :]
        c, s = np.cos(ang).astype(f32), np.sin(ang).astype(f32)
        return np.ascontiguousarray(np.stack([np.tile(c, (1, 4)), np.tile(s, (1, 4))], axis=1))
    rope_p = rope_tab(np.arange(T))
    rope_s = rope_tab(PAST + np.arange(TS))
    tri = np.triu(np.ones((128, 128), f32))
    ident = np.eye(128, dtype=f32)
    mnew = np.tile(np.triu(np.ones((4, 4), f32)), (1, 2))
    cmbp = np.zeros((8, 8), f32)
    cmbp[0:4, 0:4] = np.eye(4)
    cmbp[4:8, 4:8] = np.eye(4)
    a16 = (np.arange(128) % 16).astype(f32).reshape(128, 1)
    lamv = np.tile(np.concatenate([g("lambda_q1")[0], g("lambda_q2")[0], g("lambda_k1")[0], g("lambda_k2")[0]])[None, :], (128, 1)).astype(f32)
    ga_b = np.tile(g("subln_g")[0][None, :], (128, 1)).astype(f32)
    gn_b = np.tile(g("hgrn_norm_g")[0][None, :], (128, 1)).astype(f32)
    wgt = w_in[:, 3584:5632].reshape(8, 128, 16, 128).transpose(2, 1, 0, 3).reshape(16, 128, 1024)
    wa = g("w_branch_a")[0].reshape(4, 128, 8, 128).transpose(2, 1, 0, 3).reshape(8, 128, 512)
    wb = g("w_branch_b")[0].reshape(4, 128, 8, 128).transpose(2, 1, 0, 3).reshape(8, 128, 512)
    wup = g("w_up")[0].reshape(8, 128, 2, NFC, 128).transpose(3, 1, 2, 0, 4).reshape(NFC, 128, 2048)
    lnp = np.stack([g("ln1_g")[0], g("ln1_b")[0], g("ln2_g")[0], g("ln2_b")[0]], 0)
    lnp = np.ascontiguousarray(np.tile(lnp[None], (128, 1, 1))).astype(f32)
    cvw = np.concatenate([g("conv_w")[0], g("conv_b")], 0).reshape(4, NFC, 128).transpose(2, 1, 0)
    shared = dict(rope_p=rope_p, rope_s=rope_s, lamv=lamv, ga_b=ga_b, gn_b=gn_b, tri=tri, ident=ident, mnew=mnew,
                  cmbp=cmbp, a16=a16, wg_t=np.ascontiguousarray(wgt), wa_t=np.ascontiguousarray(wa),
                  wb_t=np.ascontiguousarray(wb), wout=np.ascontiguousarray(g("w_out")[0]),
                  wup_t=np.ascontiguousarray(wup), wdn=np.ascontiguousarray(g("w_down")[0]), lnp=lnp,
                  cvw=np.ascontiguousarray(cvw))
    cols = {"ka": 512, "qa": 0, "va": 1024, "ib": 2560, "gb": 3072, "qb": 1536, "fb": 2048}
    order = ["ka", "qa", "va", "ib", "gb", "qb", "fb"]
    sconv = g("state_conv")[:, 0]
    st_all = g("state_hgrn")[:, 0]
    lbl_all = g("lb_logits")
    in_maps = []
    for c in range(8):
        gI, h = c // 4, c % 4
        j = h
        m = dict(shared)
        m["xT_seq"] = np.ascontiguousarray(xp[gI].T)
        sb_ids = np.arange(gI * NSB, (gI + 1) * NSB)
        m["xsT"] = np.ascontiguousarray(xs[sb_ids].reshape(NSB * TS, D).T)
        m["wm"] = np.ascontiguousarray(np.concatenate([w_in[:, cols[k] + h * 128: cols[k] + (h + 1) * 128] for k in order], 1))
        m["lbl"] = np.ascontiguousarray(lbl_all[:, h * 128:(h + 1) * 128].T)
        ptb = pt[sb_ids].reshape(NSB, NG8, 8)
        m["ptr"] = np.ascontiguousarray(np.repeat(ptb.transpose(2, 0, 1), 16, axis=0).reshape(128, NSB * NG8)).astype(np.int32)
        m["ck"] = np.ascontiguousarray(ck[:, 0, :, h]).reshape(NPHYS * 16, 1024)
        m["cv"] = np.ascontiguousarray(cvv[:, 0, :, h]).reshape(NPHYS * 16, 1024)
        m["st_h"] = np.ascontiguousarray(st_all[sb_ids, h])
        tb_ids = sb_ids[j * NSL:(j + 1) * NSL]
        xs_t = xs[tb_ids].reshape(NS_T, D)
        p0 = j * TT
        halo = xp[gI, max(p0 - 2, 0):max(p0 - 2, 0) + 2]
        xcols = np.concatenate([xs_t, halo, xp[gI, p0:p0 + TT]], 0)
        m["x_t"] = np.ascontiguousarray(xcols)
        m["xT_t"] = np.ascontiguousarray(xcols.T)
        m["scv"] = np.ascontiguousarray(sconv[tb_ids].reshape(NSL, 2, NFC, 128).transpose(3, 2, 0, 1))
        m["hmask"] = np.full((128, 1), 0.0 if j == 0 else 1.0, f32)
        idx = np.zeros((128, NTILE + 2, 4, 2), np.int64)
        pr = np.arange(128)
        chunks = [NSUP + j, max(j * NTILE - 1, 0)] + [j * NTILE + k for k in range(NTILE)]
        for ci, chn in enumerate(chunks):
            for hh in range(4):
                for part in range(2):
                    idx[:, ci, hh, part] = (chn // 4) * 4096 + hh * 1024 + (chn % 4) * 256 + part * 128 + pr
        m["idx"] = np.ascontiguousarray(idx.reshape(128, -1)).astype(np.int32)
        in_maps.append({k: np.ascontiguousarray(v) for k, v in m.items()})
    return cfg, in_maps


_CACHE = {}


def kernel(**inputs):
    cfg, in_maps = _prep(inputs)
    key = tuple(sorted(cfg.items()))
    if key not in _CACHE:
        _CACHE[key] = build(cfg)
    nc = _CACHE[key]
    res = run_bass_kernel_spmd(nc, in_maps, core_ids=list(range(8))).results
    T, NSB = cfg["T"], cfg["NSB"]
    TT = T // 4
    NSL = NSB // 4
    NS_T = NSL * 4
    f32 = np.float32
    Bs = NSB * 2
    y_p = np.zeros((2, T, D), f32); y_s = np.zeros((Bs, 4, D), f32)
    k_p = np.zeros((2, 1, T, 4, 2, 64), f32); v_p = np.zeros((2, 1, T, 4, 128), f32)
    h_p = np.zeros((2, 1, 4, 128, 128), f32); c_p = np.zeros((2, 1, 2, DFF), f32)
    k_s = np.zeros((Bs, 1, 4, 4, 2, 64), f32); v_s = np.zeros((Bs, 1, 4, 4, 128), f32)
    h_s = np.zeros((Bs, 1, 4, 128, 128), f32); c_s = np.zeros((Bs, 1, 2, DFF), f32)
    for c in range(8):
        r = res[c]
        gI, h = c // 4, c % 4
        j = h
        k_p[gI, 0, :, h] = r["k_p"].reshape(T, 2, 64)
        v_p[gI, 0, :, h] = r["v_p"]
        h_p[gI, 0, h] = r["S_p"]
        sl = slice(gI * NSB, (gI + 1) * NSB)
        k_s[sl, 0, :, h] = r["k_s"].reshape(NSB, 4, 2, 64)
        v_s[sl, 0, :, h] = r["v_s"].reshape(NSB, 4, 128)
        h_s[sl, 0, h] = r["S_s"]
        tb = slice(gI * NSB + j * NSL, gI * NSB + (j + 1) * NSL)
        y_s[tb] = r["y_o"][0:NS_T].reshape(NSL, 4, D)
        y_p[gI, j * TT:(j + 1) * TT] = r["y_o"][NS_T:]
        c_s[tb, 0] = r["cv_s"].transpose(2, 3, 1, 0).reshape(NSL, 2, DFF)
        if j == 3:
            c_p[gI, 0] = r["cv_p"].transpose(2, 1, 0).reshape(2, DFF)
    return (y_p, y_s, k_p, v_p, h_p, c_p, k_s, v_s, h_s, c_s)
```
